# Optimizing a Trainium2 kernel written in Bass

```python
import jax, jax.numpy as jnp
from jax import lax
import numpy as np

D_MODEL = 1024
BATCH = 8
SEQ = 2048
DEPTH = 1
DEC_BATCH = 128
DEC_SEQ = 8
PAST_LEN = 2048
PAGE_SIZE = 128

N_META = 16
H_RET = D_MODEL // 256
DK_RET = 256
DV_RET = 512
RET_CHUNK = 128
ROPE_BASE = 10000.0
N_HEADS = D_MODEL // 128
N_KV_HEADS = 2
D_HEAD = 128
N_IDX_HEADS = 8
D_IDX = 64
INDEX_TOPK = 256
Q_BLOCK = 128
D_FF = -(-(8 * D_MODEL) // (3 * 256)) * 256
NORM_EPS = 1e-6

RET_QK_W = H_RET * DK_RET
RET_V_W = H_RET * DV_RET
DSA_Q_W = N_HEADS * D_HEAD
DSA_KV_W = N_KV_HEADS * D_HEAD
IDX_Q_W = N_IDX_HEADS * D_IDX
IN_COLS = 2 * RET_QK_W + 2 * RET_V_W + DSA_Q_W + 2 * DSA_KV_W + IDX_Q_W + D_IDX + N_IDX_HEADS + 2 * D_MODEL
IDX_SCALE = (N_IDX_HEADS ** -0.5) * (D_IDX ** -0.5)

kernel_name = "retention_dsa_gated_hybrid_step"


def rms_norm(x, w=None):
    xf = x.astype(jnp.float32)
    y = (xf * lax.rsqrt(jnp.mean(xf * xf, axis=-1, keepdims=True) + NORM_EPS)).astype(x.dtype)
    return y if w is None else y * w


def layer_norm(x, w, b):
    xf = x.astype(jnp.float32)
    mu = jnp.mean(xf, axis=-1, keepdims=True)
    var = jnp.mean(jnp.square(xf - mu), axis=-1, keepdims=True)
    return ((xf - mu) * lax.rsqrt(var + NORM_EPS)).astype(x.dtype) * w + b


def rotary(x, pos):
    half = x.shape[-1] // 2
    inv = ROPE_BASE ** (-jnp.arange(half, dtype=jnp.float32) / half)
    ang = pos.astype(jnp.float32)[:, None] * inv[None, :]
    cos = jnp.cos(ang)[None, :, None, :]
    sin = jnp.sin(ang)[None, :, None, :]
    x1 = x[..., :half].astype(jnp.float32)
    x2 = x[..., half:].astype(jnp.float32)
    return jnp.concatenate([x1 * cos - x2 * sin, x1 * sin + x2 * cos], axis=-1).astype(x.dtype)


def retention_log_gamma():
    return jnp.log1p(-jnp.exp2(-5.0 - jnp.arange(H_RET, dtype=jnp.float32)))


def retention_chunk(S, q, k, v):
    C = q.shape[2]
    lg = retention_log_gamma()
    i = jnp.arange(C, dtype=jnp.float32)
    diff = i[:, None] - i[None, :]
    decay = jnp.where(diff >= 0, jnp.exp(lg[:, None, None] * jnp.maximum(diff, 0.0)), 0.0).astype(q.dtype)
    q_decay = jnp.exp(lg[:, None] * (i + 1.0)).astype(q.dtype)
    k_decay = jnp.exp(lg[:, None] * (C - 1.0 - i)).astype(q.dtype)
    s_decay = jnp.exp(lg * C).astype(q.dtype)
    scores = jnp.einsum("bhid,bhjd->bhij", q, k) * decay
    o = (jnp.einsum("bhij,bhjv->bhiv", scores, v)
         + jnp.einsum("bhid,bhdv->bhiv", q, S) * q_decay[None, :, :, None])
    S_new = (S * s_decay[None, :, None, None]
             + jnp.einsum("bhjd,bhjv->bhdv", k * k_decay[None, :, :, None], v))
    return S_new, o


def retention_prompt(q, k, v):
    B, H, L, _ = q.shape
    S0 = jnp.zeros((B, H, DK_RET, DV_RET), v.dtype)
    S1, o_meta = retention_chunk(S0, q[:, :, :N_META], k[:, :, :N_META], v[:, :, :N_META])
    n_c = (L - N_META) // RET_CHUNK

    def to_chunks(a):
        return a[:, :, N_META:].reshape(B, H, n_c, RET_CHUNK, a.shape[-1]).transpose(2, 0, 1, 3, 4)

    S_fin, o_c = lax.scan(lambda S, qkv: retention_chunk(S, *qkv), S1,
                          (to_chunks(q), to_chunks(k), to_chunks(v)))
    o_c = o_c.transpose(1, 2, 0, 3, 4).reshape(B, H, n_c * RET_CHUNK, DV_RET)
    return jnp.concatenate([o_meta, o_c], axis=2), S_fin


def indexer_scores(qi, wi, ki):
    s = jnp.einsum("bqhd,bsd->bqhs", qi.astype(jnp.float32), ki.astype(jnp.float32))
    return jnp.einsum("bqhs,bqh->bqs", jax.nn.relu(s), wi.astype(jnp.float32) * IDX_SCALE)


def sparse_attend(q, kg, vg, valid):
    B, Q = q.shape[:2]
    qg = q.reshape(B, Q, N_KV_HEADS, N_HEADS // N_KV_HEADS, D_HEAD)
    s = jnp.einsum("bqngd,bqknd->bqngk", qg.astype(jnp.float32), kg.astype(jnp.float32)) * (D_HEAD ** -0.5)
    s = jnp.where(valid[:, :, None, None, :], s, -jnp.inf)
    p = jax.nn.softmax(s, axis=-1).astype(vg.dtype)
    o = jnp.einsum("bqngk,bqknd->bqngd", p, vg)
    return o.reshape(B, Q, N_HEADS, D_HEAD)


def dsa_prompt(q, k, v, qi, ki, wi, n_real):
    B, L = q.shape[:2]
    topk = min(INDEX_TOPK, n_real // 4)
    nb = -(-L // Q_BLOCK)
    Lp = nb * Q_BLOCK

    def blocks(a):
        a = jnp.pad(a, [(0, 0), (0, Lp - L)] + [(0, 0)] * (a.ndim - 2))
        return jnp.swapaxes(a.reshape((B, nb, Q_BLOCK) + a.shape[2:]), 0, 1)

    q_pos = jnp.arange(Lp).reshape(nb, Q_BLOCK)
    key_pos = jnp.arange(L)
    bidx = jnp.arange(B)[:, None, None]

    def one_block(args):
        qb, qib, wib, pb = args
        score = indexer_scores(qib, wib, ki)
        score = jnp.where(key_pos[None, None, :] <= pb[None, :, None], score, -jnp.inf)
        _, sel = lax.top_k(score, topk)
        valid = sel <= pb[None, :, None]
        return sparse_attend(qb, k[bidx, sel], v[bidx, sel], valid)

    o = lax.map(one_block, (blocks(q), blocks(qi), blocks(wi), q_pos))
    return jnp.swapaxes(o, 0, 1).reshape(B, Lp, N_HEADS, D_HEAD)[:, :L]


def dsa_sample(q, k_new, v_new, qi, ki_new, wi, cache_k, cache_v, cache_kidx, page_table):
    DB, Q = q.shape[:2]
    n_pages = page_table.shape[1]
    past = n_pages * PAGE_SIZE
    L = past + Q
    topk = min(INDEX_TOPK, L // 4)
    ki_all = jnp.concatenate([cache_kidx[page_table].reshape(DB, past, D_IDX), ki_new], axis=1)
    q_pos = past + jnp.arange(Q)
    key_pos = jnp.arange(L)
    score = indexer_scores(qi, wi, ki_all)
    score = jnp.where(key_pos[None, None, :] <= q_pos[None, :, None], score, -jnp.inf)
    _, sel = lax.top_k(score, topk)
    valid = sel <= q_pos[None, :, None]
    bidx = jnp.arange(DB)[:, None, None]
    in_past = (sel < past)[..., None, None]
    ps = jnp.minimum(sel, past - 1)
    phys = page_table[bidx, ps // PAGE_SIZE]
    off = ps % PAGE_SIZE
    ns = jnp.clip(sel - past, 0, Q - 1)
    kg = jnp.where(in_past, cache_k[phys, off], k_new[bidx, ns])
    vg = jnp.where(in_past, cache_v[phys, off], v_new[bidx, ns])
    return sparse_attend(q, kg, vg, valid)


def mixer_inputs(h, pos, norm_mix_w, w_in, dsa_q_norm_w, dsa_k_norm_w, idx_k_norm_w, idx_k_norm_b):
    B, L = h.shape[:2]
    z = rms_norm(h, norm_mix_w) @ w_in
    sizes = (RET_QK_W, RET_QK_W, RET_V_W, RET_V_W, DSA_Q_W, DSA_KV_W, DSA_KV_W,
             IDX_Q_W, D_IDX, N_IDX_HEADS, 2 * D_MODEL)
    rq, rk, rv, rg, aq, ak, av, iq, ik, iw, gz = jnp.split(z, [int(c) for c in np.cumsum(sizes)[:-1]], axis=-1)
    rq = rotary(rq.reshape(B, L, H_RET, DK_RET), pos).transpose(0, 2, 1, 3)
    rk = (rotary(rk.reshape(B, L, H_RET, DK_RET), pos) * (DK_RET ** -0.5)).transpose(0, 2, 1, 3)
    rv = rv.reshape(B, L, H_RET, DV_RET).transpose(0, 2, 1, 3)
    aq = rms_norm(aq.reshape(B, L, N_HEADS, D_HEAD), dsa_q_norm_w)
    ak = rms_norm(ak.reshape(B, L, N_KV_HEADS, D_HEAD), dsa_k_norm_w)
    av = av.reshape(B, L, N_KV_HEADS, D_HEAD)
    iq = iq.reshape(B, L, N_IDX_HEADS, D_IDX)
    ik = layer_norm(ik, idx_k_norm_w, idx_k_norm_b)
    return rq, rk, rv, rg, aq, ak, av, iq, ik, iw, jax.nn.sigmoid(gz)


def merge_and_ffn(h, o_ret, rg, o_dsa, gates, w_ret_proj, w_dsa_proj, w_out, norm_ffn_w, w_ffn_in, w_ffn_out):
    B, L = h.shape[:2]
    o_ret = rms_norm(o_ret.transpose(0, 2, 1, 3)).reshape(B, L, RET_V_W) * jax.nn.silu(rg)
    br_ret = o_ret @ w_ret_proj
    br_dsa = o_dsa.reshape(B, L, DSA_Q_W) @ w_dsa_proj
    g_ret, g_dsa = jnp.split(gates, 2, axis=-1)
    h = h + (g_ret * br_ret + g_dsa * br_dsa) @ w_out
    a, b = jnp.split(rms_norm(h, norm_ffn_w) @ w_ffn_in, 2, axis=-1)
    return h + (jax.nn.silu(a) * b) @ w_ffn_out


def setup_inputs(seed: int = 0) -> dict:
    key = jax.random.key(seed)
    ks = jax.random.split(key, 20)
    f32 = jnp.float32
    n_pages = PAST_LEN // PAGE_SIZE
    n_used = DEC_BATCH * n_pages
    n_pool = (n_used * 5) // 4

    def nrm(k, shape, scale=1.0):
        return jax.random.normal(k, shape, f32) * scale

    def gain(k, shape):
        return 1.0 + 0.02 * jax.random.normal(k, shape, f32)

    return {
        "x_prompt": nrm(ks[0], (BATCH, SEQ, D_MODEL)),
        "x_sample": nrm(ks[1], (DEC_BATCH, DEC_SEQ, D_MODEL)),
        "cache_k": nrm(ks[2], (DEPTH, n_pool, PAGE_SIZE, N_KV_HEADS, D_HEAD)),
        "cache_v": nrm(ks[3], (DEPTH, n_pool, PAGE_SIZE, N_KV_HEADS, D_HEAD)),
        "cache_kidx": nrm(ks[4], (DEPTH, n_pool, PAGE_SIZE, D_IDX)),
        "state_ret": nrm(ks[5], (DEPTH, DEC_BATCH, H_RET, DK_RET, DV_RET), 0.5),
        "page_table": jax.random.permutation(ks[6], n_pool)[:n_used].reshape(DEC_BATCH, n_pages).astype(jnp.int32),
        "meta_tokens": nrm(ks[7], (N_META, D_MODEL)),
        "norm_mix_w": gain(ks[8], (DEPTH, D_MODEL)),
        "w_in": nrm(ks[9], (DEPTH, D_MODEL, IN_COLS), D_MODEL ** -0.5),
        "w_ret_proj": nrm(ks[10], (DEPTH, RET_V_W, D_MODEL), RET_V_W ** -0.5),
        "dsa_q_norm_w": gain(ks[11], (DEPTH, D_HEAD)),
        "dsa_k_norm_w": gain(ks[12], (DEPTH, D_HEAD)),
        "idx_k_norm_w": gain(ks[13], (DEPTH, D_IDX)),
        "idx_k_norm_b": nrm(ks[14], (DEPTH, D_IDX), 0.02),
        "w_dsa_proj": nrm(ks[15], (DEPTH, DSA_Q_W, D_MODEL), DSA_Q_W ** -0.5),
        "w_out": nrm(ks[16], (DEPTH, D_MODEL, D_MODEL), D_MODEL ** -0.5),
        "norm_ffn_w": gain(ks[17], (DEPTH, D_MODEL)),
        "w_ffn_in": nrm(ks[18], (DEPTH, D_MODEL, 2 * D_FF), D_MODEL ** -0.5),
        "w_ffn_out": nrm(ks[19], (DEPTH, D_FF, D_MODEL), D_FF ** -0.5),
    }


def reference(x_prompt, x_sample, cache_k, cache_v, cache_kidx, state_ret, page_table,
              meta_tokens, norm_mix_w, w_in, w_ret_proj, dsa_q_norm_w, dsa_k_norm_w,
              idx_k_norm_w, idx_k_norm_b, w_dsa_proj, w_out, norm_ffn_w, w_ffn_in, w_ffn_out):
    B, n_real = x_prompt.shape[:2]
    h_p = jnp.concatenate(
        [jnp.broadcast_to(meta_tokens.astype(x_prompt.dtype)[None], (B, N_META, D_MODEL)), x_prompt], axis=1)
    h_s = x_sample
    past = page_table.shape[1] * PAGE_SIZE
    pos_p = jnp.arange(h_p.shape[1])
    pos_s = past + jnp.arange(x_sample.shape[1])
    kp, vp, kip, sp, ks_, vs_, kis, ss = [], [], [], [], [], [], [], []
    for l in range(DEPTH):
        proj = (norm_mix_w[l], w_in[l], dsa_q_norm_w[l], dsa_k_norm_w[l], idx_k_norm_w[l], idx_k_norm_b[l])
        post = (w_ret_proj[l], w_dsa_proj[l], w_out[l], norm_ffn_w[l], w_ffn_in[l], w_ffn_out[l])
        rq, rk, rv, rg, aq, ak, av, iq, ik, iw, gates = mixer_inputs(h_p, pos_p, *proj)
        o_ret, S_p = retention_prompt(rq, rk, rv)
        o_dsa = dsa_prompt(aq, ak, av, iq, ik, iw, n_real)
        kp.append(ak); vp.append(av); kip.append(ik); sp.append(S_p)
        h_p = merge_and_ffn(h_p, o_ret, rg, o_dsa, gates, *post)
        rq, rk, rv, rg, aq, ak, av, iq, ik, iw, gates = mixer_inputs(h_s, pos_s, *proj)
        S_s, o_ret = retention_chunk(state_ret[l], rq, rk, rv)
        o_dsa = dsa_sample(aq, ak, av, iq, ik, iw, cache_k[l], cache_v[l], cache_kidx[l], page_table)
        ks_.append(ak); vs_.append(av); kis.append(ik); ss.append(S_s)
        h_s = merge_and_ffn(h_s, o_ret, rg, o_dsa, gates, *post)
    y_prompt = h_p[:, N_META:]
    return (y_prompt, h_s, jnp.stack(kp), jnp.stack(vp), jnp.stack(kip), jnp.stack(sp),
            jnp.stack(ks_), jnp.stack(vs_), jnp.stack(kis), jnp.stack(ss))
```

```python
import contextlib
import numpy as np
import concourse.bass as bass
import concourse.mybir as mybir
from concourse.bass_utils import run_bass_kernel_spmd

F32 = mybir.dt.float32
BF16 = mybir.dt.bfloat16
I32 = mybir.dt.int32
AF = mybir.ActivationFunctionType
ALU = mybir.AluOpType
AX = mybir.AxisListType

DEBUG = False

D = 1024
NT = 18
ROWS = [16] + [128] * 17
TOK0 = [0] + [16 + 128 * i for i in range(17)]
TTOT = 2192
EPS = 1e-6
IN_COLS = 10312
C_RQ, C_RK, C_RV, C_RG = 0, 1024, 2048, 4096
C_DSA = 6144
N_DSA = 2120
C_GZ = 8264
DFF = 2816
IDX_SCALE = (8 ** -0.5) * (64 ** -0.5)
NBIS = 16
NEG = -1.0e30
GAM = [float(np.exp(np.log1p(-np.exp2(-5.0 - h)))) for h in range(4)]
ARENA_WORDS = 53200
NPOOL = 2560


class TT:
    def __init__(self, init=None):
        self.w = None
        self.r = list(init) if init else []
        self.dsem = None
        self.dcnt = 0


class Sched:
    ENG = ('pe', 'act', 'dve', 'pool', 'sp')

    def __init__(self, nc):
        self.nc = nc
        self.ops = {e: [] for e in self.ENG}
        self.cnt = {e: 0 for e in self.ENG}
        self.seen = {e: {} for e in self.ENG}
        self.dsems = []
        self.final = {}

    def _deps(self, eng, reads, writes):
        deps = {}

        def add(ev):
            if ev is None:
                return
            k, v = ev
            if k == eng and eng == 'pe':
                return
            if self.seen[eng].get(k, 0) >= v:
                return
            if deps.get(k, 0) < v:
                deps[k] = v
        for t in reads:
            add(t.w)
        for t in writes:
            add(t.w)
            for ev in t.r:
                add(ev)
        for k, v in deps.items():
            self.seen[eng][k] = v
        return list(deps.items())

    @staticmethod
    def _compact(evs):
        d = {}
        for k, v in evs:
            if d.get(k, 0) < v:
                d[k] = v
        return list(d.items())

    def op(self, eng, fn, reads=(), writes=()):
        waits = self._deps(eng, reads, writes)
        self.cnt[eng] += 1
        ev = (eng, self.cnt[eng])
        for t in reads:
            t.r.append(ev)
            if len(t.r) > 48:
                t.r = self._compact(t.r)
        for t in writes:
            t.w = ev
            t.r = []
        self.ops[eng].append((waits, fn, (eng, 1)))

    def dma(self, q, fn, tile, load, extra_reads=()):
        if load:
            waits = self._deps(q, extra_reads, (tile,))
        else:
            waits = self._deps(q, (tile,) + tuple(extra_reads), ())
        if tile.dsem is None:
            tile.dsem = 'd%d' % len(self.dsems)
            self.dsems.append(tile)
        tile.dcnt += 16
        ev = (tile.dsem, tile.dcnt)
        if load:
            tile.w = ev
            tile.r = []
            for t in extra_reads:
                t.r.append(ev)
        else:
            tile.r.append(ev)
            if len(tile.r) > 48:
                tile.r = self._compact(tile.r)
            self.final[tile.dsem] = tile.dcnt
        self.ops[q].append((waits, fn, (tile.dsem, 16)))

    def barrier(self):
        for e in self.ENG:
            waits = []
            for o in self.ENG:
                if o == e:
                    continue
                v = self.cnt[o]
                if v > self.seen[e].get(o, 0):
                    self.seen[e][o] = v
                    waits.append((o, v))
            for t in self.dsems:
                if t.dcnt > self.seen[e].get(t.dsem, 0):
                    self.seen[e][t.dsem] = t.dcnt
                    waits.append((t.dsem, t.dcnt))
            self.ops[e].append((waits, None, None))

    def emit(self):
        nc = self.nc
        with contextlib.ExitStack() as st:
            sems = {}
            for e in self.ENG:
                sems[e] = st.enter_context(nc.semaphore('s_' + e))
            for t in self.dsems:
                sems[t.dsem] = st.enter_context(nc.semaphore('s_' + t.dsem))
            block = st.enter_context(nc.Block())

            def run(engname, eng):
                for waits, fn, inc in self.ops[engname]:
                    for k, v in waits:
                        eng.wait_ge(sems[k], v)
                    if fn is None:
                        continue
                    ins = fn(eng)
                    ins.then_inc(sems[inc[0]], inc[1])
                if engname == 'sp':
                    for k, v in self.final.items():
                        eng.wait_ge(sems[k], v)

            @block.tensor
            def _(e):
                run('pe', e)

            @block.scalar
            def _(e):
                run('act', e)

            @block.vector
            def _(e):
                run('dve', e)

            @block.gpsimd
            def _(e):
                run('pool', e)

            @block.sync
            def _(e):
                run('sp', e)


class Buf:
    def __init__(self, ap, off, words, tts):
        self.ap = ap
        self.off = off
        self.words = words
        self.tts = tts
        self.t = tts[0]

    def __getitem__(self, key):
        return self.ap[key]


class Arena:
    def __init__(self, base, nwords):
        self.base = base
        self.free = [(0, nwords)]
        self.dead = {}

    def alloc(self, shape, dt, ntt=1):
        n = 1
        for s in shape[1:]:
            n *= s
        esz = 2 if dt == BF16 else 4
        words = (n * esz + 3) // 4
        words = (words + 15) // 16 * 16
        for i, (o, w) in enumerate(self.free):
            if w >= words:
                off = o
                if w == words:
                    self.free.pop(i)
                else:
                    self.free[i] = (o + words, w - words)
                break
        else:
            raise RuntimeError("arena out of SBUF: need %d words, free=%s" % (words, self.free))
        v = self.base[:, off:off + words]
        if dt != F32:
            v = v.bitcast(dt)
        v = v[:, 0:n]
        nd = len(shape) - 1
        if nd == 2:
            v = v.rearrange("p (a b) -> p a b", a=shape[1])
        elif nd == 3:
            v = v.rearrange("p (a b c) -> p a b c", a=shape[1], b=shape[2])
        v = v[:shape[0]]
        init = list(self.dead.items())
        return Buf(v, off, words, [TT(init) for _ in range(ntt)])

    def release(self, *bufs):
        for b in bufs:
            for t in b.tts:
                evs = list(t.r)
                if t.w is not None:
                    evs.append(t.w)
                for k, v in evs:
                    if self.dead.get(k, 0) < v:
                        self.dead[k] = v
            self.free.append((b.off, b.words))
        self.free.sort()
        merged = []
        for o, w in self.free:
            if merged and merged[-1][0] + merged[-1][1] == o:
                merged[-1] = (merged[-1][0], merged[-1][1] + w)
            else:
                merged.append((o, w))
        self.free = merged


def build_program():
    nc = bass.Bass("TRN2", target_bir_lowering=False)

    def din(name, shape, dt=F32):
        return nc.dram_tensor(name, list(shape), dt, kind="ExternalInput").ap()

    def dout(name, shape, dt=F32):
        return nc.dram_tensor(name, list(shape), dt, kind="ExternalOutput").ap()

    xp = din("xp", [2048, D])
    xs = din("xs", [128, D])
    meta = din("meta", [16, D])
    w_in = din("w_in", [D, IN_COLS])
    state = din("state", [16, 4, 256, 512])
    nmw = din("nmw", [128, 8])
    wq_bc = din("wq_bc", [128, 128])
    wk_bc = din("wk_bc", [128, 128])
    ikw_bc = din("ikw_bc", [128, 64])
    ikb_bc = din("ikb_bc", [128, 64])
    ident_d = din("ident", [128, 128])
    rot_d = din("rot", [NT, 128, 256])
    decT_d = din("decT", [128, 8, 128])
    qdec_d = din("qdec", [128, 8])
    kdec_d = din("kdec", [128, 12])
    bm_d = din("bm", [128, 16, 128])
    rm_d = din("rm", [128, 16])
    cbias_d = din("cbias", [128, 128])
    pow2_d = din("pow2", [128, NBIS + 1])
    thrcap_d = din("thrcap", [128, 1])
    pidx_d = din("pidx", [128, 1])
    bm2_d = din("bm2", [128, 8, 128])
    cbs_d = din("cbs", [128, 8])
    wrp_d = din("w_ret_proj", [2048, D])
    wdp_d = din("w_dsa_proj", [D, D])
    wo_d = din("w_out", [D, D])
    wfi_d = din("w_ffn_in", [D, 2 * DFF])
    wfo_d = din("w_ffn_out", [DFF, D])
    nfw_d = din("nfw", [128, 8])
    pt_d = din("pt2", [128, 64], I32)
    ck_d = din("ck", [NPOOL * 32, 4 * 256])
    cv_d = din("cv", [NPOOL * 32, 4 * 256])
    cki_d = din("cki", [NPOOL * 32, 4 * 64])

    o_yp = dout("o_yp", [2048, D])
    o_ys = dout("o_ys", [128, D])
    o_kp = dout("o_kp", [2064, 256])
    o_vp = dout("o_vp", [2064, 256])
    o_kip = dout("o_kip", [2064, 64])
    o_rp = dout("o_rp", [4, 256, 512])
    o_ks = dout("o_ks", [128, 256])
    o_vs = dout("o_vs", [128, 256])
    o_kis = dout("o_kis", [128, 64])
    o_rs = dout("o_rs", [16, 4, 256, 512])
    if DEBUG:
        o_dbg = dout("o_dbg", [128, 8, TTOT], BF16)

    S = Sched(nc)

    def pos0(j):
        return 0 if j == 0 else 16 + 128 * (j - 1)

    with contextlib.ExitStack() as top:
        arena_t = top.enter_context(nc.sbuf_tensor("arena", [128, ARENA_WORDS], F32))
        A = Arena(arena_t, ARENA_WORDS)
        pbank = [top.enter_context(nc.psum_tensor("pb%d" % i, [128, 512], F32)) for i in range(8)]
        t_pb = [TT() for _ in range(8)]

        def pf(i):
            return pbank[i]

        def pbf(i):
            return pbank[i][:, :].bitcast(BF16)

        ident = A.alloc([128, 128], BF16)
        nmw_sb = A.alloc([128, 8], F32)
        wq_sb = A.alloc([128, 128], F32)
        wk_sb = A.alloc([128, 128], F32)
        ikw_sb = A.alloc([128, 64], F32)
        ikb_sb = A.alloc([128, 64], F32)
        t_const = TT()
        S.dma('pool', lambda e: e.dma_start(out=ident[:, :], in_=ident_d[:, :]), ident.t, True)
        for dst, src in ((nmw_sb, nmw), (wq_sb, wq_bc), (wk_sb, wk_bc), (ikw_sb, ikw_bc), (ikb_sb, ikb_bc)):
            S.dma('sp', lambda e, dst=dst, src=src: e.dma_start(out=dst[:, :], in_=src[:, :]), t_const, True)

        xnT = A.alloc([128, 8, TTOT], BF16, ntt=NT)

        xt = [A.alloc([128, D], F32) for _ in range(2)]
        xsb = [A.alloc([128, D], BF16) for _ in range(2)]
        junk = A.alloc([128, D], F32)
        st1 = A.alloc([128, 2 * NT], F32, ntt=NT)
        for j in range(NT):
            r = ROWS[j]
            c0 = TOK0[j]
            bi = j % 2
            if j == 0:
                src = meta[:, :]
            elif j == 17:
                src = xs[:, :]
            else:
                src = xp[128 * (j - 1):128 * j, :]
            S.dma('sp', lambda e, bi=bi, r=r, src=src: e.dma_start(out=xt[bi][:r, :], in_=src), xt[bi].t, True)
            ss = st1[:r, 2 * j:2 * j + 1]
            rs = st1[:r, 2 * j + 1:2 * j + 2]
            tst = st1.tts[j]
            S.op('act', lambda e, bi=bi, r=r, ss=ss: e.activation(out=junk[:r, :], in_=xt[bi][:r, :], func=AF.Square, accum_out=ss),
                 [xt[bi].t], [junk.t, tst])
            S.op('act', lambda e, ss=ss, rs=rs: e.activation(out=rs, in_=ss, func=AF.Sqrt, scale=1.0 / D, bias=EPS), [tst], [tst])
            S.op('dve', lambda e, rs=rs: e.reciprocal(out=rs, in_=rs), [tst], [tst])
            S.op('dve', lambda e, bi=bi, r=r, rs=rs: e.tensor_scalar(out=xsb[bi][:r, :], in0=xt[bi][:r, :], scalar1=rs, scalar2=None, op0=ALU.mult),
                 [xt[bi].t, tst], [xsb[bi].t])
            pb = 6 + bi
            for k in range(8):
                S.op('pe', lambda e, bi=bi, r=r, k=k, pb=pb: e.transpose(pbf(pb)[:, k * 128:k * 128 + r], xsb[bi][:r, k * 128:(k + 1) * 128], ident[:r, :r]),
                     [xsb[bi].t, ident.t], [t_pb[pb]])
            for k in range(8):
                if k % 2 == 0:
                    S.op('act', lambda e, pb=pb, r=r, k=k, c0=c0: e.activation(out=xnT[:, k, c0:c0 + r], in_=pbf(pb)[:, k * 128:k * 128 + r],
                                                                             func=AF.Copy, scale=nmw_sb[:, k:k + 1]),
                         [t_pb[pb], t_const], [xnT.tts[j]])
                else:
                    S.op('dve', lambda e, pb=pb, r=r, k=k, c0=c0: e.tensor_scalar(out=xnT[:, k, c0:c0 + r], in0=pbf(pb)[:, k * 128:k * 128 + r],
                                                                                scalar1=nmw_sb[:, k:k + 1], scalar2=None, op0=ALU.mult),
                         [t_pb[pb], t_const], [xnT.tts[j]])
        A.release(xt[0], xt[1], xsb[0], xsb[1], junk, st1)

        o_dsaT = A.alloc([128, 8, TTOT], BF16, ntt=NT)
        wd = A.alloc([128, 8, N_DSA], BF16)
        for k in range(8):
            S.dma('pool', lambda e, k=k: e.dma_start(out=wd[:, k, :], in_=w_in[k * 128:(k + 1) * 128, C_DSA:C_DSA + N_DSA]), wd.t, True)
        st2 = A.alloc([128, 16], F32)
        junk2 = A.alloc([128, 128], F32)
        q_bf = A.alloc([128, 1024], BF16)
        k_f = A.alloc([128, 256], F32)
        k_bf = A.alloc([128, 256], BF16)
        v_f = A.alloc([128, 256], F32)
        iq_bf = A.alloc([128, 512], BF16)
        ki_f = A.alloc([128, 64], F32)
        ki_t = A.alloc([128, 64], F32)
        ik2_bf = A.alloc([128, 128], BF16)
        widx = A.alloc([128, NT, 8], F32, ntt=NT)
        V_bf = A.alloc([128, NT, 256], BF16, ntt=NT)
        KT = A.alloc([128, 2, 2064], BF16)
        ikT2 = A.alloc([128, 2064], BF16)
        QT = A.alloc([128, 8, 128], BF16)
        iqT = A.alloc([128, 4, 128], BF16)
        acc = A.alloc([128, 2064], F32)
        mask_bf = A.alloc([128, 2064], BF16)
        junk_bf = A.alloc([128, 2064], BF16)
        maskT = A.alloc([128, 17, 128], BF16)
        rl = [A.alloc([128, 2, 512], F32) for _ in range(2)]
        PT = [A.alloc([128, 4, 128], BF16) for _ in range(4)]
        rec = A.alloc([128, 512], F32)
        bs = A.alloc([128, 10 + NBIS], F32)
        tmpd = A.alloc([128, 128], F32)
        ones_bf = A.alloc([128, 128], BF16)
        cbias = A.alloc([128, 128], F32)
        pow2 = A.alloc([128, NBIS + 1], F32)
        thrcap = A.alloc([128, 1], F32)
        t_c2 = TT(list(A.dead.items()))
        for dst, src in ((cbias, cbias_d), (pow2, pow2_d), (thrcap, thrcap_d)):
            S.dma('sp', lambda e, dst=dst, src=src: e.dma_start(out=dst.ap, in_=src), t_c2, True)
        S.op('pool', lambda e: e.memset(ones_bf[:, :], 1.0), [], [ones_bf.t])
        att_cnt = [0, 0]
        QT_s = A.alloc([128, 8, 128], BF16)
        iqT8 = A.alloc([128, 8, 128], BF16)
        iq_dup = A.alloc([128, 8, 128], BF16)
        KT_s = A.alloc([128, 2, 128], BF16)
        ikT_s = A.alloc([128, 128], BF16)

        def idx_scores_prompt(r, L, iqT_ap, t_iqT, w_ap, t_w):
            for cc0 in range(0, L, 512):
                w = min(512, L - cc0)
                for hp in range(4):
                    rb = rl[att_cnt[0] % 2]
                    bo = 2 * (att_cnt[0] % 2)
                    att_cnt[0] += 1
                    for e_ in range(2):
                        S.op('pe', lambda e, e_=e_, hp=hp, w=w, cc0=cc0, bo=bo: e.matmul(pf(bo + e_)[:r, :w], iqT_ap[e_ * 64:(e_ + 1) * 64, hp, :], ikT2[e_ * 64:(e_ + 1) * 64, cc0:cc0 + w],
                                                                                        start=True, stop=True), [t_iqT, ikT2.t], [t_pb[bo + e_]])
                        S.op('act', lambda e, e_=e_, w=w, rb=rb, bo=bo: e.activation(out=rb[:r, e_, :w], in_=pf(bo + e_)[:r, :w], func=AF.Relu), [t_pb[bo + e_]], [rb.t])
                    for e_ in range(2):
                        h = 2 * hp + e_
                        if h == 0:
                            S.op('dve', lambda e, w=w, cc0=cc0, rb=rb: e.tensor_scalar(out=acc[:r, cc0:cc0 + w], in0=rb[:r, 0, :w], scalar1=w_ap(0), scalar2=None, op0=ALU.mult),
                                 [rb.t, t_w], [acc.t])
                        else:
                            S.op('dve', lambda e, w=w, cc0=cc0, rb=rb, e_=e_, h=h: e.scalar_tensor_tensor(out=acc[:r, cc0:cc0 + w], in0=rb[:r, e_, :w], scalar=w_ap(h),
                                                                                                        in1=acc[:r, cc0:cc0 + w], op0=ALU.mult, op1=ALU.add),
                                 [rb.t, t_w, acc.t], [acc.t])

        def thresh_mask(r, L, ktiles, diag0, dw, cb_ap, need_bis, cap, diag_min):
            S.op('dve', lambda e: e.tensor_tensor(out=acc[:r, diag0:diag0 + dw], in0=acc[:r, diag0:diag0 + dw], in1=cb_ap, op=ALU.add), [acc.t, t_c2], [acc.t])
            if not need_bis:
                S.op('dve', lambda e: e.memset(bs[:r, 6:7], -1.0e29), [], [bs.t])
            else:
                S.op('dve', lambda e: e.tensor_reduce(out=bs[:r, 0:1], in_=acc[:r, :L], axis=AX.X, op=ALU.max), [acc.t], [bs.t])
                S.op('dve', lambda e: e.tensor_reduce(out=bs[:r, 2:3], in_=acc[:r, :diag0], axis=AX.X, op=ALU.min), [acc.t], [bs.t])
                if diag_min:
                    S.op('dve', lambda e: e.scalar_tensor_tensor(out=tmpd[:r, :dw], in0=cb_ap, scalar=-2.0, in1=acc[:r, diag0:diag0 + dw], op0=ALU.mult, op1=ALU.add),
                         [acc.t, t_c2], [tmpd.t])
                    S.op('dve', lambda e: e.tensor_reduce(out=bs[:r, 1:2], in_=tmpd[:r, :dw], axis=AX.X, op=ALU.min), [tmpd.t], [bs.t])
                    S.op('dve', lambda e: e.tensor_tensor(out=bs[:r, 2:3], in0=bs[:r, 1:2], in1=bs[:r, 2:3], op=ALU.min), [bs.t], [bs.t])
                S.op('dve', lambda e: e.tensor_tensor(out=bs[:r, 7:8], in0=bs[:r, 0:1], in1=bs[:r, 2:3], op=ALU.subtract), [bs.t], [bs.t])
                S.op('dve', lambda e: e.tensor_scalar(out=bs[:r, 8:9 + NBIS], in0=pow2[:r, :], scalar1=bs[:r, 7:8], scalar2=None, op0=ALU.mult), [bs.t, t_c2], [bs.t])
                S.op('dve', lambda e: e.tensor_tensor(out=bs[:r, 3:4], in0=bs[:r, 2:3], in1=bs[:r, 8:9], op=ALU.add), [bs.t], [bs.t])
                for k in range(NBIS):
                    S.op('dve', lambda e: e.tensor_scalar(out=junk_bf[:r, :L], in0=acc[:r, :L], scalar1=bs[:r, 3:4], scalar2=None, op0=ALU.is_ge, op1=ALU.add,
                                                          accum_out=bs[:r, 4:5]), [acc.t, bs.t], [junk_bf.t, bs.t])
                    S.op('dve', lambda e, k=k: e.tensor_scalar(out=bs[:r, 5:6], in0=bs[:r, 4:5], scalar1=255.5, scalar2=bs[:r, 8 + k:9 + k], op0=ALU.is_ge, op1=ALU.mult),
                         [bs.t], [bs.t])
                    S.op('dve', lambda e, k=k: e.scalar_tensor_tensor(out=bs[:r, 3:4], in0=bs[:r, 3:4], scalar=bs[:r, 9 + k:10 + k], in1=bs[:r, 5:6], op0=ALU.subtract, op1=ALU.add),
                         [bs.t], [bs.t])
                if cap:
                    S.op('dve', lambda e: e.tensor_tensor(out=bs[:r, 2:3], in0=bs[:r, 3:4], in1=bs[:r, 8 + NBIS:9 + NBIS], op=ALU.subtract), [bs.t], [bs.t])
                    S.op('dve', lambda e: e.tensor_tensor(out=bs[:r, 6:7], in0=bs[:r, 2:3], in1=thrcap[:r, 0:1], op=ALU.min), [bs.t, t_c2], [bs.t])
                else:
                    S.op('dve', lambda e: e.tensor_tensor(out=bs[:r, 6:7], in0=bs[:r, 3:4], in1=bs[:r, 8 + NBIS:9 + NBIS], op=ALU.subtract), [bs.t], [bs.t])
            S.op('dve', lambda e: e.tensor_scalar(out=mask_bf[:r, :L], in0=acc[:r, :L], scalar1=bs[:r, 6:7], scalar2=None, op0=ALU.is_ge), [acc.t, bs.t], [mask_bf.t])
            for g0 in range(0, len(ktiles), 8):
                grp = ktiles[g0:g0 + 8]
                pb = 6 + (g0 // 8) % 2
                for s_, (kp0, kr, vi) in enumerate(grp):
                    S.op('pe', lambda e, s_=s_, kp0=kp0, kr=kr, pb=pb: e.transpose(pbf(pb)[:kr, s_ * 128:s_ * 128 + r], mask_bf[:r, kp0:kp0 + kr], ident[:r, :r]),
                         [mask_bf.t, ident.t], [t_pb[pb]])
                n_ = len(grp)
                S.op('act', lambda e, g0=g0, n_=n_, pb=pb: e.activation(out=maskT[:, g0:g0 + n_, :r], in_=pbf(pb)[:, 0:n_ * 128].rearrange("p (s q) -> p s q", s=n_)[:, :, :r],
                                                                       func=AF.Copy), [t_pb[pb]], [maskT.t])

        def attn_prompt(r, ktiles, qT_ap, t_qT, out_ap_fn, t_out):
            nk = len(ktiles)
            units = [(n, ii) + ktiles[ii] for n in range(2) for ii in range(nk)]
            LOOK = 2
            bufs = {}

            def emit_st(ui):
                n, ii, kp0, kr, vi = units[ui]
                sbk = att_cnt[1] % 4
                pt = PT[att_cnt[1] % 4]
                att_cnt[1] += 1
                bufs[ui] = pt
                S.op('pe', lambda e: e.matmul(pf(sbk)[:kr, 0:4 * r].rearrange("p (h q) -> p h q", h=4), KT[:, n, kp0:kp0 + kr],
                                              qT_ap[:, 4 * n:4 * n + 4, :], start=True, stop=True), [KT.t, t_qT], [t_pb[sbk]])
                S.op('act', lambda e: e.activation(out=pt[:kr, :, :r], in_=pf(sbk)[:kr, 0:4 * r].rearrange("p (h q) -> p h q", h=4),
                                                   func=AF.Exp, scale=128 ** -0.5), [t_pb[sbk]], [pt.t])
                S.op('dve', lambda e: e.tensor_tensor(out=pt[:kr, :, :r], in0=pt[:kr, :, :r],
                                                      in1=maskT[:kr, ii:ii + 1, :r].to_broadcast([kr, 4, r]), op=ALU.mult), [pt.t, maskT.t], [pt.t])

            def emit_pv(ui):
                n, ii, kp0, kr, vi = units[ui]
                pt = bufs.pop(ui)
                S.op('pe', lambda e: e.matmul(pf(4)[:, 0:4 * r].rearrange("p (h q) -> p h q", h=4), V_bf[:kr, vi, n * 128:(n + 1) * 128],
                                              pt[:kr, :, :r], start=(ii == 0), stop=(ii == nk - 1)), [V_bf.tts[vi], pt.t], [t_pb[4]])
                S.op('pe', lambda e: e.matmul(pf(5)[:, 0:4 * r].rearrange("p (h q) -> p h q", h=4), ones_bf[:kr, :],
                                              pt[:kr, :, :r], start=(ii == 0), stop=(ii == nk - 1)), [ones_bf.t, pt.t], [t_pb[5]])
                if ii == nk - 1:
                    S.op('dve', lambda e: e.reciprocal(out=rec[:, 0:4 * r], in_=pf(5)[:, 0:4 * r]), [t_pb[5]], [rec.t])
                    S.op('dve', lambda e: e.tensor_tensor(out=out_ap_fn(n), in0=pf(4)[:, 0:4 * r].rearrange("p (h q) -> p h q", h=4),
                                                          in1=rec[:, 0:4 * r].rearrange("p (h q) -> p h q", h=4), op=ALU.mult), [t_pb[4], rec.t], [t_out])

            for i in range(len(units) + LOOK):
                if i < len(units):
                    emit_st(i)
                if i - LOOK >= 0:
                    emit_pv(i - LOOK)

        chunks = [(0, 512), (512, 512), (1024, 512), (1536, 512), (2048, 72)]
        for j in range(NT):
            r = ROWS[j]
            c0 = TOK0[j]
            for c, (cc0, cw) in enumerate(chunks):
                for k in range(8):
                    S.op('pe', lambda e, c=c, cc0=cc0, cw=cw, k=k, c0=c0, r=r: e.matmul(pf(c)[:r, :cw], xnT[:, k, c0:c0 + r], wd[:, k, cc0:cc0 + cw],
                                                                                     start=(k == 0), stop=(k == 7)),
                         [xnT.tts[j], wd.t], [t_pb[c]])
            for h in range(10):
                if h < 8:
                    src = pf(h // 4)[:r, (h % 4) * 128:(h % 4) * 128 + 128]
                    tp = t_pb[h // 4]
                else:
                    src = pf(2)[:r, (h - 8) * 128:(h - 8) * 128 + 128]
                    tp = t_pb[2]
                S.op('act', lambda e, src=src, r=r, h=h: e.activation(out=junk2[:r, :], in_=src, func=AF.Square, accum_out=st2[:r, h:h + 1]),
                     [tp], [junk2.t, st2.t])
            S.op('act', lambda e, r=r: e.activation(out=st2[:r, 0:10], in_=st2[:r, 0:10], func=AF.Sqrt, scale=1.0 / 128, bias=EPS), [st2.t], [st2.t])
            S.op('dve', lambda e, r=r: e.reciprocal(out=st2[:r, 0:10], in_=st2[:r, 0:10]), [st2.t], [st2.t])
            for h in range(8):
                src = pf(h // 4)[:r, (h % 4) * 128:(h % 4) * 128 + 128]
                S.op('dve', lambda e, src=src, r=r, h=h: e.scalar_tensor_tensor(out=q_bf[:r, h * 128:(h + 1) * 128], in0=src, scalar=st2[:r, h:h + 1],
                                                                              in1=wq_sb[:r, :], op0=ALU.mult, op1=ALU.mult),
                     [t_pb[h // 4], st2.t, t_const], [q_bf.t])
            for n in range(2):
                src = pf(2)[:r, n * 128:(n + 1) * 128]
                S.op('dve', lambda e, src=src, r=r, n=n: e.scalar_tensor_tensor(out=k_f[:r, n * 128:(n + 1) * 128], in0=src, scalar=st2[:r, 8 + n:9 + n],
                                                                              in1=wk_sb[:r, :], op0=ALU.mult, op1=ALU.mult),
                     [t_pb[2], st2.t, t_const], [k_f.t])
            S.op('act', lambda e, r=r: e.activation(out=k_bf[:r, :], in_=k_f[:r, :], func=AF.Copy), [k_f.t], [k_bf.t])
            S.op('act', lambda e, r=r: e.activation(out=v_f[:r, :], in_=pf(2)[:r, 256:512], func=AF.Copy), [t_pb[2]], [v_f.t])
            S.op('dve', lambda e, r=r, j=j: e.tensor_copy(out=V_bf[:r, j, :], in_=pf(2)[:r, 256:512]), [t_pb[2]], [V_bf.tts[j]])
            S.op('act', lambda e, r=r: e.activation(out=iq_bf[:r, :], in_=pf(3)[:r, :], func=AF.Copy), [t_pb[3]], [iq_bf.t])
            S.op('act', lambda e, r=r: e.activation(out=junk2[:r, 0:64], in_=pf(4)[:r, 0:64], func=AF.Copy, accum_out=st2[:r, 10:11]),
                 [t_pb[4]], [junk2.t, st2.t])
            S.op('act', lambda e, r=r: e.activation(out=junk2[:r, 0:64], in_=pf(4)[:r, 0:64], func=AF.Square, accum_out=st2[:r, 11:12]),
                 [t_pb[4]], [junk2.t, st2.t])
            S.op('dve', lambda e, r=r: e.tensor_scalar(out=st2[:r, 10:12], in0=st2[:r, 10:12], scalar1=1.0 / 64, scalar2=None, op0=ALU.mult), [st2.t], [st2.t])
            S.op('dve', lambda e, r=r: e.tensor_tensor(out=st2[:r, 12:13], in0=st2[:r, 10:11], in1=st2[:r, 10:11], op=ALU.mult), [st2.t], [st2.t])
            S.op('dve', lambda e, r=r: e.tensor_tensor(out=st2[:r, 12:13], in0=st2[:r, 11:12], in1=st2[:r, 12:13], op=ALU.subtract), [st2.t], [st2.t])
            S.op('act', lambda e, r=r: e.activation(out=st2[:r, 12:13], in_=st2[:r, 12:13], func=AF.Sqrt, scale=1.0, bias=EPS), [st2.t], [st2.t])
            S.op('dve', lambda e, r=r: e.reciprocal(out=st2[:r, 12:13], in_=st2[:r, 12:13]), [st2.t], [st2.t])
            S.op('dve', lambda e, r=r: e.tensor_scalar(out=ki_t[:r, :], in0=pf(4)[:r, 0:64], scalar1=st2[:r, 10:11], scalar2=st2[:r, 12:13],
                                                       op0=ALU.subtract, op1=ALU.mult), [t_pb[4], st2.t], [ki_t.t])
            S.op('dve', lambda e, r=r: e.tensor_tensor(out=ki_t[:r, :], in0=ki_t[:r, :], in1=ikw_sb[:r, :], op=ALU.mult), [ki_t.t, t_const], [ki_t.t])
            S.op('dve', lambda e, r=r: e.tensor_tensor(out=ki_f[:r, :], in0=ki_t[:r, :], in1=ikb_sb[:r, :], op=ALU.add), [ki_t.t, t_const], [ki_f.t])
            S.op('act', lambda e, r=r: e.activation(out=ik2_bf[:r, 0:64], in_=ki_f[:r, :], func=AF.Copy), [ki_f.t], [ik2_bf.t])
            S.op('act', lambda e, r=r: e.activation(out=ik2_bf[:r, 64:128], in_=ki_f[:r, :], func=AF.Copy), [ki_f.t], [ik2_bf.t])
            S.op('act', lambda e, r=r, j=j: e.activation(out=widx[:r, j, :], in_=pf(4)[:r, 64:72], func=AF.Copy, scale=IDX_SCALE), [t_pb[4]], [widx.tts[j]])
            if j == 17:
                dk, dv, dki = o_ks[:, :], o_vs[:, :], o_kis[:, :]
            else:
                p0 = pos0(j)
                dk, dv, dki = o_kp[p0:p0 + r, :], o_vp[p0:p0 + r, :], o_kip[p0:p0 + r, :]
            S.dma('sp', lambda e, r=r, dk=dk: e.dma_start(out=dk, in_=k_f[:r, :]), k_f.t, False)
            S.dma('sp', lambda e, r=r, dv=dv: e.dma_start(out=dv, in_=v_f[:r, :]), v_f.t, False)
            S.dma('sp', lambda e, r=r, dki=dki: e.dma_start(out=dki, in_=ki_f[:r, :]), ki_f.t, False)
            p0 = pos0(j) if j < 17 else 0
            for h in range(8):
                S.op('pe', lambda e, h=h, r=r: e.transpose(pbf(6)[:, h * 128:h * 128 + r], q_bf[:r, h * 128:(h + 1) * 128], ident[:r, :r]), [q_bf.t, ident.t], [t_pb[6]])
            QTd = QT if j < 17 else QT_s
            S.op('dve', lambda e, r=r, QTd=QTd: e.tensor_copy(out=QTd[:, :, :r], in_=pbf(6)[:, :].rearrange("p (s q) -> p s q", s=8)[:, :, :r]), [t_pb[6]], [QTd.t])
            for n in range(2):
                S.op('pe', lambda e, n=n, r=r: e.transpose(pbf(7)[:, n * 128:n * 128 + r], k_bf[:r, n * 128:(n + 1) * 128], ident[:r, :r]), [k_bf.t, ident.t], [t_pb[7]])
            for hp in range(4):
                S.op('pe', lambda e, hp=hp, r=r: e.transpose(pbf(7)[:, (2 + hp) * 128:(2 + hp) * 128 + r], iq_bf[:r, hp * 128:(hp + 1) * 128], ident[:r, :r]),
                     [iq_bf.t, ident.t], [t_pb[7]])
            S.op('pe', lambda e, r=r: e.transpose(pbf(7)[:, 6 * 128:6 * 128 + r], ik2_bf[:r, :], ident[:r, :r]), [ik2_bf.t, ident.t], [t_pb[7]])
            if j == 17:
                S.op('act', lambda e: e.activation(out=KT_s[:, :, :], in_=pbf(7)[:, 0:256].rearrange("p (s q) -> p s q", s=2), func=AF.Copy), [t_pb[7]], [KT_s.t])
                S.op('act', lambda e: e.activation(out=ikT_s[:, :], in_=pbf(7)[:, 768:896], func=AF.Copy), [t_pb[7]], [ikT_s.t])
                S.op('dve', lambda e: e.tensor_copy(out=iq_dup[:, :, 0:64], in_=iq_bf[:, :].rearrange("p (h d) -> p h d", h=8)), [iq_bf.t], [iq_dup.t])
                S.op('dve', lambda e: e.tensor_copy(out=iq_dup[:, :, 64:128], in_=iq_bf[:, :].rearrange("p (h d) -> p h d", h=8)), [iq_bf.t], [iq_dup.t])
                for h in range(8):
                    S.op('pe', lambda e, h=h: e.transpose(pbf(6)[:, h * 128:(h + 1) * 128], iq_dup[:, h, :], ident[:, :]), [iq_dup.t, ident.t], [t_pb[6]])
                S.op('act', lambda e: e.activation(out=iqT8[:, :, :], in_=pbf(6)[:, :].rearrange("p (s q) -> p s q", s=8), func=AF.Copy), [t_pb[6]], [iqT8.t])
                continue
            S.op('act', lambda e, r=r, p0=p0: e.activation(out=KT[:, :, p0:p0 + r], in_=pbf(7)[:, 0:256].rearrange("p (s q) -> p s q", s=2)[:, :, :r], func=AF.Copy),
                 [t_pb[7]], [KT.t])
            S.op('act', lambda e, r=r: e.activation(out=iqT[:, :, :r], in_=pbf(7)[:, 256:768].rearrange("p (s q) -> p s q", s=4)[:, :, :r], func=AF.Copy),
                 [t_pb[7]], [iqT.t])
            S.op('act', lambda e, r=r, p0=p0: e.activation(out=ikT2[:, p0:p0 + r], in_=pbf(7)[:, 768:768 + r], func=AF.Copy), [t_pb[7]], [ikT2.t])
            kts = [(pos0(i), ROWS[i], i) for i in range(j + 1)]
            idx_scores_prompt(r, p0 + r, iqT[:, :, :r], iqT.t, (lambda h, r=r, j=j: widx[:r, j, h:h + 1]), widx.tts[j])
            thresh_mask(r, p0 + r, kts, p0, r, cbias[:r, :r], (p0 + r - 1 >= 256), (j == 2), True)
            attn_prompt(r, kts, QT[:, :, :r], QT.t, (lambda n, r=r, c0=c0: o_dsaT[:, 4 * n:4 * n + 4, c0:c0 + r]), o_dsaT.tts[j])
        A.release(wd, st2, junk2, q_bf, k_f, k_bf, v_f, iq_bf, ki_f, ki_t, ik2_bf, QT, iqT, iq_dup, ikT2, tmpd)
        ptb = A.alloc([128, 64], I32)
        idx_i = A.alloc([128, 64], I32)
        pidx = A.alloc([128, 1], F32)
        S.dma('sp', lambda e: e.dma_start(out=ptb[:, :], in_=pt_d), ptb.t, True)
        S.dma('sp', lambda e: e.dma_start(out=pidx[:, :], in_=pidx_d), pidx.t, True)
        S.op('dve', lambda e: e.tensor_scalar(out=idx_i[:, :], in0=ptb[:, :], scalar1=32.0, scalar2=pidx[:, 0:1], op0=ALU.mult, op1=ALU.add),
             [ptb.t, pidx.t], [idx_i.t])
        ikT_all = A.alloc([128, 8, 2056], BF16)
        kipg = [A.alloc([128, 16, 64], F32) for _ in range(2)]
        ik2pg = A.alloc([128, 16, 128], BF16)
        iqm = A.alloc([128, 8, 8, 128], BF16)
        bm2 = A.alloc([128, 8, 128], BF16)
        cbs = A.alloc([128, 8], F32)
        S.dma('pool', lambda e: e.dma_start(out=bm2.ap, in_=bm2_d), bm2.t, True)
        S.dma('sp', lambda e: e.dma_start(out=cbs.ap, in_=cbs_d), cbs.t, True)
        S.op('dve', lambda e: e.tensor_tensor(out=iqm[:, :, :, :], in0=iqT8[:, :, :].unsqueeze(2).to_broadcast([128, 8, 8, 128]),
                                              in1=bm2[:, :, :].unsqueeze(1).to_broadcast([128, 8, 8, 128]), op=ALU.mult), [iqT8.t, bm2.t], [iqm.t])
        for p in range(8):
            for half in range(2):
                b = p + 8 * half
                for hh in range(4):
                    S.dma('pool', lambda e, b=b, half=half, hh=hh: e.indirect_dma_start(out=kipg[half][:, hh * 4:(hh + 1) * 4, :].rearrange("p a b -> p (a b)"), out_offset=None,
                                                                                    in_=cki_d[:, :], in_offset=bass.IndirectOffsetOnAxis(ap=idx_i[:, 4 * b + hh:4 * b + hh + 1], axis=0)),
                          kipg[half].t, True, extra_reads=[idx_i.t])
                S.op('act', lambda e, half=half: e.activation(out=ik2pg[:, :, half * 64:(half + 1) * 64], in_=kipg[half][:, :, :], func=AF.Copy), [kipg[half].t], [ik2pg.t])
            for g in range(2):
                for s_ in range(8):
                    S.op('pe', lambda e, g=g, s_=s_: e.transpose(pbf(6 + g)[:, s_ * 128:(s_ + 1) * 128], ik2pg[:, g * 8 + s_, :], ident[:, :]),
                         [ik2pg.t, ident.t], [t_pb[6 + g]])
                S.op('act', lambda e, g=g, p=p: e.activation(out=ikT_all[:, p, g * 1024:(g + 1) * 1024], in_=pbf(6 + g)[:, :], func=AF.Copy), [t_pb[6 + g]], [ikT_all.t])
            S.op('dve', lambda e, p=p: e.tensor_copy(out=ikT_all[0:64, p, 2048:2056], in_=ikT_s[0:64, p * 8:(p + 1) * 8]), [ikT_s.t], [ikT_all.t])
            S.op('dve', lambda e, p=p: e.tensor_copy(out=ikT_all[64:128, p, 2048:2056], in_=ikT_s[64:128, (p + 8) * 8:(p + 9) * 8]), [ikT_s.t], [ikT_all.t])
        for cc0 in range(0, 2056, 512):
            w = min(512, 2056 - cc0)
            for hp in range(4):
                rb = rl[att_cnt[0] % 2]
                att_cnt[0] += 1
                for e_ in range(2):
                    h = 2 * hp + e_
                    for p in range(8):
                        S.op('pe', lambda e, e_=e_, h=h, p=p, w=w, cc0=cc0: e.matmul(pf(e_)[:, :w], iqm[:, h, p, :], ikT_all[:, p, cc0:cc0 + w], start=(p == 0), stop=(p == 7)),
                             [iqm.t, ikT_all.t], [t_pb[e_]])
                    S.op('act', lambda e, e_=e_, w=w, rb=rb: e.activation(out=rb[:, e_, :w], in_=pf(e_)[:, :w], func=AF.Relu), [t_pb[e_]], [rb.t])
                for e_ in range(2):
                    h = 2 * hp + e_
                    if h == 0:
                        S.op('dve', lambda e, w=w, cc0=cc0, rb=rb: e.tensor_scalar(out=acc[:, cc0:cc0 + w], in0=rb[:, 0, :w], scalar1=widx[:, 17, 0:1], scalar2=None, op0=ALU.mult),
                             [rb.t, widx.tts[17]], [acc.t])
                    else:
                        S.op('dve', lambda e, w=w, cc0=cc0, rb=rb, e_=e_, h=h: e.scalar_tensor_tensor(out=acc[:, cc0:cc0 + w], in0=rb[:, e_, :w], scalar=widx[:, 17, h:h + 1],
                                                                                                    in1=acc[:, cc0:cc0 + w], op0=ALU.mult, op1=ALU.add),
                             [rb.t, widx.tts[17], acc.t], [acc.t])
        skt = [(pg * 128, 128, pg) for pg in range(16)] + [(2048, 8, 16)]
        thresh_mask(128, 2056, skt, 2048, 8, cbs[:, :], True, False, False)
        A.release(ikT_all, kipg[0], kipg[1], ik2pg, iqm, bm2, cbs, acc, junk_bf, rl[0], rl[1], mask_bf)
        Kg = [A.alloc([128, 16, 256], BF16) for _ in range(2)]
        Vg = [A.alloc([128, 16, 256], BF16) for _ in range(2)]
        Vn = [A.alloc([8, 256], BF16) for _ in range(2)]
        KTb = [KT, A.alloc([128, 2, 2064], BF16)]
        PTs = [A.alloc([128, 16, 4, 8], BF16) for _ in range(2)]
        PTn = [A.alloc([128, 4, 8], BF16) for _ in range(2)]
        pend = [None]
        sample_bufs = {}

        def emit_st_s(b, n, db, KTc):
            pts = PTs[att_cnt[1] % 2]
            ptn = PTn[att_cnt[1] % 2]
            sbk = 2 + att_cnt[1] % 2
            att_cnt[1] += 1
            sample_bufs[(b, n)] = (pts, ptn)
            qv = QT_s[:, 4 * n:4 * n + 4, b * 8:(b + 1) * 8]
            for pg in range(16):
                S.op('pe', lambda e, pg=pg: e.matmul(pf(sbk)[:, pg * 32:(pg + 1) * 32].rearrange("p (h q) -> p h q", h=4), KTc[:, n, pg * 128:(pg + 1) * 128],
                                                     qv, start=True, stop=True), [KTc.t, QT_s.t], [t_pb[sbk]])
            S.op('pe', lambda e: e.matmul(pf(0)[:8, 0:32].rearrange("p (h q) -> p h q", h=4), KTc[:, n, 2048:2056], qv, start=True, stop=True),
                 [KTc.t, QT_s.t], [t_pb[0]])
            S.op('act', lambda e: e.activation(out=pts[:, :, :, :], in_=pf(sbk)[:, :].rearrange("p (g h q) -> p g h q", g=16, h=4), func=AF.Exp, scale=128 ** -0.5),
                 [t_pb[sbk]], [pts.t])
            S.op('act', lambda e: e.activation(out=ptn[:8, :, :], in_=pf(0)[:8, 0:32].rearrange("p (h q) -> p h q", h=4), func=AF.Exp, scale=128 ** -0.5),
                 [t_pb[0]], [ptn.t])
            S.op('dve', lambda e: e.tensor_tensor(out=pts[:, :, :, :], in0=pts[:, :, :, :],
                                                  in1=maskT[:, 0:16, b * 8:(b + 1) * 8].unsqueeze(2).to_broadcast([128, 16, 4, 8]), op=ALU.mult), [pts.t, maskT.t], [pts.t])
            S.op('dve', lambda e: e.tensor_tensor(out=ptn[:8, :, :], in0=ptn[:8, :, :],
                                                  in1=maskT[:8, 16:17, b * 8:(b + 1) * 8].to_broadcast([8, 4, 8]), op=ALU.mult), [ptn.t, maskT.t], [ptn.t])

        def emit_pv_s(b, n, db, pb_):
            pts, ptn = pb_
            cs_ = TOK0[17] + b * 8
            for pg in range(17):
                if pg < 16:
                    lv = Vg[db][:, pg, n * 128:(n + 1) * 128]
                    lo_ = ones_bf[:, :]
                    rv = pts[:, pg, :, :]
                    rt = pts.t
                    vt = Vg[db].t
                else:
                    lv = Vn[db][:8, n * 128:(n + 1) * 128]
                    lo_ = ones_bf[:8, :]
                    rv = ptn[:8, :, :]
                    rt = ptn.t
                    vt = Vn[db].t
                S.op('pe', lambda e, lv=lv, rv=rv, pg=pg: e.matmul(pf(4)[:, 0:32].rearrange("p (h q) -> p h q", h=4), lv, rv, start=(pg == 0), stop=(pg == 16)),
                     [vt, rt], [t_pb[4]])
                S.op('pe', lambda e, lo_=lo_, rv=rv, pg=pg: e.matmul(pf(5)[:, 0:32].rearrange("p (h q) -> p h q", h=4), lo_, rv, start=(pg == 0), stop=(pg == 16)),
                     [ones_bf.t, rt], [t_pb[5]])
            S.op('dve', lambda e: e.reciprocal(out=rec[:, 0:32], in_=pf(5)[:, 0:32]), [t_pb[5]], [rec.t])
            S.op('dve', lambda e: e.tensor_tensor(out=o_dsaT[:, 4 * n:4 * n + 4, cs_:cs_ + 8], in0=pf(4)[:, 0:32].rearrange("p (h q) -> p h q", h=4),
                                                  in1=rec[:, 0:32].rearrange("p (h q) -> p h q", h=4), op=ALU.mult), [t_pb[4], rec.t], [o_dsaT.tts[17]])

        for b in range(16):
            db = b % 2
            KTc = KTb[db]
            for hh in range(4):
                S.dma('pool', lambda e, b=b, db=db, hh=hh: e.indirect_dma_start(out=Kg[db][:, hh * 4:(hh + 1) * 4, :].rearrange("p a b -> p (a b)"), out_offset=None, in_=ck_d[:, :],
                                                                            in_offset=bass.IndirectOffsetOnAxis(ap=idx_i[:, 4 * b + hh:4 * b + hh + 1], axis=0)),
                      Kg[db].t, True, extra_reads=[idx_i.t])
            for hh in range(4):
                S.dma('pool', lambda e, b=b, db=db, hh=hh: e.indirect_dma_start(out=Vg[db][:, hh * 4:(hh + 1) * 4, :].rearrange("p a b -> p (a b)"), out_offset=None, in_=cv_d[:, :],
                                                                            in_offset=bass.IndirectOffsetOnAxis(ap=idx_i[:, 4 * b + hh:4 * b + hh + 1], axis=0)),
                      Vg[db].t, True, extra_reads=[idx_i.t])
            for g in range(4):
                pb = 6 + g % 2
                for pl in range(4):
                    for n in range(2):
                        S.op('pe', lambda e, g=g, pl=pl, n=n, pb=pb, db=db: e.transpose(pbf(pb)[:, (pl * 2 + n) * 128:(pl * 2 + n + 1) * 128], Kg[db][:, g * 4 + pl, n * 128:(n + 1) * 128], ident[:, :]),
                             [Kg[db].t, ident.t], [t_pb[pb]])
                S.op('act', lambda e, g=g, pb=pb, KTc=KTc: e.activation(out=KTc[:, :, g * 512:(g + 1) * 512].rearrange("p n (g q) -> p n g q", g=4),
                                                                       in_=pbf(pb)[:, :].rearrange("p (g n q) -> p n g q", g=4, n=2), func=AF.Copy), [t_pb[pb]], [KTc.t])
            S.op('dve', lambda e, b=b, KTc=KTc: e.tensor_copy(out=KTc[:, :, 2048:2056], in_=KT_s[:, :, b * 8:(b + 1) * 8]), [KT_s.t], [KTc.t])
            S.dma('sp', lambda e, b=b, db=db: e.dma_start(out=Vn[db][0:8, :], in_=V_bf[b * 8:(b + 1) * 8, 17, :]), Vn[db].t, True, extra_reads=[V_bf.tts[17]])
            for n in range(2):
                emit_st_s(b, n, db, KTc)
                if pend[0] is not None:
                    emit_pv_s(*pend[0])
                pend[0] = (b, n, db, sample_bufs.pop((b, n)))
        emit_pv_s(*pend[0])
        A.release(widx, V_bf, KT, maskT, PT[0], PT[1], PT[2], PT[3], rec, bs, ones_bf, cbias, pow2, thrcap,
                  QT_s, iqT8, KT_s, ikT_s, ptb, idx_i, pidx, Kg[0], Kg[1], Vg[0], Vg[1], Vn[0], Vn[1], KTb[1], PTs[0], PTs[1], PTn[0], PTn[1])

        o_retT = A.alloc([128, 16, TTOT], BF16, ntt=NT)
        decT = A.alloc([128, 8, 128], F32)
        qdec = A.alloc([128, 8], F32)
        kdec = A.alloc([128, 12], F32)
        bm = A.alloc([128, 16, 128], BF16)
        rm = A.alloc([128, 16], F32)
        t_rc = TT(list(A.dead.items()))
        for dst, src in ((decT, decT_d), (qdec, qdec_d), (kdec, kdec_d), (rm, rm_d)):
            S.dma('sp', lambda e, dst=dst, src=src: e.dma_start(out=dst.ap, in_=src), t_rc, True)
        S.dma('pool', lambda e: e.dma_start(out=bm.ap, in_=bm_d), t_rc, True)
        wr = A.alloc([128, 8, 1536], BF16)
        rot = [A.alloc([128, 256], F32) for _ in range(2)]
        rA = [A.alloc([128, 512], F32) for _ in range(2)]
        rB = [A.alloc([128, 512], F32)] * 2
        qk6 = [A.alloc([128, 4, 256], BF16) for _ in range(2)]
        v_bf = [A.alloc([128, 512], BF16) for _ in range(2)]
        g_s = [A.alloc([128, 512], F32) for _ in range(3)]
        T6 = [A.alloc([128, 6, 128], BF16) for _ in range(2)]
        scT = A.alloc([128, 128], BF16)
        o_bf = A.alloc([128, 512], BF16)
        st3 = A.alloc([128, 4], F32)
        S_f = [A.alloc([128, 2, 512], F32, ntt=2) for _ in range(2)]
        S_bf = A.alloc([128, 2, 512], BF16, ntt=2)
        qsm = [A.alloc([128, 2, 128], BF16)] * 2
        kdm = [A.alloc([128, 256], BF16)] * 2

        def v4(ap):
            return ap.rearrange("p (a h d) -> p a h d", a=2, h=2)

        def ret_load_w(h):
            segs = [(C_RQ + h * 256, 256, 0), (C_RK + h * 256, 256, 256), (C_RV + h * 512, 512, 512), (C_RG + h * 512, 512, 1024)]
            for (s0, w, d0) in segs:
                for k in range(8):
                    S.dma('pool', lambda e, s0=s0, w=w, d0=d0, k=k: e.dma_start(out=wr[:, k, d0:d0 + w], in_=w_in[k * 128:(k + 1) * 128, s0:s0 + w]), wr.t, True)

        def ret_front(h, j, pp, gi):
            r = ROWS[j]
            c0 = TOK0[j]
            var = 1 if j == 17 else 0
            kvar = 2 if j == 17 else (1 if j == 0 else 0)
            rA_, rB_, qk_, vb_, gs_, T6_, rot_ = rA[pp], rB[pp], qk6[pp], v_bf[pp], g_s[gi], T6[pp], rot[pp]
            S.dma('sp', lambda e: e.dma_start(out=rot_[:r, :], in_=rot_d[j, :r, :]), rot_.t, True)
            for (bank, cc0) in ((0, 0), (1, 512), (2, 1024)):
                for k in range(8):
                    S.op('pe', lambda e, bank=bank, cc0=cc0, k=k: e.matmul(pf(bank)[:r, :], xnT[:, k, c0:c0 + r], wr[:, k, cc0:cc0 + 512], start=(k == 0), stop=(k == 7)),
                         [xnT.tts[j], wr.t], [t_pb[bank]])
            S.op('act', lambda e: e.activation(out=vb_[:r, :], in_=pf(1)[:r, :], func=AF.Copy), [t_pb[1]], [vb_.t])
            S.op('act', lambda e: e.activation(out=gs_[:r, :], in_=pf(2)[:r, :], func=AF.Silu), [t_pb[2]], [gs_.t])
            S.op('dve', lambda e: e.tensor_tensor(out=rA_[:r, :].rearrange("p (b d) -> p b d", b=4), in0=pf(0)[:r, :].rearrange("p (b d) -> p b d", b=4),
                                                  in1=rot_[:r, 0:128].unsqueeze(1).to_broadcast([r, 4, 128]), op=ALU.mult), [t_pb[0], rot_.t], [rA_.t])
            S.op('dve', lambda e: e.tensor_tensor(out=rB_[:r, :].rearrange("p (b d) -> p b d", b=4), in0=pf(0)[:r, :].rearrange("p (b d) -> p b d", b=4),
                                                  in1=rot_[:r, 128:256].unsqueeze(1).to_broadcast([r, 4, 128]), op=ALU.mult), [t_pb[0], rot_.t], [rB_.t])
            S.op('pool', lambda e: e.tensor_tensor(out=v4(rA_[:r, :])[:, :, 0, :], in0=v4(rA_[:r, :])[:, :, 0, :], in1=v4(rB_[:r, :])[:, :, 1, :], op=ALU.subtract),
                 [rA_.t, rB_.t], [rA_.t])
            S.op('pool', lambda e: e.tensor_tensor(out=v4(rA_[:r, :])[:, :, 1, :], in0=v4(rB_[:r, :])[:, :, 0, :], in1=v4(rA_[:r, :])[:, :, 1, :], op=ALU.add),
                 [rA_.t, rB_.t], [rA_.t])
            S.op('act', lambda e: e.activation(out=qk_[:r, 0, :], in_=rA_[:r, 0:256], func=AF.Copy), [rA_.t], [qk_.t])
            S.op('act', lambda e: e.activation(out=qk_[:r, 1, :], in_=rA_[:r, 0:256], func=AF.Copy, scale=qdec[:r, var * 4 + h:var * 4 + h + 1]), [rA_.t, t_rc], [qk_.t])
            S.op('act', lambda e: e.activation(out=qk_[:r, 2, :], in_=rA_[:r, 256:512], func=AF.Copy, scale=1.0 / 16), [rA_.t], [qk_.t])
            S.op('dve', lambda e: e.tensor_scalar(out=qk_[:r, 3, :], in0=rA_[:r, 256:512], scalar1=kdec[:r, kvar * 4 + h:kvar * 4 + h + 1], scalar2=None, op0=ALU.mult),
                 [rA_.t, t_rc], [qk_.t])
            for s_ in range(3):
                for c in range(2):
                    S.op('pe', lambda e, s_=s_, c=c: e.transpose(pbf(6)[:, (2 * s_ + c) * 128:(2 * s_ + c) * 128 + r], qk_[:r, s_, c * 128:(c + 1) * 128], ident[:r, :r]),
                         [qk_.t, ident.t], [t_pb[6]])
            S.op('dve', lambda e: e.tensor_copy(out=T6_[:, :, :r], in_=pbf(6)[:, 0:768].rearrange("p (s q) -> p s q", s=6)[:, :, :r]), [t_pb[6]], [T6_.t])

        def ret_mid(h, j, pp, ob):
            r = ROWS[j]
            c0 = TOK0[j]
            var = 1 if j == 17 else 0
            Cj = 8 if j == 17 else r
            qk_, vb_, T6_ = qk6[pp], v_bf[pp], T6[pp]
            if j == 0:
                S.op('dve', lambda e: e.memset(S_f[0][:, :, :], 0.0), [], S_f[0].tts)
                S.op('dve', lambda e: e.memset(S_bf[:, :, :], 0.0), [], S_bf.tts)
            for c in range(2):
                S.op('pe', lambda e, c=c: e.matmul(pf(3)[:r, :r], T6_[:, 4 + c, :r], T6_[:, c, :r], start=(c == 0), stop=(c == 1)), [T6_.t], [t_pb[3]])
            S.op('dve', lambda e: e.tensor_tensor(out=scT[:r, :r], in0=pf(3)[:r, :r], in1=decT[:r, var * 4 + h, :r], op=ALU.mult), [t_pb[3], t_rc], [scT.t])
            if j < 17:
                S.op('pe', lambda e: e.matmul(pf(ob)[:r, :], scT[:r, :r], vb_[:r, :], start=True, stop=False), [scT.t, vb_.t], [t_pb[ob]])
                for c in range(2):
                    S.op('pe', lambda e, c=c: e.matmul(pf(ob)[:r, :], T6_[:, 2 + c, :r], S_bf[:, c, :], start=False, stop=(c == 1)), [T6_.t, S_bf.tts[c]], [t_pb[ob]])
                for c in range(2):
                    bk = 5
                    S.op('pe', lambda e, c=c, bk=bk: e.matmul(pf(bk)[:, :], qk_[:r, 3, c * 128:(c + 1) * 128], vb_[:r, :], start=True, stop=True), [qk_.t, vb_.t], [t_pb[bk]])
                    S.op('dve', lambda e, c=c, bk=bk: e.scalar_tensor_tensor(out=S_f[0][:, c, :], in0=S_f[0][:, c, :], scalar=GAM[h] ** Cj, in1=pf(bk)[:, :],
                                                                           op0=ALU.mult, op1=ALU.add), [S_f[0].tts[c], t_pb[bk]], [S_f[0].tts[c]])
                    S.op('act', lambda e, c=c: e.activation(out=S_bf[:, c, :], in_=S_f[0][:, c, :], func=AF.Copy), [S_f[0].tts[c]], [S_bf.tts[c]])
                if j == 16:
                    for c in range(2):
                        S.dma('sp', lambda e, c=c: e.dma_start(out=o_rp[h, c * 128:(c + 1) * 128, :], in_=S_f[0][:, c, :]), S_f[0].tts[c], False)
            else:
                S.op('pe', lambda e: e.matmul(pf(ob)[:, :], scT[:, :], vb_[:, :], start=True, stop=False), [scT.t, vb_.t], [t_pb[ob]])
                for b in range(16):
                    S.op('dve', lambda e, b=b: e.tensor_tensor(out=qsm[0][:, :, :], in0=T6_[:, 2:4, :], in1=bm[:, b:b + 1, :].to_broadcast([128, 2, 128]), op=ALU.mult),
                         [T6_.t, t_rc], [qsm[0].t])
                    S.op('dve', lambda e, b=b: e.tensor_scalar(out=kdm[0][:, :], in0=qk_[:, 3, :], scalar1=rm[:, b:b + 1], scalar2=None, op0=ALU.mult),
                         [qk_.t, t_rc], [kdm[0].t])
                    for c in range(2):
                        u = 2 * b + c
                        su = u % 4
                        Sb, tS = S_f[su // 2], S_f[su // 2].tts[su % 2]
                        hs = su % 2
                        bk = 5 if u % 2 == 0 else 3
                        S.dma('pool', lambda e, b=b, c=c, Sb=Sb, hs=hs: e.dma_start(out=Sb[:, hs, :], in_=state[b, h, c * 128:(c + 1) * 128, :]), tS, True)
                        S.op('act', lambda e, Sb=Sb, hs=hs, u=u: e.activation(out=S_bf[:, u % 2, :], in_=Sb[:, hs, :], func=AF.Copy), [tS], [S_bf.tts[u % 2]])
                        S.op('pe', lambda e, c=c, b=b, u=u: e.matmul(pf(ob)[:, :], qsm[0][:, c, :], S_bf[:, u % 2, :], start=False, stop=(b == 15 and c == 1)),
                             [qsm[0].t, S_bf.tts[u % 2]], [t_pb[ob]])
                        S.op('pe', lambda e, c=c, bk=bk: e.matmul(pf(bk)[:, :], kdm[0][:, c * 128:(c + 1) * 128], vb_[:, :], start=True, stop=True),
                             [kdm[0].t, vb_.t], [t_pb[bk]])
                        S.op('dve', lambda e, Sb=Sb, hs=hs, bk=bk: e.scalar_tensor_tensor(out=Sb[:, hs, :], in0=Sb[:, hs, :], scalar=GAM[h] ** 8, in1=pf(bk)[:, :],
                                                                                        op0=ALU.mult, op1=ALU.add), [tS, t_pb[bk]], [tS])
                        S.dma('sp', lambda e, b=b, c=c, Sb=Sb, hs=hs: e.dma_start(out=o_rs[b, h, c * 128:(c + 1) * 128, :], in_=Sb[:, hs, :]), tS, False)

        def ret_tail(h, j, gi, ob):
            r = ROWS[j]
            c0 = TOK0[j]
            gs_ = g_s[gi]
            S.op('act', lambda e: e.activation(out=o_bf[:r, :], in_=pf(ob)[:r, :], func=AF.Square, accum_out=st3[:r, 0:1]), [t_pb[ob]], [o_bf.t, st3.t])
            S.op('act', lambda e: e.activation(out=st3[:r, 0:1], in_=st3[:r, 0:1], func=AF.Sqrt, scale=1.0 / 512, bias=EPS), [st3.t], [st3.t])
            S.op('dve', lambda e: e.reciprocal(out=st3[:r, 0:1], in_=st3[:r, 0:1]), [st3.t], [st3.t])
            S.op('dve', lambda e: e.scalar_tensor_tensor(out=o_bf[:r, :], in0=pf(ob)[:r, :], scalar=st3[:r, 0:1], in1=gs_[:r, :], op0=ALU.mult, op1=ALU.mult),
                 [t_pb[ob], st3.t, gs_.t], [o_bf.t])
            for c in range(4):
                S.op('pe', lambda e, c=c: e.transpose(pbf(3)[:, 256 + c * 128:256 + c * 128 + r], o_bf[:r, c * 128:(c + 1) * 128], ident[:r, :r]), [o_bf.t, ident.t], [t_pb[3]])
            S.op('act', lambda e: e.activation(out=o_retT[:, 4 * h:4 * h + 4, c0:c0 + r], in_=pbf(3)[:, 256:768].rearrange("p (s q) -> p s q", s=4)[:, :, :r], func=AF.Copy),
                 [t_pb[3]], [o_retT.tts[j]])

        steps = [(h, j) for h in range(4) for j in range(NT)]
        N_ = len(steps)
        OB = (4, 7)
        ret_load_w(0)
        for i in range(N_ + 2):
            boundary = i < N_ and steps[i][1] == 0 and i > 0
            if boundary:
                ret_load_w(steps[i][0])
            if i < N_ and not boundary:
                ret_front(steps[i][0], steps[i][1], i % 2, i % 3)
            if 0 <= i - 1 < N_:
                ret_mid(steps[i - 1][0], steps[i - 1][1], (i - 1) % 2, OB[(i - 1) % 2])
            if 0 <= i - 2 < N_:
                ret_tail(steps[i - 2][0], steps[i - 2][1], (i - 2) % 3, OB[(i - 2) % 2])
            if boundary:
                ret_front(steps[i][0], steps[i][1], i % 2, i % 3)
        A.release(decT, qdec, kdec, bm, rm, wr, rot[0], rot[1], rA[0], rA[1], rB[0], qk6[0], qk6[1], v_bf[0], v_bf[1], g_s[0], g_s[1], g_s[2], T6[0], T6[1],
                  scT, o_bf, st3, S_f[0], S_f[1], S_bf, qsm[0], kdm[0])

        def tts_for(buf, t0, w):
            return [buf.tts[j] for j in range(NT) if TOK0[j] < t0 + w and TOK0[j] + ROWS[j] > t0]

        mT = A.alloc([128, 8, TTOT], BF16, ntt=NT)
        wrp_c = [A.alloc([128, 16, 128], BF16) for _ in range(2)]
        wdp_c = [A.alloc([128, 8, 128], BF16) for _ in range(2)]
        wg_c = [A.alloc([128, 8, 2, 128], BF16) for _ in range(2)]
        g1 = [A.alloc([128, 512], F32) for _ in range(2)]
        g2 = [A.alloc([128, 512], F32) for _ in range(2)]
        mchunks = [(t0, min(512, TTOT - t0)) for t0 in range(0, TTOT, 512)]
        nmc = 0
        ring = [A.alloc([128, 2, 128], F32) for _ in range(4)]
        nring = [0]

        def load_cast(dst, dst_tt, src):
            rb = ring[nring[0] % 4]
            nring[0] += 1
            S.dma('sp', lambda e: e.dma_start(out=rb[:, :, :], in_=src), rb.t, True)
            S.op('pool', lambda e: e.tensor_copy(out=dst, in_=rb[:, :, :]), [rb.t], [dst_tt])

        def merge_load(c):
            wb_ = c % 2
            cs = slice(c * 128, (c + 1) * 128)
            for k0 in range(0, 16, 2):
                load_cast(wrp_c[wb_][:, k0:k0 + 2, :], wrp_c[wb_].t, wrp_d[k0 * 128:(k0 + 2) * 128, cs].rearrange("(k p) c -> p k c", p=128))
            for k0 in range(0, 8, 2):
                load_cast(wdp_c[wb_][:, k0:k0 + 2, :], wdp_c[wb_].t, wdp_d[k0 * 128:(k0 + 2) * 128, cs].rearrange("(k p) c -> p k c", p=128))
                for gg in range(2):
                    gs = slice(C_GZ + gg * 1024 + c * 128, C_GZ + gg * 1024 + (c + 1) * 128)
                    load_cast(wg_c[wb_][:, k0:k0 + 2, gg, :], wg_c[wb_].t, w_in[k0 * 128:(k0 + 2) * 128, gs].rearrange("(k p) c -> p k c", p=128))

        merge_load(0)
        for c in range(8):
            wb_ = c % 2
            if c + 1 < 8:
                merge_load(c + 1)
            for (t0, w) in mchunks:
                bb = 4 * (nmc % 2)
                gb = nmc % 2
                nmc += 1
                for k in range(16):
                    S.op('pe', lambda e, k=k, t0=t0, w=w, bb=bb, wb_=wb_: e.matmul(pf(bb)[:, :w], wrp_c[wb_][:, k, :], o_retT[:, k, t0:t0 + w], start=(k == 0), stop=(k == 15)),
                         [wrp_c[wb_].t] + tts_for(o_retT, t0, w), [t_pb[bb]])
                for k in range(8):
                    S.op('pe', lambda e, k=k, t0=t0, w=w, bb=bb, wb_=wb_: e.matmul(pf(bb + 1)[:, :w], wdp_c[wb_][:, k, :], o_dsaT[:, k, t0:t0 + w], start=(k == 0), stop=(k == 7)),
                         [wdp_c[wb_].t] + tts_for(o_dsaT, t0, w), [t_pb[bb + 1]])
                for gg in range(2):
                    for k in range(8):
                        S.op('pe', lambda e, k=k, t0=t0, w=w, bb=bb, wb_=wb_, gg=gg: e.matmul(pf(bb + 2 + gg)[:, :w], wg_c[wb_][:, k, gg, :], xnT[:, k, t0:t0 + w],
                                                                                         start=(k == 0), stop=(k == 7)),
                             [wg_c[wb_].t] + tts_for(xnT, t0, w), [t_pb[bb + 2 + gg]])
                S.op('act', lambda e, w=w, bb=bb, gb=gb: e.activation(out=g1[gb][:, :w], in_=pf(bb + 2)[:, :w], func=AF.Sigmoid), [t_pb[bb + 2]], [g1[gb].t])
                S.op('act', lambda e, w=w, bb=bb, gb=gb: e.activation(out=g2[gb][:, :w], in_=pf(bb + 3)[:, :w], func=AF.Sigmoid), [t_pb[bb + 3]], [g2[gb].t])
                S.op('dve', lambda e, w=w, bb=bb, gb=gb: e.tensor_tensor(out=g1[gb][:, :w], in0=g1[gb][:, :w], in1=pf(bb)[:, :w], op=ALU.mult), [g1[gb].t, t_pb[bb]], [g1[gb].t])
                S.op('dve', lambda e, w=w, bb=bb, gb=gb: e.tensor_tensor(out=g2[gb][:, :w], in0=g2[gb][:, :w], in1=pf(bb + 1)[:, :w], op=ALU.mult), [g2[gb].t, t_pb[bb + 1]], [g2[gb].t])
                S.op('pool', lambda e, w=w, gb=gb, c=c, t0=t0: e.tensor_tensor(out=mT[:, c, t0:t0 + w], in0=g1[gb][:, :w], in1=g2[gb][:, :w], op=ALU.add),
                     [g1[gb].t, g2[gb].t], tts_for(mT, t0, w))
        if DEBUG:
            S.barrier()
            S.dma('sp', lambda e: e.dma_start(out=o_dbg, in_=mT.ap), mT.tts[0], False)
        A.release(xnT, o_dsaT, o_retT, wrp_c[0], wrp_c[1], wdp_c[0], wdp_c[1], wg_c[0], wg_c[1], g1[0], g1[1], g2[0], g2[1], ring[0], ring[1], ring[2], ring[3])

        wo = A.alloc([128, 8, D], BF16)
        for k in range(8):
            S.dma('pool', lambda e, k=k: e.dma_start(out=wo[:, k, :], in_=wo_d[k * 128:(k + 1) * 128, :]), wo.t, True)
        nfw = A.alloc([128, 8], F32)
        S.dma('sp', lambda e: e.dma_start(out=nfw[:, :], in_=nfw_d), nfw.t, True)
        yacc = A.alloc([128, NT, D], F32, ntt=NT)
        h2nT = A.alloc([128, 8, TTOT], BF16, ntt=NT)
        xt2 = [A.alloc([128, D], F32) for _ in range(2)]
        xsb2 = [A.alloc([128, D], BF16) for _ in range(2)]
        junk4 = A.alloc([128, D], F32)
        st4 = A.alloc([128, 2 * NT], F32, ntt=NT)
        for j in range(1, NT):
            c0 = TOK0[j]
            bi = j % 2
            src = xs[:, :] if j == 17 else xp[128 * (j - 1):128 * j, :]
            S.dma('sp', lambda e, bi=bi, src=src: e.dma_start(out=xt2[bi][:, :], in_=src), xt2[bi].t, True)
            for half in range(2):
                bank = (2 * j + half) % 4
                for c in range(8):
                    S.op('pe', lambda e, c=c, c0=c0, half=half, bank=bank: e.matmul(pf(bank)[:, :], mT[:, c, c0:c0 + 128], wo[:, c, half * 512:(half + 1) * 512],
                                                                                 start=(c == 0), stop=(c == 7)), [mT.tts[j], wo.t], [t_pb[bank]])
                S.op('dve', lambda e, j=j, half=half, bank=bank, bi=bi: e.tensor_tensor(out=yacc[:, j, half * 512:(half + 1) * 512], in0=pf(bank)[:, :],
                                                                                     in1=xt2[bi][:, half * 512:(half + 1) * 512], op=ALU.add),
                     [t_pb[bank], xt2[bi].t], [yacc.tts[j]])
            ss = st4[:, 2 * j:2 * j + 1]
            rs = st4[:, 2 * j + 1:2 * j + 2]
            tst = st4.tts[j]
            S.op('act', lambda e, j=j, ss=ss: e.activation(out=junk4[:, :], in_=yacc[:, j, :], func=AF.Square, accum_out=ss), [yacc.tts[j]], [junk4.t, tst])
            S.op('act', lambda e, ss=ss, rs=rs: e.activation(out=rs, in_=ss, func=AF.Sqrt, scale=1.0 / D, bias=EPS), [tst], [tst])
            S.op('dve', lambda e, rs=rs: e.reciprocal(out=rs, in_=rs), [tst], [tst])
            S.op('dve', lambda e, bi=bi, j=j, rs=rs: e.tensor_scalar(out=xsb2[bi][:, :], in0=yacc[:, j, :], scalar1=rs, scalar2=None, op0=ALU.mult),
                 [yacc.tts[j], tst], [xsb2[bi].t])
            pb = 6 + bi
            for k in range(8):
                S.op('pe', lambda e, bi=bi, k=k, pb=pb: e.transpose(pbf(pb)[:, k * 128:(k + 1) * 128], xsb2[bi][:, k * 128:(k + 1) * 128], ident[:, :]),
                     [xsb2[bi].t, ident.t], [t_pb[pb]])
            for k in range(8):
                if k % 2 == 0:
                    S.op('act', lambda e, pb=pb, k=k, c0=c0: e.activation(out=h2nT[:, k, c0:c0 + 128], in_=pbf(pb)[:, k * 128:(k + 1) * 128], func=AF.Copy, scale=nfw[:, k:k + 1]),
                         [t_pb[pb], nfw.t], [h2nT.tts[j]])
                else:
                    S.op('dve', lambda e, pb=pb, k=k, c0=c0: e.tensor_scalar(out=h2nT[:, k, c0:c0 + 128], in0=pbf(pb)[:, k * 128:(k + 1) * 128], scalar1=nfw[:, k:k + 1],
                                                                           scalar2=None, op0=ALU.mult), [t_pb[pb], nfw.t], [h2nT.tts[j]])
        A.release(mT, wo, nfw, xt2[0], xt2[1], xsb2[0], xsb2[1], junk4, st4)

        groups = [(0, 4), (4, 4), (8, 4), (12, 4), (16, 4), (20, 2)]
        wa = [A.alloc([128, 8, 512], BF16) for _ in range(2)]
        wb2 = [A.alloc([128, 8, 512], BF16) for _ in range(2)]
        wo2 = [A.alloc([128, 4, D], BF16) for _ in range(2)]
        uT = [A.alloc([128, 4, 512], BF16) for _ in range(2)]
        sa = [A.alloc([128, 512], F32) for _ in range(2)]
        tchunks = [(1, 4), (5, 4), (9, 4), (13, 4), (17, 1)]
        nu = 0
        nsa = 0
        for gi, (f0c, nf) in enumerate(groups):
            wb_ = gi % 2
            f0 = f0c * 128
            for k in range(8):
                S.dma('pool', lambda e, k=k, f0=f0, nf=nf, wb_=wb_: e.dma_start(out=wa[wb_][:, k, 0:nf * 128], in_=wfi_d[k * 128:(k + 1) * 128, f0:f0 + nf * 128]), wa[wb_].t, True)
                S.dma('pool', lambda e, k=k, f0=f0, nf=nf, wb_=wb_: e.dma_start(out=wb2[wb_][:, k, 0:nf * 128], in_=wfi_d[k * 128:(k + 1) * 128, DFF + f0:DFF + f0 + nf * 128]),
                      wb2[wb_].t, True)
            for fi in range(nf):
                S.dma('pool', lambda e, fi=fi, f0=f0, wb_=wb_: e.dma_start(out=wo2[wb_][:, fi, :], in_=wfo_d[f0 + fi * 128:f0 + (fi + 1) * 128, :]), wo2[wb_].t, True)
            for (j0, ntl) in tchunks:
                t0 = TOK0[j0]
                w = ntl * 128
                ub = uT[nu % 2]
                nu += 1
                rtt = [h2nT.tts[j] for j in range(j0, j0 + ntl)]
                for fi in range(nf):
                    sb_ = sa[nsa % 2]
                    ba = 2 * (nsa % 2)
                    nsa += 1
                    for k in range(8):
                        S.op('pe', lambda e, k=k, fi=fi, t0=t0, w=w, ba=ba, wb_=wb_: e.matmul(pf(ba)[:, :w], wa[wb_][:, k, fi * 128:(fi + 1) * 128], h2nT[:, k, t0:t0 + w],
                                                                                         start=(k == 0), stop=(k == 7)), [wa[wb_].t] + rtt, [t_pb[ba]])
                    for k in range(8):
                        S.op('pe', lambda e, k=k, fi=fi, t0=t0, w=w, ba=ba, wb_=wb_: e.matmul(pf(ba + 1)[:, :w], wb2[wb_][:, k, fi * 128:(fi + 1) * 128], h2nT[:, k, t0:t0 + w],
                                                                                         start=(k == 0), stop=(k == 7)), [wb2[wb_].t] + rtt, [t_pb[ba + 1]])
                    S.op('act', lambda e, w=w, ba=ba, sb_=sb_: e.activation(out=sb_[:, :w], in_=pf(ba)[:, :w], func=AF.Silu), [t_pb[ba]], [sb_.t])
                    S.op('dve', lambda e, w=w, ba=ba, sb_=sb_, ub=ub, fi=fi: e.tensor_tensor(out=ub[:, fi, :w], in0=sb_[:, :w], in1=pf(ba + 1)[:, :w], op=ALU.mult),
                         [sb_.t, t_pb[ba + 1]], [ub.t])
                for jj in range(ntl):
                    j = j0 + jj
                    for half in range(2):
                        bank = 4 + (2 * j + half) % 4
                        for fi in range(nf):
                            S.op('pe', lambda e, fi=fi, jj=jj, half=half, bank=bank, ub=ub, wb_=wb_, nf=nf: e.matmul(pf(bank)[:, :], ub[:, fi, jj * 128:(jj + 1) * 128],
                                                                                                           wo2[wb_][:, fi, half * 512:(half + 1) * 512],
                                                                                                           start=(fi == 0), stop=(fi == nf - 1)),
                                 [ub.t, wo2[wb_].t], [t_pb[bank]])
                        S.op('dve', lambda e, j=j, half=half, bank=bank: e.tensor_tensor(out=yacc[:, j, half * 512:(half + 1) * 512], in0=pf(bank)[:, :],
                                                                                      in1=yacc[:, j, half * 512:(half + 1) * 512], op=ALU.add),
                             [t_pb[bank], yacc.tts[j]], [yacc.tts[j]])
                    if gi == len(groups) - 1:
                        dst = o_ys[:, :] if j == 17 else o_yp[128 * (j - 1):128 * j, :]
                        S.dma('sp', lambda e, j=j, dst=dst: e.dma_start(out=dst, in_=yacc[:, j, :]), yacc.tts[j], False)

        S.emit()
    return nc


_NC_CACHE = {}


def _consts():
    c = {}
    c["ident"] = np.eye(128, dtype=np.float32)
    half = 128
    inv = np.power(np.float32(10000.0), -np.arange(half, dtype=np.float32) / np.float32(half)).astype(np.float32)
    rot = np.zeros((NT, 128, 256), np.float32)
    for j in range(NT):
        if j == 0:
            pos = np.arange(16)
        elif j == 17:
            pos = 2048 + (np.arange(128) % 8)
        else:
            pos = 16 + 128 * (j - 1) + np.arange(128)
        ang = pos.astype(np.float32)[:, None] * inv[None, :]
        rot[j, :len(pos), 0:128] = np.cos(ang)
        rot[j, :len(pos), 128:256] = np.sin(ang)
    c["rot"] = rot
    lg = np.log1p(-np.exp2(-5.0 - np.arange(4, dtype=np.float64)))
    i = np.arange(128)
    decT = np.zeros((128, 8, 128), np.float32)
    for h in range(4):
        diff = i[None, :] - i[:, None]
        decT[:, h, :] = np.where(diff >= 0, np.exp(lg[h] * np.maximum(diff, 0)), 0.0)
        same = (i[None, :] // 8) == (i[:, None] // 8)
        d8 = (i[None, :] % 8) - (i[:, None] % 8)
        decT[:, 4 + h, :] = np.where(same & (d8 >= 0), np.exp(lg[h] * np.maximum(d8, 0)), 0.0)
    c["decT"] = decT
    qdec = np.zeros((128, 8), np.float32)
    kdec = np.zeros((128, 12), np.float32)
    for h in range(4):
        qdec[:, h] = np.exp(lg[h] * (i + 1.0))
        qdec[:, 4 + h] = np.exp(lg[h] * ((i % 8) + 1.0))
        kdec[:, h] = np.exp(lg[h] * (127.0 - i)) / 16.0
        kdec[:16, 4 + h] = np.exp(lg[h] * (15.0 - i[:16])) / 16.0
        kdec[:, 8 + h] = np.exp(lg[h] * (7.0 - (i % 8))) / 16.0
    c["qdec"] = qdec
    c["kdec"] = kdec
    bm = np.zeros((128, 16, 128), np.float32)
    rm = np.zeros((128, 16), np.float32)
    for b in range(16):
        bm[:, b, 8 * b:8 * b + 8] = 1.0
        rm[8 * b:8 * b + 8, b] = 1.0
    c["bm"] = bm
    c["rm"] = rm
    c["cbias"] = np.where(i[None, :] <= i[:, None], 0.0, NEG).astype(np.float32)
    c["pow2"] = np.broadcast_to((0.5 ** (np.arange(NBIS + 1) + 1.0))[None, :], (128, NBIS + 1)).astype(np.float32).copy()
    bm2 = np.zeros((128, 8, 128), np.float32)
    for p in range(8):
        bm2[0:64, p, 8 * p:8 * p + 8] = 1.0
        bm2[64:128, p, 8 * (p + 8):8 * (p + 8) + 8] = 1.0
    c["bm2"] = bm2
    c["cbs"] = np.where(np.arange(8)[None, :] <= (i % 8)[:, None], 0.0, NEG).astype(np.float32)
    c["pidx"] = (np.arange(128) % 32).astype(np.float32)[:, None].copy()
    c["thrcap"] = np.where(i <= 111, -1.0e29, 1.0e30).astype(np.float32)[:, None].copy()
    return c


def kernel(x_prompt, x_sample, cache_k, cache_v, cache_kidx, state_ret, page_table,
           meta_tokens, norm_mix_w, w_in, w_ret_proj, dsa_q_norm_w, dsa_k_norm_w,
           idx_k_norm_w, idx_k_norm_b, w_dsa_proj, w_out, norm_ffn_w, w_ffn_in, w_ffn_out):
    f = lambda a: np.ascontiguousarray(np.asarray(a))
    x_prompt, x_sample, state_ret = f(x_prompt), f(x_sample), f(state_ret)
    ck = f(cache_k)[0].reshape(NPOOL * 32, 4 * 256)
    cv = f(cache_v)[0].reshape(NPOOL * 32, 4 * 256)
    cki = f(cache_kidx)[0].reshape(NPOOL * 32, 4 * 64)
    ptab = f(page_table).astype(np.int32)
    if 'nc' not in _NC_CACHE:
        _NC_CACHE['nc'] = build_program()
    nc = _NC_CACHE['nc']
    ncore = 8
    common = {
        "meta": f(meta_tokens),
        "w_in": f(np.asarray(w_in)[0]),
        "nmw": f(np.asarray(norm_mix_w)[0].reshape(8, 128).T),
        "wq_bc": f(np.broadcast_to(np.asarray(dsa_q_norm_w)[0][None, :], (128, 128))),
        "wk_bc": f(np.broadcast_to(np.asarray(dsa_k_norm_w)[0][None, :], (128, 128))),
        "ikw_bc": f(np.broadcast_to(np.asarray(idx_k_norm_w)[0][None, :], (128, 64))),
        "ikb_bc": f(np.broadcast_to(np.asarray(idx_k_norm_b)[0][None, :], (128, 64))),
        "w_ret_proj": f(np.asarray(w_ret_proj)[0]),
        "w_dsa_proj": f(np.asarray(w_dsa_proj)[0]),
        "w_out": f(np.asarray(w_out)[0]),
        "w_ffn_in": f(np.asarray(w_ffn_in)[0]),
        "w_ffn_out": f(np.asarray(w_ffn_out)[0]),
        "nfw": f(np.asarray(norm_ffn_w)[0].reshape(8, 128).T),
    }
    common.update(_consts())
    in_maps = []
    for c in range(ncore):
        m = dict(common)
        m["xp"] = x_prompt[c]
        m["xs"] = f(x_sample[16 * c:16 * c + 16].reshape(128, D))
        m["state"] = state_ret[0, 16 * c:16 * c + 16]
        ptc = ptab[16 * c:16 * c + 16]
        m["pt2"] = f(np.repeat(ptc.reshape(16, 4, 4).transpose(2, 0, 1).reshape(4, 64), 32, axis=0))
        m["ck"], m["cv"], m["cki"] = ck, cv, cki
        in_maps.append(m)
    res = run_bass_kernel_spmd(nc, in_maps, core_ids=list(range(ncore)))
    R = res.results
    if DEBUG:
        _NC_CACHE['dbg'] = R[0]["o_dbg"]
    y_prompt = np.stack([R[c]["o_yp"] for c in range(ncore)]).reshape(8, 2048, D)
    y_sample = np.concatenate([R[c]["o_ys"].reshape(16, 8, D) for c in range(ncore)], 0)
    k_prompt = np.stack([R[c]["o_kp"].reshape(2064, 2, 128) for c in range(ncore)])[None]
    v_prompt = np.stack([R[c]["o_vp"].reshape(2064, 2, 128) for c in range(ncore)])[None]
    kidx_prompt = np.stack([R[c]["o_kip"] for c in range(ncore)])[None]
    ret_prompt = np.stack([R[c]["o_rp"] for c in range(ncore)])[None]
    k_sample = np.concatenate([R[c]["o_ks"].reshape(16, 8, 2, 128) for c in range(ncore)], 0)[None]
    v_sample = np.concatenate([R[c]["o_vs"].reshape(16, 8, 2, 128) for c in range(ncore)], 0)[None]
    kidx_sample = np.concatenate([R[c]["o_kis"].reshape(16, 8, 64) for c in range(ncore)], 0)[None]
    ret_sample = np.concatenate([R[c]["o_rs"] for c in range(ncore)], 0)[None]
    outs = (y_prompt, y_sample, k_prompt, v_prompt, kidx_prompt, ret_prompt, k_sample, v_sample, kidx_sample, ret_sample)
    return tuple(np.ascontiguousarray(o, dtype=np.float32) for o in outs)
```

```python
import contextlib
import numpy as np
import concourse.bass as bass
import concourse.mybir as mybir
from concourse.bass_utils import run_bass_kernel_spmd

F32 = mybir.dt.float32
BF16 = mybir.dt.bfloat16
I32 = mybir.dt.int32
AF = mybir.ActivationFunctionType
ALU = mybir.AluOpType
AX = mybir.AxisListType

DEBUG = False

D = 1024
NT = 18
ROWS = [16] + [128] * 17
TOK0 = [0] + [16 + 128 * i for i in range(17)]
TTOT = 2192
EPS = 1e-6
IN_COLS = 10312
C_RQ, C_RK, C_RV, C_RG = 0, 1024, 2048, 4096
C_DSA = 6144
N_DSA = 2120
C_GZ = 8264
DFF = 2816
IDX_SCALE = (8 ** -0.5) * (64 ** -0.5)
NBIS = 16
NEG = -1.0e30
GAM = [float(np.exp(np.log1p(-np.exp2(-5.0 - h)))) for h in range(4)]
ARENA_WORDS = 53200
NPOOL = 2560


class TT:
    def __init__(self, init=None):
        self.w = None
        self.r = list(init) if init else []
        self.dsem = None
        self.dcnt = 0


class Sched:
    ENG = ('pe', 'act', 'dve', 'pool', 'sp')

    def __init__(self, nc):
        self.nc = nc
        self.ops = {e: [] for e in self.ENG}
        self.cnt = {e: 0 for e in self.ENG}
        self.seen = {e: {} for e in self.ENG}
        self.dsems = []
        self.final = {}

    def _deps(self, eng, reads, writes):
        deps = {}

        def add(ev):
            if ev is None:
                return
            k, v = ev
            if k == eng and eng == 'pe':
                return
            if self.seen[eng].get(k, 0) >= v:
                return
            if deps.get(k, 0) < v:
                deps[k] = v
        for t in reads:
            add(t.w)
        for t in writes:
            add(t.w)
            for ev in t.r:
                add(ev)
        for k, v in deps.items():
            self.seen[eng][k] = v
        return list(deps.items())

    @staticmethod
    def _compact(evs):
        d = {}
        for k, v in evs:
            if d.get(k, 0) < v:
                d[k] = v
        return list(d.items())

    def op(self, eng, fn, reads=(), writes=()):
        waits = self._deps(eng, reads, writes)
        self.cnt[eng] += 1
        ev = (eng, self.cnt[eng])
        for t in reads:
            t.r.append(ev)
            if len(t.r) > 48:
                t.r = self._compact(t.r)
        for t in writes:
            t.w = ev
            t.r = []
        self.ops[eng].append((waits, fn, (eng, 1)))

    def dma(self, q, fn, tile, load, extra_reads=()):
        if load:
            waits = self._deps(q, extra_reads, (tile,))
        else:
            waits = self._deps(q, (tile,) + tuple(extra_reads), ())
        if tile.dsem is None:
            tile.dsem = 'd%d' % len(self.dsems)
            self.dsems.append(tile)
        tile.dcnt += 16
        ev = (tile.dsem, tile.dcnt)
        if load:
            tile.w = ev
            tile.r = []
            for t in extra_reads:
                t.r.append(ev)
        else:
            tile.r.append(ev)
            if len(tile.r) > 48:
                tile.r = self._compact(tile.r)
            self.final[tile.dsem] = tile.dcnt
        self.ops[q].append((waits, fn, (tile.dsem, 16)))

    def barrier(self):
        for e in self.ENG:
            waits = []
            for o in self.ENG:
                if o == e:
                    continue
                v = self.cnt[o]
                if v > self.seen[e].get(o, 0):
                    self.seen[e][o] = v
                    waits.append((o, v))
            for t in self.dsems:
                if t.dcnt > self.seen[e].get(t.dsem, 0):
                    self.seen[e][t.dsem] = t.dcnt
                    waits.append((t.dsem, t.dcnt))
            self.ops[e].append((waits, None, None))

    def emit(self):
        nc = self.nc
        with contextlib.ExitStack() as st:
            sems = {}
            for e in self.ENG:
                sems[e] = st.enter_context(nc.semaphore('s_' + e))
            for t in self.dsems:
                sems[t.dsem] = st.enter_context(nc.semaphore('s_' + t.dsem))
            block = st.enter_context(nc.Block())

            def run(engname, eng):
                for waits, fn, inc in self.ops[engname]:
                    for k, v in waits:
                        eng.wait_ge(sems[k], v)
                    if fn is None:
                        continue
                    ins = fn(eng)
                    ins.then_inc(sems[inc[0]], inc[1])
                if engname == 'sp':
                    for k, v in self.final.items():
                        eng.wait_ge(sems[k], v)

            @block.tensor
            def _(e):
                run('pe', e)

            @block.scalar
            def _(e):
                run('act', e)

            @block.vector
            def _(e):
                run('dve', e)

            @block.gpsimd
            def _(e):
                run('pool', e)

            @block.sync
            def _(e):
                run('sp', e)


class Buf:
    def __init__(self, ap, off, words, tts):
        self.ap = ap
        self.off = off
        self.words = words
        self.tts = tts
        self.t = tts[0]

    def __getitem__(self, key):
        return self.ap[key]


class Arena:
    def __init__(self, base, nwords):
        self.base = base
        self.free = [(0, nwords)]
        self.dead = {}

    def alloc(self, shape, dt, ntt=1):
        n = 1
        for s in shape[1:]:
            n *= s
        esz = 2 if dt == BF16 else 4
        words = (n * esz + 3) // 4
        words = (words + 15) // 16 * 16
        for i, (o, w) in enumerate(self.free):
            if w >= words:
                off = o
                if w == words:
                    self.free.pop(i)
                else:
                    self.free[i] = (o + words, w - words)
                break
        else:
            raise RuntimeError("arena out of SBUF: need %d words, free=%s" % (words, self.free))
        v = self.base[:, off:off + words]
        if dt != F32:
            v = v.bitcast(dt)
        v = v[:, 0:n]
        nd = len(shape) - 1
        if nd == 2:
            v = v.rearrange("p (a b) -> p a b", a=shape[1])
        elif nd == 3:
            v = v.rearrange("p (a b c) -> p a b c", a=shape[1], b=shape[2])
        v = v[:shape[0]]
        init = list(self.dead.items())
        return Buf(v, off, words, [TT(init) for _ in range(ntt)])

    def release(self, *bufs):
        for b in bufs:
            for t in b.tts:
                evs = list(t.r)
                if t.w is not None:
                    evs.append(t.w)
                for k, v in evs:
                    if self.dead.get(k, 0) < v:
                        self.dead[k] = v
            self.free.append((b.off, b.words))
        self.free.sort()
        merged = []
        for o, w in self.free:
            if merged and merged[-1][0] + merged[-1][1] == o:
                merged[-1] = (merged[-1][0], merged[-1][1] + w)
            else:
                merged.append((o, w))
        self.free = merged


def build_program():
    nc = bass.Bass("TRN2", target_bir_lowering=False)

    def din(name, shape, dt=F32):
        return nc.dram_tensor(name, list(shape), dt, kind="ExternalInput").ap()

    def dout(name, shape, dt=F32):
        return nc.dram_tensor(name, list(shape), dt, kind="ExternalOutput").ap()

    xp = din("xp", [2048, D])
    xs = din("xs", [128, D])
    meta = din("meta", [16, D])
    w_in = din("w_in", [D, IN_COLS])
    state = din("state", [16, 4, 256, 512])
    nmw = din("nmw", [128, 8])
    wq_bc = din("wq_bc", [128, 128])
    wk_bc = din("wk_bc", [128, 128])
    ikw_bc = din("ikw_bc", [128, 64])
    ikb_bc = din("ikb_bc", [128, 64])
    ident_d = din("ident", [128, 128])
    rot_d = din("rot", [NT, 128, 256])
    decT_d = din("decT", [128, 8, 128])
    qdec_d = din("qdec", [128, 8])
    kdec_d = din("kdec", [128, 12])
    bm_d = din("bm", [128, 16, 128])
    rm_d = din("rm", [128, 16])
    cbias_d = din("cbias", [128, 128])
    pow2_d = din("pow2", [128, NBIS + 1])
    thrcap_d = din("thrcap", [128, 1])
    pidx_d = din("pidx", [128, 1])
    bm2_d = din("bm2", [128, 8, 128])
    cbs_d = din("cbs", [128, 8])
    wrp_d = din("w_ret_proj", [2048, D])
    wdp_d = din("w_dsa_proj", [D, D])
    wo_d = din("w_out", [D, D])
    wfi_d = din("w_ffn_in", [D, 2 * DFF])
    wfo_d = din("w_ffn_out", [DFF, D])
    nfw_d = din("nfw", [128, 8])
    pt_d = din("pt2", [128, 64], I32)
    ck_d = din("ck", [NPOOL * 32, 4 * 256])
    cv_d = din("cv", [NPOOL * 32, 4 * 256])
    cki_d = din("cki", [NPOOL * 32, 4 * 64])

    o_yp = dout("o_yp", [2048, D])
    o_ys = dout("o_ys", [128, D])
    o_kp = dout("o_kp", [2064, 256])
    o_vp = dout("o_vp", [2064, 256])
    o_kip = dout("o_kip", [2064, 64])
    o_rp = dout("o_rp", [4, 256, 512])
    o_ks = dout("o_ks", [128, 256])
    o_vs = dout("o_vs", [128, 256])
    o_kis = dout("o_kis", [128, 64])
    o_rs = dout("o_rs", [16, 4, 256, 512])
    if DEBUG:
        o_dbg = dout("o_dbg", [128, 8, TTOT], BF16)

    S = Sched(nc)

    def pos0(j):
        return 0 if j == 0 else 16 + 128 * (j - 1)

    with contextlib.ExitStack() as top:
        arena_t = top.enter_context(nc.sbuf_tensor("arena", [128, ARENA_WORDS], F32))
        A = Arena(arena_t, ARENA_WORDS)
        pbank = [top.enter_context(nc.psum_tensor("pb%d" % i, [128, 512], F32)) for i in range(8)]
        t_pb = [TT() for _ in range(8)]

        def pf(i):
            return pbank[i]

        def pbf(i):
            return pbank[i][:, :].bitcast(BF16)

        ident = A.alloc([128, 128], BF16)
        nmw_sb = A.alloc([128, 8], F32)
        wq_sb = A.alloc([128, 128], F32)
        wk_sb = A.alloc([128, 128], F32)
        ikw_sb = A.alloc([128, 64], F32)
        ikb_sb = A.alloc([128, 64], F32)
        t_const = TT()
        S.dma('pool', lambda e: e.dma_start(out=ident[:, :], in_=ident_d[:, :]), ident.t, True)
        for dst, src in ((nmw_sb, nmw), (wq_sb, wq_bc), (wk_sb, wk_bc), (ikw_sb, ikw_bc), (ikb_sb, ikb_bc)):
            S.dma('sp', lambda e, dst=dst, src=src: e.dma_start(out=dst[:, :], in_=src[:, :]), t_const, True)

        xnT = A.alloc([128, 8, TTOT], BF16, ntt=NT)

        xt = [A.alloc([128, D], F32) for _ in range(2)]
        xsb = [A.alloc([128, D], BF16) for _ in range(2)]
        junk = A.alloc([128, D], F32)
        st1 = A.alloc([128, 2 * NT], F32, ntt=NT)
        for j in range(NT):
            r = ROWS[j]
            c0 = TOK0[j]
            bi = j % 2
            if j == 0:
                src = meta[:, :]
            elif j == 17:
                src = xs[:, :]
            else:
                src = xp[128 * (j - 1):128 * j, :]
            S.dma('sp', lambda e, bi=bi, r=r, src=src: e.dma_start(out=xt[bi][:r, :], in_=src), xt[bi].t, True)
            ss = st1[:r, 2 * j:2 * j + 1]
            rs = st1[:r, 2 * j + 1:2 * j + 2]
            tst = st1.tts[j]
            S.op('act', lambda e, bi=bi, r=r, ss=ss: e.activation(out=junk[:r, :], in_=xt[bi][:r, :], func=AF.Square, accum_out=ss),
                 [xt[bi].t], [junk.t, tst])
            S.op('act', lambda e, ss=ss, rs=rs: e.activation(out=rs, in_=ss, func=AF.Sqrt, scale=1.0 / D, bias=EPS), [tst], [tst])
            S.op('dve', lambda e, rs=rs: e.reciprocal(out=rs, in_=rs), [tst], [tst])
            S.op('dve', lambda e, bi=bi, r=r, rs=rs: e.tensor_scalar(out=xsb[bi][:r, :], in0=xt[bi][:r, :], scalar1=rs, scalar2=None, op0=ALU.mult),
                 [xt[bi].t, tst], [xsb[bi].t])
            pb = 6 + bi
            for k in range(8):
                S.op('pe', lambda e, bi=bi, r=r, k=k, pb=pb: e.transpose(pbf(pb)[:, k * 128:k * 128 + r], xsb[bi][:r, k * 128:(k + 1) * 128], ident[:r, :r]),
                     [xsb[bi].t, ident.t], [t_pb[pb]])
            for k in range(8):
                if k % 2 == 0:
                    S.op('act', lambda e, pb=pb, r=r, k=k, c0=c0: e.activation(out=xnT[:, k, c0:c0 + r], in_=pbf(pb)[:, k * 128:k * 128 + r],
                                                                             func=AF.Copy, scale=nmw_sb[:, k:k + 1]),
                         [t_pb[pb], t_const], [xnT.tts[j]])
                else:
                    S.op('dve', lambda e, pb=pb, r=r, k=k, c0=c0: e.tensor_scalar(out=xnT[:, k, c0:c0 + r], in0=pbf(pb)[:, k * 128:k * 128 + r],
                                                                                scalar1=nmw_sb[:, k:k + 1], scalar2=None, op0=ALU.mult),
                         [t_pb[pb], t_const], [xnT.tts[j]])
        A.release(xt[0], xt[1], xsb[0], xsb[1], junk, st1)

        o_dsaT = A.alloc([128, 8, TTOT], BF16, ntt=NT)
        wd = A.alloc([128, 8, N_DSA], BF16)
        for k in range(8):
            S.dma('pool', lambda e, k=k: e.dma_start(out=wd[:, k, :], in_=w_in[k * 128:(k + 1) * 128, C_DSA:C_DSA + N_DSA]), wd.t, True)
        st2 = A.alloc([128, 16], F32)
        junk2 = A.alloc([128, 128], F32)
        q_bf = A.alloc([128, 1024], BF16)
        k_f = A.alloc([128, 256], F32)
        k_bf = A.alloc([128, 256], BF16)
        v_f = A.alloc([128, 256], F32)
        iq_bf = A.alloc([128, 512], BF16)
        ki_f = A.alloc([128, 64], F32)
        ki_t = A.alloc([128, 64], F32)
        ik2_bf = A.alloc([128, 128], BF16)
        widx = A.alloc([128, NT, 8], F32, ntt=NT)
        V_bf = A.alloc([128, NT, 256], BF16, ntt=NT)
        KT = A.alloc([128, 2, 2064], BF16)
        ikT2 = A.alloc([128, 2064], BF16)
        QT = A.alloc([128, 8, 128], BF16)
        iqT = A.alloc([128, 4, 128], BF16)
        acc = A.alloc([128, 2064], F32)
        mask_bf = A.alloc([128, 2064], BF16)
        junk_bf = A.alloc([128, 2064], BF16)
        maskT = A.alloc([128, 17, 128], BF16)
        rl = [A.alloc([128, 2, 512], F32) for _ in range(2)]
        PT = [A.alloc([128, 4, 128], BF16) for _ in range(4)]
        rec = A.alloc([128, 512], F32)
        bs = A.alloc([128, 10 + NBIS], F32)
        tmpd = A.alloc([128, 128], F32)
        ones_bf = A.alloc([128, 128], BF16)
        cbias = A.alloc([128, 128], F32)
        pow2 = A.alloc([128, NBIS + 1], F32)
        thrcap = A.alloc([128, 1], F32)
        t_c2 = TT(list(A.dead.items()))
        for dst, src in ((cbias, cbias_d), (pow2, pow2_d), (thrcap, thrcap_d)):
            S.dma('sp', lambda e, dst=dst, src=src: e.dma_start(out=dst.ap, in_=src), t_c2, True)
        S.op('pool', lambda e: e.memset(ones_bf[:, :], 1.0), [], [ones_bf.t])
        att_cnt = [0, 0]
        QT_s = A.alloc([128, 8, 128], BF16)
        iqT8 = A.alloc([128, 8, 128], BF16)
        iq_dup = A.alloc([128, 8, 128], BF16)
        KT_s = A.alloc([128, 2, 128], BF16)
        ikT_s = A.alloc([128, 128], BF16)

        def idx_scores_prompt(r, L, iqT_ap, t_iqT, w_ap, t_w):
            for cc0 in range(0, L, 512):
                w = min(512, L - cc0)
                for hp in range(4):
                    rb = rl[att_cnt[0] % 2]
                    bo = 2 * (att_cnt[0] % 2)
                    att_cnt[0] += 1
                    for e_ in range(2):
                        S.op('pe', lambda e, e_=e_, hp=hp, w=w, cc0=cc0, bo=bo: e.matmul(pf(bo + e_)[:r, :w], iqT_ap[e_ * 64:(e_ + 1) * 64, hp, :], ikT2[e_ * 64:(e_ + 1) * 64, cc0:cc0 + w],
                                                                                        start=True, stop=True), [t_iqT, ikT2.t], [t_pb[bo + e_]])
                        S.op('act', lambda e, e_=e_, w=w, rb=rb, bo=bo: e.activation(out=rb[:r, e_, :w], in_=pf(bo + e_)[:r, :w], func=AF.Relu), [t_pb[bo + e_]], [rb.t])
                    for e_ in range(2):
                        h = 2 * hp + e_
                        if h == 0:
                            S.op('dve', lambda e, w=w, cc0=cc0, rb=rb: e.tensor_scalar(out=acc[:r, cc0:cc0 + w], in0=rb[:r, 0, :w], scalar1=w_ap(0), scalar2=None, op0=ALU.mult),
                                 [rb.t, t_w], [acc.t])
                        else:
                            S.op('dve', lambda e, w=w, cc0=cc0, rb=rb, e_=e_, h=h: e.scalar_tensor_tensor(out=acc[:r, cc0:cc0 + w], in0=rb[:r, e_, :w], scalar=w_ap(h),
                                                                                                        in1=acc[:r, cc0:cc0 + w], op0=ALU.mult, op1=ALU.add),
                                 [rb.t, t_w, acc.t], [acc.t])

        def thresh_mask(r, L, ktiles, diag0, dw, cb_ap, need_bis, cap, diag_min, as_bias=False):
            S.op('dve', lambda e: e.tensor_tensor(out=acc[:r, diag0:diag0 + dw], in0=acc[:r, diag0:diag0 + dw], in1=cb_ap, op=ALU.add), [acc.t, t_c2], [acc.t])
            if not need_bis:
                S.op('dve', lambda e: e.memset(bs[:r, 6:7], -1.0e29), [], [bs.t])
            else:
                S.op('dve', lambda e: e.tensor_reduce(out=bs[:r, 0:1], in_=acc[:r, :L], axis=AX.X, op=ALU.max), [acc.t], [bs.t])
                S.op('dve', lambda e: e.tensor_reduce(out=bs[:r, 2:3], in_=acc[:r, :diag0], axis=AX.X, op=ALU.min), [acc.t], [bs.t])
                if diag_min:
                    S.op('dve', lambda e: e.scalar_tensor_tensor(out=tmpd[:r, :dw], in0=cb_ap, scalar=-2.0, in1=acc[:r, diag0:diag0 + dw], op0=ALU.mult, op1=ALU.add),
                         [acc.t, t_c2], [tmpd.t])
                    S.op('dve', lambda e: e.tensor_reduce(out=bs[:r, 1:2], in_=tmpd[:r, :dw], axis=AX.X, op=ALU.min), [tmpd.t], [bs.t])
                    S.op('dve', lambda e: e.tensor_tensor(out=bs[:r, 2:3], in0=bs[:r, 1:2], in1=bs[:r, 2:3], op=ALU.min), [bs.t], [bs.t])
                S.op('dve', lambda e: e.tensor_tensor(out=bs[:r, 7:8], in0=bs[:r, 0:1], in1=bs[:r, 2:3], op=ALU.subtract), [bs.t], [bs.t])
                S.op('dve', lambda e: e.tensor_scalar(out=bs[:r, 8:9 + NBIS], in0=pow2[:r, :], scalar1=bs[:r, 7:8], scalar2=None, op0=ALU.mult), [bs.t, t_c2], [bs.t])
                S.op('dve', lambda e: e.tensor_tensor(out=bs[:r, 3:4], in0=bs[:r, 2:3], in1=bs[:r, 8:9], op=ALU.add), [bs.t], [bs.t])
                for k in range(NBIS):
                    S.op('dve', lambda e: e.tensor_scalar(out=junk_bf[:r, :L], in0=acc[:r, :L], scalar1=bs[:r, 3:4], scalar2=None, op0=ALU.is_ge, op1=ALU.add,
                                                          accum_out=bs[:r, 4:5]), [acc.t, bs.t], [junk_bf.t, bs.t])
                    S.op('dve', lambda e, k=k: e.tensor_scalar(out=bs[:r, 5:6], in0=bs[:r, 4:5], scalar1=255.5, scalar2=bs[:r, 8 + k:9 + k], op0=ALU.is_ge, op1=ALU.mult),
                         [bs.t], [bs.t])
                    S.op('dve', lambda e, k=k: e.scalar_tensor_tensor(out=bs[:r, 3:4], in0=bs[:r, 3:4], scalar=bs[:r, 9 + k:10 + k], in1=bs[:r, 5:6], op0=ALU.subtract, op1=ALU.add),
                         [bs.t], [bs.t])
                if cap:
                    S.op('dve', lambda e: e.tensor_tensor(out=bs[:r, 2:3], in0=bs[:r, 3:4], in1=bs[:r, 8 + NBIS:9 + NBIS], op=ALU.subtract), [bs.t], [bs.t])
                    S.op('dve', lambda e: e.tensor_tensor(out=bs[:r, 6:7], in0=bs[:r, 2:3], in1=thrcap[:r, 0:1], op=ALU.min), [bs.t, t_c2], [bs.t])
                else:
                    S.op('dve', lambda e: e.tensor_tensor(out=bs[:r, 6:7], in0=bs[:r, 3:4], in1=bs[:r, 8 + NBIS:9 + NBIS], op=ALU.subtract), [bs.t], [bs.t])
            S.op('dve', lambda e: e.tensor_scalar(out=mask_bf[:r, :L], in0=acc[:r, :L], scalar1=bs[:r, 6:7], scalar2=None, op0=ALU.is_ge), [acc.t, bs.t], [mask_bf.t])
            for g0 in range(0, len(ktiles), 8):
                grp = ktiles[g0:g0 + 8]
                pb = 6 + (g0 // 8) % 2
                for s_, (kp0, kr, vi) in enumerate(grp):
                    S.op('pe', lambda e, s_=s_, kp0=kp0, kr=kr, pb=pb: e.transpose(pbf(pb)[:kr, s_ * 128:s_ * 128 + r], mask_bf[:r, kp0:kp0 + kr], ident[:r, :r]),
                         [mask_bf.t, ident.t], [t_pb[pb]])
                n_ = len(grp)
                if as_bias:
                    S.op('act', lambda e, g0=g0, n_=n_, pb=pb: e.activation(out=maskT[:, g0:g0 + n_, :r], in_=pbf(pb)[:, 0:n_ * 128].rearrange("p (s q) -> p s q", s=n_)[:, :, :r],
                                                                           func=AF.Copy, scale=30000.0, bias=-30000.0), [t_pb[pb]], [maskT.t])
                else:
                    S.op('act', lambda e, g0=g0, n_=n_, pb=pb: e.activation(out=maskT[:, g0:g0 + n_, :r], in_=pbf(pb)[:, 0:n_ * 128].rearrange("p (s q) -> p s q", s=n_)[:, :, :r],
                                                                           func=AF.Copy), [t_pb[pb]], [maskT.t])

        def attn_prompt(r, ktiles, qT_ap, t_qT, out_ap_fn, t_out):
            nk = len(ktiles)
            units = [(n, ii) + ktiles[ii] for n in range(2) for ii in range(nk)]
            LOOK = 2
            bufs = {}

            def emit_st(ui):
                n, ii, kp0, kr, vi = units[ui]
                sbk = att_cnt[1] % 4
                pt = PT[att_cnt[1] % 4]
                att_cnt[1] += 1
                bufs[ui] = pt
                S.op('pe', lambda e: e.matmul(pf(sbk)[:kr, 0:4 * r].rearrange("p (h q) -> p h q", h=4), KT[:, n, kp0:kp0 + kr],
                                              qT_ap[:, 4 * n:4 * n + 4, :], start=True, stop=False), [KT.t, t_qT], [t_pb[sbk]])
                S.op('pe', lambda e: e.matmul(pf(sbk)[:kr, 0:4 * r].rearrange("p (h q) -> p h q", h=4), ident[:kr, :kr],
                                              maskT[:kr, ii:ii + 1, :r].to_broadcast([kr, 4, r]), start=False, stop=True), [ident.t, maskT.t], [t_pb[sbk]])
                S.op('act', lambda e: e.activation(out=pt[:kr, :, :r], in_=pf(sbk)[:kr, 0:4 * r].rearrange("p (h q) -> p h q", h=4),
                                                   func=AF.Exp, scale=128 ** -0.5), [t_pb[sbk]], [pt.t])

            def emit_pv(ui):
                n, ii, kp0, kr, vi = units[ui]
                pt = bufs.pop(ui)
                S.op('pe', lambda e: e.matmul(pf(4)[:, 0:4 * r].rearrange("p (h q) -> p h q", h=4), V_bf[:kr, vi, n * 128:(n + 1) * 128],
                                              pt[:kr, :, :r], start=(ii == 0), stop=(ii == nk - 1)), [V_bf.tts[vi], pt.t], [t_pb[4]])
                S.op('pe', lambda e: e.matmul(pf(5)[:, 0:4 * r].rearrange("p (h q) -> p h q", h=4), ones_bf[:kr, :],
                                              pt[:kr, :, :r], start=(ii == 0), stop=(ii == nk - 1)), [ones_bf.t, pt.t], [t_pb[5]])
                if ii == nk - 1:
                    S.op('dve', lambda e: e.reciprocal(out=rec[:, 0:4 * r], in_=pf(5)[:, 0:4 * r]), [t_pb[5]], [rec.t])
                    S.op('dve', lambda e: e.tensor_tensor(out=out_ap_fn(n), in0=pf(4)[:, 0:4 * r].rearrange("p (h q) -> p h q", h=4),
                                                          in1=rec[:, 0:4 * r].rearrange("p (h q) -> p h q", h=4), op=ALU.mult), [t_pb[4], rec.t], [t_out])

            for i in range(len(units) + LOOK):
                if i < len(units):
                    emit_st(i)
                if i - LOOK >= 0:
                    emit_pv(i - LOOK)

        chunks = [(0, 512), (512, 512), (1024, 512), (1536, 512), (2048, 72)]
        for j in range(NT):
            r = ROWS[j]
            c0 = TOK0[j]
            for c, (cc0, cw) in enumerate(chunks):
                for k in range(8):
                    S.op('pe', lambda e, c=c, cc0=cc0, cw=cw, k=k, c0=c0, r=r: e.matmul(pf(c)[:r, :cw], xnT[:, k, c0:c0 + r], wd[:, k, cc0:cc0 + cw],
                                                                                     start=(k == 0), stop=(k == 7)),
                         [xnT.tts[j], wd.t], [t_pb[c]])
            for h in range(10):
                if h < 8:
                    src = pf(h // 4)[:r, (h % 4) * 128:(h % 4) * 128 + 128]
                    tp = t_pb[h // 4]
                else:
                    src = pf(2)[:r, (h - 8) * 128:(h - 8) * 128 + 128]
                    tp = t_pb[2]
                S.op('act', lambda e, src=src, r=r, h=h: e.activation(out=junk2[:r, :], in_=src, func=AF.Square, accum_out=st2[:r, h:h + 1]),
                     [tp], [junk2.t, st2.t])
            S.op('act', lambda e, r=r: e.activation(out=st2[:r, 0:10], in_=st2[:r, 0:10], func=AF.Sqrt, scale=1.0 / 128, bias=EPS), [st2.t], [st2.t])
            S.op('dve', lambda e, r=r: e.reciprocal(out=st2[:r, 0:10], in_=st2[:r, 0:10]), [st2.t], [st2.t])
            for h in range(8):
                src = pf(h // 4)[:r, (h % 4) * 128:(h % 4) * 128 + 128]
                S.op('dve', lambda e, src=src, r=r, h=h: e.scalar_tensor_tensor(out=q_bf[:r, h * 128:(h + 1) * 128], in0=src, scalar=st2[:r, h:h + 1],
                                                                              in1=wq_sb[:r, :], op0=ALU.mult, op1=ALU.mult),
                     [t_pb[h // 4], st2.t, t_const], [q_bf.t])
            for n in range(2):
                src = pf(2)[:r, n * 128:(n + 1) * 128]
                S.op('dve', lambda e, src=src, r=r, n=n: e.scalar_tensor_tensor(out=k_f[:r, n * 128:(n + 1) * 128], in0=src, scalar=st2[:r, 8 + n:9 + n],
                                                                              in1=wk_sb[:r, :], op0=ALU.mult, op1=ALU.mult),
                     [t_pb[2], st2.t, t_const], [k_f.t])
            S.op('act', lambda e, r=r: e.activation(out=k_bf[:r, :], in_=k_f[:r, :], func=AF.Copy), [k_f.t], [k_bf.t])
            S.op('act', lambda e, r=r: e.activation(out=v_f[:r, :], in_=pf(2)[:r, 256:512], func=AF.Copy), [t_pb[2]], [v_f.t])
            S.op('dve', lambda e, r=r, j=j: e.tensor_copy(out=V_bf[:r, j, :], in_=pf(2)[:r, 256:512]), [t_pb[2]], [V_bf.tts[j]])
            S.op('act', lambda e, r=r: e.activation(out=iq_bf[:r, :], in_=pf(3)[:r, :], func=AF.Copy), [t_pb[3]], [iq_bf.t])
            S.op('act', lambda e, r=r: e.activation(out=junk2[:r, 0:64], in_=pf(4)[:r, 0:64], func=AF.Copy, accum_out=st2[:r, 10:11]),
                 [t_pb[4]], [junk2.t, st2.t])
            S.op('act', lambda e, r=r: e.activation(out=junk2[:r, 0:64], in_=pf(4)[:r, 0:64], func=AF.Square, accum_out=st2[:r, 11:12]),
                 [t_pb[4]], [junk2.t, st2.t])
            S.op('dve', lambda e, r=r: e.tensor_scalar(out=st2[:r, 10:12], in0=st2[:r, 10:12], scalar1=1.0 / 64, scalar2=None, op0=ALU.mult), [st2.t], [st2.t])
            S.op('dve', lambda e, r=r: e.tensor_tensor(out=st2[:r, 12:13], in0=st2[:r, 10:11], in1=st2[:r, 10:11], op=ALU.mult), [st2.t], [st2.t])
            S.op('dve', lambda e, r=r: e.tensor_tensor(out=st2[:r, 12:13], in0=st2[:r, 11:12], in1=st2[:r, 12:13], op=ALU.subtract), [st2.t], [st2.t])
            S.op('act', lambda e, r=r: e.activation(out=st2[:r, 12:13], in_=st2[:r, 12:13], func=AF.Sqrt, scale=1.0, bias=EPS), [st2.t], [st2.t])
            S.op('dve', lambda e, r=r: e.reciprocal(out=st2[:r, 12:13], in_=st2[:r, 12:13]), [st2.t], [st2.t])
            S.op('dve', lambda e, r=r: e.tensor_scalar(out=ki_t[:r, :], in0=pf(4)[:r, 0:64], scalar1=st2[:r, 10:11], scalar2=st2[:r, 12:13],
                                                       op0=ALU.subtract, op1=ALU.mult), [t_pb[4], st2.t], [ki_t.t])
            S.op('dve', lambda e, r=r: e.tensor_tensor(out=ki_t[:r, :], in0=ki_t[:r, :], in1=ikw_sb[:r, :], op=ALU.mult), [ki_t.t, t_const], [ki_t.t])
            S.op('dve', lambda e, r=r: e.tensor_tensor(out=ki_f[:r, :], in0=ki_t[:r, :], in1=ikb_sb[:r, :], op=ALU.add), [ki_t.t, t_const], [ki_f.t])
            S.op('act', lambda e, r=r: e.activation(out=ik2_bf[:r, 0:64], in_=ki_f[:r, :], func=AF.Copy), [ki_f.t], [ik2_bf.t])
            S.op('act', lambda e, r=r: e.activation(out=ik2_bf[:r, 64:128], in_=ki_f[:r, :], func=AF.Copy), [ki_f.t], [ik2_bf.t])
            S.op('act', lambda e, r=r, j=j: e.activation(out=widx[:r, j, :], in_=pf(4)[:r, 64:72], func=AF.Copy, scale=IDX_SCALE), [t_pb[4]], [widx.tts[j]])
            if j == 17:
                dk, dv, dki = o_ks[:, :], o_vs[:, :], o_kis[:, :]
            else:
                p0 = pos0(j)
                dk, dv, dki = o_kp[p0:p0 + r, :], o_vp[p0:p0 + r, :], o_kip[p0:p0 + r, :]
            S.dma('sp', lambda e, r=r, dk=dk: e.dma_start(out=dk, in_=k_f[:r, :]), k_f.t, False)
            S.dma('sp', lambda e, r=r, dv=dv: e.dma_start(out=dv, in_=v_f[:r, :]), v_f.t, False)
            S.dma('sp', lambda e, r=r, dki=dki: e.dma_start(out=dki, in_=ki_f[:r, :]), ki_f.t, False)
            p0 = pos0(j) if j < 17 else 0
            for h in range(8):
                S.op('pe', lambda e, h=h, r=r: e.transpose(pbf(6)[:, h * 128:h * 128 + r], q_bf[:r, h * 128:(h + 1) * 128], ident[:r, :r]), [q_bf.t, ident.t], [t_pb[6]])
            QTd = QT if j < 17 else QT_s
            S.op('dve', lambda e, r=r, QTd=QTd: e.tensor_copy(out=QTd[:, :, :r], in_=pbf(6)[:, :].rearrange("p (s q) -> p s q", s=8)[:, :, :r]), [t_pb[6]], [QTd.t])
            for n in range(2):
                S.op('pe', lambda e, n=n, r=r: e.transpose(pbf(7)[:, n * 128:n * 128 + r], k_bf[:r, n * 128:(n + 1) * 128], ident[:r, :r]), [k_bf.t, ident.t], [t_pb[7]])
            for hp in range(4):
                S.op('pe', lambda e, hp=hp, r=r: e.transpose(pbf(7)[:, (2 + hp) * 128:(2 + hp) * 128 + r], iq_bf[:r, hp * 128:(hp + 1) * 128], ident[:r, :r]),
                     [iq_bf.t, ident.t], [t_pb[7]])
            S.op('pe', lambda e, r=r: e.transpose(pbf(7)[:, 6 * 128:6 * 128 + r], ik2_bf[:r, :], ident[:r, :r]), [ik2_bf.t, ident.t], [t_pb[7]])
            if j == 17:
                S.op('act', lambda e: e.activation(out=KT_s[:, :, :], in_=pbf(7)[:, 0:256].rearrange("p (s q) -> p s q", s=2), func=AF.Copy), [t_pb[7]], [KT_s.t])
                S.op('act', lambda e: e.activation(out=ikT_s[:, :], in_=pbf(7)[:, 768:896], func=AF.Copy), [t_pb[7]], [ikT_s.t])
                S.op('dve', lambda e: e.tensor_copy(out=iq_dup[:, :, 0:64], in_=iq_bf[:, :].rearrange("p (h d) -> p h d", h=8)), [iq_bf.t], [iq_dup.t])
                S.op('dve', lambda e: e.tensor_copy(out=iq_dup[:, :, 64:128], in_=iq_bf[:, :].rearrange("p (h d) -> p h d", h=8)), [iq_bf.t], [iq_dup.t])
                for h in range(8):
                    S.op('pe', lambda e, h=h: e.transpose(pbf(6)[:, h * 128:(h + 1) * 128], iq_dup[:, h, :], ident[:, :]), [iq_dup.t, ident.t], [t_pb[6]])
                S.op('act', lambda e: e.activation(out=iqT8[:, :, :], in_=pbf(6)[:, :].rearrange("p (s q) -> p s q", s=8), func=AF.Copy), [t_pb[6]], [iqT8.t])
                continue
            S.op('act', lambda e, r=r, p0=p0: e.activation(out=KT[:, :, p0:p0 + r], in_=pbf(7)[:, 0:256].rearrange("p (s q) -> p s q", s=2)[:, :, :r], func=AF.Copy),
                 [t_pb[7]], [KT.t])
            S.op('act', lambda e, r=r: e.activation(out=iqT[:, :, :r], in_=pbf(7)[:, 256:768].rearrange("p (s q) -> p s q", s=4)[:, :, :r], func=AF.Copy),
                 [t_pb[7]], [iqT.t])
            S.op('act', lambda e, r=r, p0=p0: e.activation(out=ikT2[:, p0:p0 + r], in_=pbf(7)[:, 768:768 + r], func=AF.Copy), [t_pb[7]], [ikT2.t])
            kts = [(pos0(i), ROWS[i], i) for i in range(j + 1)]
            idx_scores_prompt(r, p0 + r, iqT[:, :, :r], iqT.t, (lambda h, r=r, j=j: widx[:r, j, h:h + 1]), widx.tts[j])
            thresh_mask(r, p0 + r, kts, p0, r, cbias[:r, :r], (p0 + r - 1 >= 256), (j == 2), True, as_bias=True)
            attn_prompt(r, kts, QT[:, :, :r], QT.t, (lambda n, r=r, c0=c0: o_dsaT[:, 4 * n:4 * n + 4, c0:c0 + r]), o_dsaT.tts[j])
        A.release(wd, st2, junk2, q_bf, k_f, k_bf, v_f, iq_bf, ki_f, ki_t, ik2_bf, QT, iqT, iq_dup, ikT2, tmpd)
        ptb = A.alloc([128, 64], I32)
        idx_i = A.alloc([128, 64], I32)
        pidx = A.alloc([128, 1], F32)
        S.dma('sp', lambda e: e.dma_start(out=ptb[:, :], in_=pt_d), ptb.t, True)
        S.dma('sp', lambda e: e.dma_start(out=pidx[:, :], in_=pidx_d), pidx.t, True)
        S.op('dve', lambda e: e.tensor_scalar(out=idx_i[:, :], in0=ptb[:, :], scalar1=32.0, scalar2=pidx[:, 0:1], op0=ALU.mult, op1=ALU.add),
             [ptb.t, pidx.t], [idx_i.t])
        ikT_all = A.alloc([128, 8, 2056], BF16)
        kipg = [A.alloc([128, 16, 64], F32) for _ in range(2)]
        ik2pg = A.alloc([128, 16, 128], BF16)
        iqm = A.alloc([128, 8, 8, 128], BF16)
        bm2 = A.alloc([128, 8, 128], BF16)
        cbs = A.alloc([128, 8], F32)
        S.dma('pool', lambda e: e.dma_start(out=bm2.ap, in_=bm2_d), bm2.t, True)
        S.dma('sp', lambda e: e.dma_start(out=cbs.ap, in_=cbs_d), cbs.t, True)
        S.op('dve', lambda e: e.tensor_tensor(out=iqm[:, :, :, :], in0=iqT8[:, :, :].unsqueeze(2).to_broadcast([128, 8, 8, 128]),
                                              in1=bm2[:, :, :].unsqueeze(1).to_broadcast([128, 8, 8, 128]), op=ALU.mult), [iqT8.t, bm2.t], [iqm.t])
        for p in range(8):
            for half in range(2):
                b = p + 8 * half
                for hh in range(4):
                    S.dma('pool', lambda e, b=b, half=half, hh=hh: e.indirect_dma_start(out=kipg[half][:, hh * 4:(hh + 1) * 4, :].rearrange("p a b -> p (a b)"), out_offset=None,
                                                                                    in_=cki_d[:, :], in_offset=bass.IndirectOffsetOnAxis(ap=idx_i[:, 4 * b + hh:4 * b + hh + 1], axis=0)),
                          kipg[half].t, True, extra_reads=[idx_i.t])
                S.op('act', lambda e, half=half: e.activation(out=ik2pg[:, :, half * 64:(half + 1) * 64], in_=kipg[half][:, :, :], func=AF.Copy), [kipg[half].t], [ik2pg.t])
            for g in range(2):
                for s_ in range(8):
                    S.op('pe', lambda e, g=g, s_=s_: e.transpose(pbf(6 + g)[:, s_ * 128:(s_ + 1) * 128], ik2pg[:, g * 8 + s_, :], ident[:, :]),
                         [ik2pg.t, ident.t], [t_pb[6 + g]])
                S.op('act', lambda e, g=g, p=p: e.activation(out=ikT_all[:, p, g * 1024:(g + 1) * 1024], in_=pbf(6 + g)[:, :], func=AF.Copy), [t_pb[6 + g]], [ikT_all.t])
            S.op('dve', lambda e, p=p: e.tensor_copy(out=ikT_all[0:64, p, 2048:2056], in_=ikT_s[0:64, p * 8:(p + 1) * 8]), [ikT_s.t], [ikT_all.t])
            S.op('dve', lambda e, p=p: e.tensor_copy(out=ikT_all[64:128, p, 2048:2056], in_=ikT_s[64:128, (p + 8) * 8:(p + 9) * 8]), [ikT_s.t], [ikT_all.t])
        for cc0 in range(0, 2056, 512):
            w = min(512, 2056 - cc0)
            for hp in range(4):
                rb = rl[att_cnt[0] % 2]
                att_cnt[0] += 1
                for e_ in range(2):
                    h = 2 * hp + e_
                    for p in range(8):
                        S.op('pe', lambda e, e_=e_, h=h, p=p, w=w, cc0=cc0: e.matmul(pf(e_)[:, :w], iqm[:, h, p, :], ikT_all[:, p, cc0:cc0 + w], start=(p == 0), stop=(p == 7)),
                             [iqm.t, ikT_all.t], [t_pb[e_]])
                    S.op('act', lambda e, e_=e_, w=w, rb=rb: e.activation(out=rb[:, e_, :w], in_=pf(e_)[:, :w], func=AF.Relu), [t_pb[e_]], [rb.t])
                for e_ in range(2):
                    h = 2 * hp + e_
                    if h == 0:
                        S.op('dve', lambda e, w=w, cc0=cc0, rb=rb: e.tensor_scalar(out=acc[:, cc0:cc0 + w], in0=rb[:, 0, :w], scalar1=widx[:, 17, 0:1], scalar2=None, op0=ALU.mult),
                             [rb.t, widx.tts[17]], [acc.t])
                    else:
                        S.op('dve', lambda e, w=w, cc0=cc0, rb=rb, e_=e_, h=h: e.scalar_tensor_tensor(out=acc[:, cc0:cc0 + w], in0=rb[:, e_, :w], scalar=widx[:, 17, h:h + 1],
                                                                                                    in1=acc[:, cc0:cc0 + w], op0=ALU.mult, op1=ALU.add),
                             [rb.t, widx.tts[17], acc.t], [acc.t])
        skt = [(pg * 128, 128, pg) for pg in range(16)] + [(2048, 8, 16)]
        thresh_mask(128, 2056, skt, 2048, 8, cbs[:, :], True, False, False)
        A.release(ikT_all, kipg[0], kipg[1], ik2pg, iqm, bm2, cbs, acc, junk_bf, rl[0], rl[1], mask_bf)
        Kg = [A.alloc([128, 16, 256], BF16) for _ in range(2)]
        Vg = [A.alloc([128, 16, 256], BF16) for _ in range(2)]
        Vn = [A.alloc([8, 256], BF16) for _ in range(2)]
        KTb = [KT, A.alloc([128, 2, 2064], BF16)]
        PTs = [A.alloc([128, 16, 4, 8], BF16) for _ in range(2)]
        PTn = [A.alloc([128, 4, 8], BF16) for _ in range(2)]
        pend = [None]
        sample_bufs = {}

        def emit_st_s(b, n, db, KTc):
            pts = PTs[att_cnt[1] % 2]
            ptn = PTn[att_cnt[1] % 2]
            sbk = 2 + att_cnt[1] % 2
            att_cnt[1] += 1
            sample_bufs[(b, n)] = (pts, ptn)
            qv = QT_s[:, 4 * n:4 * n + 4, b * 8:(b + 1) * 8]
            for pg in range(16):
                S.op('pe', lambda e, pg=pg: e.matmul(pf(sbk)[:, pg * 32:(pg + 1) * 32].rearrange("p (h q) -> p h q", h=4), KTc[:, n, pg * 128:(pg + 1) * 128],
                                                     qv, start=True, stop=True), [KTc.t, QT_s.t], [t_pb[sbk]])
            S.op('pe', lambda e: e.matmul(pf(0)[:8, 0:32].rearrange("p (h q) -> p h q", h=4), KTc[:, n, 2048:2056], qv, start=True, stop=True),
                 [KTc.t, QT_s.t], [t_pb[0]])
            S.op('act', lambda e: e.activation(out=pts[:, :, :, :], in_=pf(sbk)[:, :].rearrange("p (g h q) -> p g h q", g=16, h=4), func=AF.Exp, scale=128 ** -0.5),
                 [t_pb[sbk]], [pts.t])
            S.op('act', lambda e: e.activation(out=ptn[:8, :, :], in_=pf(0)[:8, 0:32].rearrange("p (h q) -> p h q", h=4), func=AF.Exp, scale=128 ** -0.5),
                 [t_pb[0]], [ptn.t])
            S.op('dve', lambda e: e.tensor_tensor(out=pts[:, :, :, :], in0=pts[:, :, :, :],
                                                  in1=maskT[:, 0:16, b * 8:(b + 1) * 8].unsqueeze(2).to_broadcast([128, 16, 4, 8]), op=ALU.mult), [pts.t, maskT.t], [pts.t])
            S.op('dve', lambda e: e.tensor_tensor(out=ptn[:8, :, :], in0=ptn[:8, :, :],
                                                  in1=maskT[:8, 16:17, b * 8:(b + 1) * 8].to_broadcast([8, 4, 8]), op=ALU.mult), [ptn.t, maskT.t], [ptn.t])

        def emit_pv_s(b, n, db, pb_):
            pts, ptn = pb_
            cs_ = TOK0[17] + b * 8
            for pg in range(17):
                if pg < 16:
                    lv = Vg[db][:, pg, n * 128:(n + 1) * 128]
                    lo_ = ones_bf[:, :]
                    rv = pts[:, pg, :, :]
                    rt = pts.t
                    vt = Vg[db].t
                else:
                    lv = Vn[db][:8, n * 128:(n + 1) * 128]
                    lo_ = ones_bf[:8, :]
                    rv = ptn[:8, :, :]
                    rt = ptn.t
                    vt = Vn[db].t
                S.op('pe', lambda e, lv=lv, rv=rv, pg=pg: e.matmul(pf(4)[:, 0:32].rearrange("p (h q) -> p h q", h=4), lv, rv, start=(pg == 0), stop=(pg == 16)),
                     [vt, rt], [t_pb[4]])
                S.op('pe', lambda e, lo_=lo_, rv=rv, pg=pg: e.matmul(pf(5)[:, 0:32].rearrange("p (h q) -> p h q", h=4), lo_, rv, start=(pg == 0), stop=(pg == 16)),
                     [ones_bf.t, rt], [t_pb[5]])
            S.op('dve', lambda e: e.reciprocal(out=rec[:, 0:32], in_=pf(5)[:, 0:32]), [t_pb[5]], [rec.t])
            S.op('dve', lambda e: e.tensor_tensor(out=o_dsaT[:, 4 * n:4 * n + 4, cs_:cs_ + 8], in0=pf(4)[:, 0:32].rearrange("p (h q) -> p h q", h=4),
                                                  in1=rec[:, 0:32].rearrange("p (h q) -> p h q", h=4), op=ALU.mult), [t_pb[4], rec.t], [o_dsaT.tts[17]])

        for b in range(16):
            db = b % 2
            KTc = KTb[db]
            for hh in range(4):
                S.dma('pool', lambda e, b=b, db=db, hh=hh: e.indirect_dma_start(out=Kg[db][:, hh * 4:(hh + 1) * 4, :].rearrange("p a b -> p (a b)"), out_offset=None, in_=ck_d[:, :],
                                                                            in_offset=bass.IndirectOffsetOnAxis(ap=idx_i[:, 4 * b + hh:4 * b + hh + 1], axis=0)),
                      Kg[db].t, True, extra_reads=[idx_i.t])
            for hh in range(4):
                S.dma('pool', lambda e, b=b, db=db, hh=hh: e.indirect_dma_start(out=Vg[db][:, hh * 4:(hh + 1) * 4, :].rearrange("p a b -> p (a b)"), out_offset=None, in_=cv_d[:, :],
                                                                            in_offset=bass.IndirectOffsetOnAxis(ap=idx_i[:, 4 * b + hh:4 * b + hh + 1], axis=0)),
                      Vg[db].t, True, extra_reads=[idx_i.t])
            for g in range(4):
                pb = 6 + g % 2
                for pl in range(4):
                    for n in range(2):
                        S.op('pe', lambda e, g=g, pl=pl, n=n, pb=pb, db=db: e.transpose(pbf(pb)[:, (pl * 2 + n) * 128:(pl * 2 + n + 1) * 128], Kg[db][:, g * 4 + pl, n * 128:(n + 1) * 128], ident[:, :]),
                             [Kg[db].t, ident.t], [t_pb[pb]])
                S.op('act', lambda e, g=g, pb=pb, KTc=KTc: e.activation(out=KTc[:, :, g * 512:(g + 1) * 512].rearrange("p n (g q) -> p n g q", g=4),
                                                                       in_=pbf(pb)[:, :].rearrange("p (g n q) -> p n g q", g=4, n=2), func=AF.Copy), [t_pb[pb]], [KTc.t])
            S.op('dve', lambda e, b=b, KTc=KTc: e.tensor_copy(out=KTc[:, :, 2048:2056], in_=KT_s[:, :, b * 8:(b + 1) * 8]), [KT_s.t], [KTc.t])
            S.dma('sp', lambda e, b=b, db=db: e.dma_start(out=Vn[db][0:8, :], in_=V_bf[b * 8:(b + 1) * 8, 17, :]), Vn[db].t, True, extra_reads=[V_bf.tts[17]])
            for n in range(2):
                emit_st_s(b, n, db, KTc)
                if pend[0] is not None:
                    emit_pv_s(*pend[0])
                pend[0] = (b, n, db, sample_bufs.pop((b, n)))
        emit_pv_s(*pend[0])
        A.release(widx, V_bf, KT, maskT, PT[0], PT[1], PT[2], PT[3], rec, bs, ones_bf, cbias, pow2, thrcap,
                  QT_s, iqT8, KT_s, ikT_s, ptb, idx_i, pidx, Kg[0], Kg[1], Vg[0], Vg[1], Vn[0], Vn[1], KTb[1], PTs[0], PTs[1], PTn[0], PTn[1])

        o_retT = A.alloc([128, 16, TTOT], BF16, ntt=NT)
        decT = A.alloc([128, 8, 128], F32)
        qdec = A.alloc([128, 8], F32)
        kdec = A.alloc([128, 12], F32)
        bm = A.alloc([128, 16, 128], BF16)
        rm = A.alloc([128, 16], F32)
        t_rc = TT(list(A.dead.items()))
        for dst, src in ((decT, decT_d), (qdec, qdec_d), (kdec, kdec_d), (rm, rm_d)):
            S.dma('sp', lambda e, dst=dst, src=src: e.dma_start(out=dst.ap, in_=src), t_rc, True)
        S.dma('pool', lambda e: e.dma_start(out=bm.ap, in_=bm_d), t_rc, True)
        wr = A.alloc([128, 8, 1536], BF16)
        rot = [A.alloc([128, 256], F32) for _ in range(2)]
        rA = [A.alloc([128, 512], F32) for _ in range(2)]
        rB = [A.alloc([128, 512], F32)] * 2
        qk6 = [A.alloc([128, 4, 256], BF16) for _ in range(2)]
        v_bf = [A.alloc([128, 512], BF16) for _ in range(2)]
        g_s = [A.alloc([128, 512], F32) for _ in range(2)]
        T6 = [A.alloc([128, 6, 128], BF16) for _ in range(2)]
        scT = A.alloc([128, 128], BF16)
        junk3 = A.alloc([128, 512], BF16)
        o_bf = A.alloc([128, 512], BF16)
        st3 = A.alloc([128, 4], F32)
        S_f = [A.alloc([128, 2, 512], F32, ntt=2) for _ in range(2)]
        S_bf = A.alloc([128, 2, 512], BF16, ntt=2)
        qsm = [A.alloc([128, 2, 128], BF16)] * 2
        kdm = [A.alloc([128, 256], BF16)] * 2

        def v4(ap):
            return ap.rearrange("p (a h d) -> p a h d", a=2, h=2)

        def ret_load_w(h):
            segs = [(C_RQ + h * 256, 256, 0), (C_RK + h * 256, 256, 256), (C_RV + h * 512, 512, 512), (C_RG + h * 512, 512, 1024)]
            for (s0, w, d0) in segs:
                for k in range(8):
                    S.dma('pool', lambda e, s0=s0, w=w, d0=d0, k=k: e.dma_start(out=wr[:, k, d0:d0 + w], in_=w_in[k * 128:(k + 1) * 128, s0:s0 + w]), wr.t, True)

        def ret_front(h, j, pp):
            r = ROWS[j]
            c0 = TOK0[j]
            var = 1 if j == 17 else 0
            kvar = 2 if j == 17 else (1 if j == 0 else 0)
            rA_, rB_, qk_, vb_, gs_, T6_, rot_ = rA[pp], rB[pp], qk6[pp], v_bf[pp], g_s[pp], T6[pp], rot[pp]
            S.dma('sp', lambda e: e.dma_start(out=rot_[:r, :], in_=rot_d[j, :r, :]), rot_.t, True)
            for (bank, cc0) in ((0, 0), (1, 512), (2, 1024)):
                for k in range(8):
                    S.op('pe', lambda e, bank=bank, cc0=cc0, k=k: e.matmul(pf(bank)[:r, :], xnT[:, k, c0:c0 + r], wr[:, k, cc0:cc0 + 512], start=(k == 0), stop=(k == 7)),
                         [xnT.tts[j], wr.t], [t_pb[bank]])
            S.op('dve', lambda e: e.tensor_tensor(out=rA_[:r, :].rearrange("p (b d) -> p b d", b=4), in0=pf(0)[:r, :].rearrange("p (b d) -> p b d", b=4),
                                                  in1=rot_[:r, 0:128].unsqueeze(1).to_broadcast([r, 4, 128]), op=ALU.mult), [t_pb[0], rot_.t], [rA_.t])
            S.op('dve', lambda e: e.tensor_tensor(out=rB_[:r, :].rearrange("p (b d) -> p b d", b=4), in0=pf(0)[:r, :].rearrange("p (b d) -> p b d", b=4),
                                                  in1=rot_[:r, 128:256].unsqueeze(1).to_broadcast([r, 4, 128]), op=ALU.mult), [t_pb[0], rot_.t], [rB_.t])
            S.op('pool', lambda e: e.tensor_tensor(out=v4(rA_[:r, :])[:, :, 0, :], in0=v4(rA_[:r, :])[:, :, 0, :], in1=v4(rB_[:r, :])[:, :, 1, :], op=ALU.subtract),
                 [rA_.t, rB_.t], [rA_.t])
            S.op('pool', lambda e: e.tensor_tensor(out=v4(rA_[:r, :])[:, :, 1, :], in0=v4(rB_[:r, :])[:, :, 0, :], in1=v4(rA_[:r, :])[:, :, 1, :], op=ALU.add),
                 [rA_.t, rB_.t], [rA_.t])
            S.op('act', lambda e: e.activation(out=qk_[:r, 0, :], in_=rA_[:r, 0:256], func=AF.Copy), [rA_.t], [qk_.t])
            S.op('act', lambda e: e.activation(out=qk_[:r, 1, :], in_=rA_[:r, 0:256], func=AF.Copy, scale=qdec[:r, var * 4 + h:var * 4 + h + 1]), [rA_.t, t_rc], [qk_.t])
            S.op('act', lambda e: e.activation(out=qk_[:r, 2, :], in_=rA_[:r, 256:512], func=AF.Copy, scale=1.0 / 16), [rA_.t], [qk_.t])
            S.op('dve', lambda e: e.tensor_scalar(out=qk_[:r, 3, :], in0=rA_[:r, 256:512], scalar1=kdec[:r, kvar * 4 + h:kvar * 4 + h + 1], scalar2=None, op0=ALU.mult),
                 [rA_.t, t_rc], [qk_.t])
            S.op('act', lambda e: e.activation(out=vb_[:r, :], in_=pf(1)[:r, :], func=AF.Copy), [t_pb[1]], [vb_.t])
            S.op('act', lambda e: e.activation(out=gs_[:r, :], in_=pf(2)[:r, :], func=AF.Silu), [t_pb[2]], [gs_.t])
            for s_ in range(3):
                for c in range(2):
                    S.op('pe', lambda e, s_=s_, c=c: e.transpose(pbf(6)[:, (2 * s_ + c) * 128:(2 * s_ + c) * 128 + r], qk_[:r, s_, c * 128:(c + 1) * 128], ident[:r, :r]),
                         [qk_.t, ident.t], [t_pb[6]])
            S.op('dve', lambda e: e.tensor_copy(out=T6_[:, :, :r], in_=pbf(6)[:, 0:768].rearrange("p (s q) -> p s q", s=6)[:, :, :r]), [t_pb[6]], [T6_.t])

        def ret_back(h, j, pp):
            r = ROWS[j]
            c0 = TOK0[j]
            var = 1 if j == 17 else 0
            Cj = 8 if j == 17 else r
            qk_, vb_, gs_, T6_ = qk6[pp], v_bf[pp], g_s[pp], T6[pp]
            if j == 0:
                S.op('dve', lambda e: e.memset(S_f[0][:, :, :], 0.0), [], S_f[0].tts)
                S.op('dve', lambda e: e.memset(S_bf[:, :, :], 0.0), [], S_bf.tts)
            for c in range(2):
                S.op('pe', lambda e, c=c: e.matmul(pf(3)[:r, :r], T6_[:, 4 + c, :r], T6_[:, c, :r], start=(c == 0), stop=(c == 1)), [T6_.t], [t_pb[3]])
            S.op('dve', lambda e: e.tensor_tensor(out=scT[:r, :r], in0=pf(3)[:r, :r], in1=decT[:r, var * 4 + h, :r], op=ALU.mult), [t_pb[3], t_rc], [scT.t])
            if j < 17:
                S.op('pe', lambda e: e.matmul(pf(4)[:r, :], scT[:r, :r], vb_[:r, :], start=True, stop=False), [scT.t, vb_.t], [t_pb[4]])
                for c in range(2):
                    S.op('pe', lambda e, c=c: e.matmul(pf(4)[:r, :], T6_[:, 2 + c, :r], S_bf[:, c, :], start=False, stop=(c == 1)), [T6_.t, S_bf.tts[c]], [t_pb[4]])
                for c in range(2):
                    bk = 5 if c == 0 else 3
                    S.op('pe', lambda e, c=c, bk=bk: e.matmul(pf(bk)[:, :], qk_[:r, 3, c * 128:(c + 1) * 128], vb_[:r, :], start=True, stop=True), [qk_.t, vb_.t], [t_pb[bk]])
                    S.op('dve', lambda e, c=c, bk=bk: e.scalar_tensor_tensor(out=S_f[0][:, c, :], in0=S_f[0][:, c, :], scalar=GAM[h] ** Cj, in1=pf(bk)[:, :],
                                                                           op0=ALU.mult, op1=ALU.add), [S_f[0].tts[c], t_pb[bk]], [S_f[0].tts[c]])
                    S.op('act', lambda e, c=c: e.activation(out=S_bf[:, c, :], in_=S_f[0][:, c, :], func=AF.Copy), [S_f[0].tts[c]], [S_bf.tts[c]])
                if j == 16:
                    for c in range(2):
                        S.dma('sp', lambda e, c=c: e.dma_start(out=o_rp[h, c * 128:(c + 1) * 128, :], in_=S_f[0][:, c, :]), S_f[0].tts[c], False)
            else:
                S.op('pe', lambda e: e.matmul(pf(4)[:, :], scT[:, :], vb_[:, :], start=True, stop=False), [scT.t, vb_.t], [t_pb[4]])
                for b in range(16):
                    S.op('dve', lambda e, b=b: e.tensor_tensor(out=qsm[0][:, :, :], in0=T6_[:, 2:4, :], in1=bm[:, b:b + 1, :].to_broadcast([128, 2, 128]), op=ALU.mult),
                         [T6_.t, t_rc], [qsm[0].t])
                    S.op('dve', lambda e, b=b: e.tensor_scalar(out=kdm[0][:, :], in0=qk_[:, 3, :], scalar1=rm[:, b:b + 1], scalar2=None, op0=ALU.mult),
                         [qk_.t, t_rc], [kdm[0].t])
                    for c in range(2):
                        u = 2 * b + c
                        su = u % 4
                        Sb, tS = S_f[su // 2], S_f[su // 2].tts[su % 2]
                        hs = su % 2
                        bk = 5 if u % 2 == 0 else 3
                        S.dma('pool', lambda e, b=b, c=c, Sb=Sb, hs=hs: e.dma_start(out=Sb[:, hs, :], in_=state[b, h, c * 128:(c + 1) * 128, :]), tS, True)
                        S.op('act', lambda e, Sb=Sb, hs=hs, u=u: e.activation(out=S_bf[:, u % 2, :], in_=Sb[:, hs, :], func=AF.Copy), [tS], [S_bf.tts[u % 2]])
                        S.op('pe', lambda e, c=c, b=b, u=u: e.matmul(pf(4)[:, :], qsm[0][:, c, :], S_bf[:, u % 2, :], start=False, stop=(b == 15 and c == 1)),
                             [qsm[0].t, S_bf.tts[u % 2]], [t_pb[4]])
                        S.op('pe', lambda e, c=c, bk=bk: e.matmul(pf(bk)[:, :], kdm[0][:, c * 128:(c + 1) * 128], vb_[:, :], start=True, stop=True),
                             [kdm[0].t, vb_.t], [t_pb[bk]])
                        S.op('dve', lambda e, Sb=Sb, hs=hs, bk=bk: e.scalar_tensor_tensor(out=Sb[:, hs, :], in0=Sb[:, hs, :], scalar=GAM[h] ** 8, in1=pf(bk)[:, :],
                                                                                        op0=ALU.mult, op1=ALU.add), [tS, t_pb[bk]], [tS])
                        S.dma('sp', lambda e, b=b, c=c, Sb=Sb, hs=hs: e.dma_start(out=o_rs[b, h, c * 128:(c + 1) * 128, :], in_=Sb[:, hs, :]), tS, False)
            S.op('act', lambda e: e.activation(out=junk3[:r, :], in_=pf(4)[:r, :], func=AF.Square, accum_out=st3[:r, 0:1]), [t_pb[4]], [junk3.t, st3.t])
            S.op('act', lambda e: e.activation(out=st3[:r, 0:1], in_=st3[:r, 0:1], func=AF.Sqrt, scale=1.0 / 512, bias=EPS), [st3.t], [st3.t])
            S.op('dve', lambda e: e.reciprocal(out=st3[:r, 0:1], in_=st3[:r, 0:1]), [st3.t], [st3.t])
            S.op('dve', lambda e: e.scalar_tensor_tensor(out=o_bf[:r, :], in0=pf(4)[:r, :], scalar=st3[:r, 0:1], in1=gs_[:r, :], op0=ALU.mult, op1=ALU.mult),
                 [t_pb[4], st3.t, gs_.t], [o_bf.t])
            for c in range(4):
                S.op('pe', lambda e, c=c: e.transpose(pbf(7)[:, c * 128:c * 128 + r], o_bf[:r, c * 128:(c + 1) * 128], ident[:r, :r]), [o_bf.t, ident.t], [t_pb[7]])
            S.op('act', lambda e: e.activation(out=o_retT[:, 4 * h:4 * h + 4, c0:c0 + r], in_=pbf(7)[:, 0:512].rearrange("p (s q) -> p s q", s=4)[:, :, :r], func=AF.Copy),
                 [t_pb[7]], [o_retT.tts[j]])

        steps = [(h, j) for h in range(4) for j in range(NT)]
        ret_load_w(0)
        ret_front(steps[0][0], steps[0][1], 0)
        for i in range(len(steps)):
            nxt = steps[i + 1] if i + 1 < len(steps) else None
            if nxt is not None and nxt[1] == 0:
                ret_load_w(nxt[0])
                ret_back(steps[i][0], steps[i][1], i % 2)
                ret_front(nxt[0], nxt[1], (i + 1) % 2)
            else:
                if nxt is not None:
                    ret_front(nxt[0], nxt[1], (i + 1) % 2)
                ret_back(steps[i][0], steps[i][1], i % 2)
        A.release(decT, qdec, kdec, bm, rm, wr, rot[0], rot[1], rA[0], rA[1], rB[0], qk6[0], qk6[1], v_bf[0], v_bf[1], g_s[0], g_s[1], T6[0], T6[1],
                  scT, junk3, o_bf, st3, S_f[0], S_f[1], S_bf, qsm[0], kdm[0])

        def tts_for(buf, t0, w):
            return [buf.tts[j] for j in range(NT) if TOK0[j] < t0 + w and TOK0[j] + ROWS[j] > t0]

        mT = A.alloc([128, 8, TTOT], BF16, ntt=NT)
        wrp_c = [A.alloc([128, 16, 128], BF16) for _ in range(2)]
        wdp_c = [A.alloc([128, 8, 128], BF16) for _ in range(2)]
        wg_c = [A.alloc([128, 8, 2, 128], BF16) for _ in range(2)]
        g1 = [A.alloc([128, 512], F32) for _ in range(2)]
        g2 = [A.alloc([128, 512], F32) for _ in range(2)]
        mchunks = [(t0, min(512, TTOT - t0)) for t0 in range(0, TTOT, 512)]
        nmc = 0
        ring = [A.alloc([128, 2, 128], F32) for _ in range(4)]
        nring = [0]

        def load_cast(dst, dst_tt, src):
            rb = ring[nring[0] % 4]
            nring[0] += 1
            S.dma('sp', lambda e: e.dma_start(out=rb[:, :, :], in_=src), rb.t, True)
            S.op('pool', lambda e: e.tensor_copy(out=dst, in_=rb[:, :, :]), [rb.t], [dst_tt])

        def merge_load(c):
            wb_ = c % 2
            cs = slice(c * 128, (c + 1) * 128)
            for k0 in range(0, 16, 2):
                load_cast(wrp_c[wb_][:, k0:k0 + 2, :], wrp_c[wb_].t, wrp_d[k0 * 128:(k0 + 2) * 128, cs].rearrange("(k p) c -> p k c", p=128))
            for k0 in range(0, 8, 2):
                load_cast(wdp_c[wb_][:, k0:k0 + 2, :], wdp_c[wb_].t, wdp_d[k0 * 128:(k0 + 2) * 128, cs].rearrange("(k p) c -> p k c", p=128))
                for gg in range(2):
                    gs = slice(C_GZ + gg * 1024 + c * 128, C_GZ + gg * 1024 + (c + 1) * 128)
                    load_cast(wg_c[wb_][:, k0:k0 + 2, gg, :], wg_c[wb_].t, w_in[k0 * 128:(k0 + 2) * 128, gs].rearrange("(k p) c -> p k c", p=128))

        merge_load(0)
        for c in range(8):
            wb_ = c % 2
            if c + 1 < 8:
                merge_load(c + 1)
            for (t0, w) in mchunks:
                bb = 4 * (nmc % 2)
                gb = nmc % 2
                nmc += 1
                for k in range(16):
                    S.op('pe', lambda e, k=k, t0=t0, w=w, bb=bb, wb_=wb_: e.matmul(pf(bb)[:, :w], wrp_c[wb_][:, k, :], o_retT[:, k, t0:t0 + w], start=(k == 0), stop=(k == 15)),
                         [wrp_c[wb_].t] + tts_for(o_retT, t0, w), [t_pb[bb]])
                for k in range(8):
                    S.op('pe', lambda e, k=k, t0=t0, w=w, bb=bb, wb_=wb_: e.matmul(pf(bb + 1)[:, :w], wdp_c[wb_][:, k, :], o_dsaT[:, k, t0:t0 + w], start=(k == 0), stop=(k == 7)),
                         [wdp_c[wb_].t] + tts_for(o_dsaT, t0, w), [t_pb[bb + 1]])
                for gg in range(2):
                    for k in range(8):
                        S.op('pe', lambda e, k=k, t0=t0, w=w, bb=bb, wb_=wb_, gg=gg: e.matmul(pf(bb + 2 + gg)[:, :w], wg_c[wb_][:, k, gg, :], xnT[:, k, t0:t0 + w],
                                                                                         start=(k == 0), stop=(k == 7)),
                             [wg_c[wb_].t] + tts_for(xnT, t0, w), [t_pb[bb + 2 + gg]])
                S.op('act', lambda e, w=w, bb=bb, gb=gb: e.activation(out=g1[gb][:, :w], in_=pf(bb + 2)[:, :w], func=AF.Sigmoid), [t_pb[bb + 2]], [g1[gb].t])
                S.op('act', lambda e, w=w, bb=bb, gb=gb: e.activation(out=g2[gb][:, :w], in_=pf(bb + 3)[:, :w], func=AF.Sigmoid), [t_pb[bb + 3]], [g2[gb].t])
                S.op('dve', lambda e, w=w, bb=bb, gb=gb: e.tensor_tensor(out=g1[gb][:, :w], in0=g1[gb][:, :w], in1=pf(bb)[:, :w], op=ALU.mult), [g1[gb].t, t_pb[bb]], [g1[gb].t])
                S.op('dve', lambda e, w=w, bb=bb, gb=gb: e.tensor_tensor(out=g2[gb][:, :w], in0=g2[gb][:, :w], in1=pf(bb + 1)[:, :w], op=ALU.mult), [g2[gb].t, t_pb[bb + 1]], [g2[gb].t])
                S.op('pool', lambda e, w=w, gb=gb, c=c, t0=t0: e.tensor_tensor(out=mT[:, c, t0:t0 + w], in0=g1[gb][:, :w], in1=g2[gb][:, :w], op=ALU.add),
                     [g1[gb].t, g2[gb].t], tts_for(mT, t0, w))
        if DEBUG:
            S.barrier()
            S.dma('sp', lambda e: e.dma_start(out=o_dbg, in_=mT.ap), mT.tts[0], False)
        A.release(xnT, o_dsaT, o_retT, wrp_c[0], wrp_c[1], wdp_c[0], wdp_c[1], wg_c[0], wg_c[1], g1[0], g1[1], g2[0], g2[1], ring[0], ring[1], ring[2], ring[3])

        wo = A.alloc([128, 8, D], BF16)
        for k in range(8):
            S.dma('pool', lambda e, k=k: e.dma_start(out=wo[:, k, :], in_=wo_d[k * 128:(k + 1) * 128, :]), wo.t, True)
        nfw = A.alloc([128, 8], F32)
        S.dma('sp', lambda e: e.dma_start(out=nfw[:, :], in_=nfw_d), nfw.t, True)
        yacc = A.alloc([128, NT, D], F32, ntt=NT)
        h2nT = A.alloc([128, 8, TTOT], BF16, ntt=NT)
        xt2 = [A.alloc([128, D], F32) for _ in range(2)]
        xsb2 = [A.alloc([128, D], BF16) for _ in range(2)]
        junk4 = A.alloc([128, D], F32)
        st4 = A.alloc([128, 2 * NT], F32, ntt=NT)
        for j in range(1, NT):
            c0 = TOK0[j]
            bi = j % 2
            src = xs[:, :] if j == 17 else xp[128 * (j - 1):128 * j, :]
            S.dma('sp', lambda e, bi=bi, src=src: e.dma_start(out=xt2[bi][:, :], in_=src), xt2[bi].t, True)
            for half in range(2):
                bank = (2 * j + half) % 4
                for c in range(8):
                    S.op('pe', lambda e, c=c, c0=c0, half=half, bank=bank: e.matmul(pf(bank)[:, :], mT[:, c, c0:c0 + 128], wo[:, c, half * 512:(half + 1) * 512],
                                                                                 start=(c == 0), stop=(c == 7)), [mT.tts[j], wo.t], [t_pb[bank]])
                S.op('dve', lambda e, j=j, half=half, bank=bank, bi=bi: e.tensor_tensor(out=yacc[:, j, half * 512:(half + 1) * 512], in0=pf(bank)[:, :],
                                                                                     in1=xt2[bi][:, half * 512:(half + 1) * 512], op=ALU.add),
                     [t_pb[bank], xt2[bi].t], [yacc.tts[j]])
            ss = st4[:, 2 * j:2 * j + 1]
            rs = st4[:, 2 * j + 1:2 * j + 2]
            tst = st4.tts[j]
            S.op('act', lambda e, j=j, ss=ss: e.activation(out=junk4[:, :], in_=yacc[:, j, :], func=AF.Square, accum_out=ss), [yacc.tts[j]], [junk4.t, tst])
            S.op('act', lambda e, ss=ss, rs=rs: e.activation(out=rs, in_=ss, func=AF.Sqrt, scale=1.0 / D, bias=EPS), [tst], [tst])
            S.op('dve', lambda e, rs=rs: e.reciprocal(out=rs, in_=rs), [tst], [tst])
            S.op('dve', lambda e, bi=bi, j=j, rs=rs: e.tensor_scalar(out=xsb2[bi][:, :], in0=yacc[:, j, :], scalar1=rs, scalar2=None, op0=ALU.mult),
                 [yacc.tts[j], tst], [xsb2[bi].t])
            pb = 6 + bi
            for k in range(8):
                S.op('pe', lambda e, bi=bi, k=k, pb=pb: e.transpose(pbf(pb)[:, k * 128:(k + 1) * 128], xsb2[bi][:, k * 128:(k + 1) * 128], ident[:, :]),
                     [xsb2[bi].t, ident.t], [t_pb[pb]])
            for k in range(8):
                if k % 2 == 0:
                    S.op('act', lambda e, pb=pb, k=k, c0=c0: e.activation(out=h2nT[:, k, c0:c0 + 128], in_=pbf(pb)[:, k * 128:(k + 1) * 128], func=AF.Copy, scale=nfw[:, k:k + 1]),
                         [t_pb[pb], nfw.t], [h2nT.tts[j]])
                else:
                    S.op('dve', lambda e, pb=pb, k=k, c0=c0: e.tensor_scalar(out=h2nT[:, k, c0:c0 + 128], in0=pbf(pb)[:, k * 128:(k + 1) * 128], scalar1=nfw[:, k:k + 1],
                                                                           scalar2=None, op0=ALU.mult), [t_pb[pb], nfw.t], [h2nT.tts[j]])
        A.release(mT, wo, nfw, xt2[0], xt2[1], xsb2[0], xsb2[1], junk4, st4)

        groups = [(0, 4), (4, 4), (8, 4), (12, 4), (16, 4), (20, 2)]
        wa = [A.alloc([128, 8, 512], BF16) for _ in range(2)]
        wb2 = [A.alloc([128, 8, 512], BF16) for _ in range(2)]
        wo2 = [A.alloc([128, 4, D], BF16) for _ in range(2)]
        uT = [A.alloc([128, 4, 512], BF16) for _ in range(2)]
        sa = [A.alloc([128, 512], F32) for _ in range(2)]
        tchunks = [(1, 4), (5, 4), (9, 4), (13, 4), (17, 1)]
        nu = 0
        nsa = 0
        for gi, (f0c, nf) in enumerate(groups):
            wb_ = gi % 2
            f0 = f0c * 128
            for k in range(8):
                S.dma('pool', lambda e, k=k, f0=f0, nf=nf, wb_=wb_: e.dma_start(out=wa[wb_][:, k, 0:nf * 128], in_=wfi_d[k * 128:(k + 1) * 128, f0:f0 + nf * 128]), wa[wb_].t, True)
                S.dma('pool', lambda e, k=k, f0=f0, nf=nf, wb_=wb_: e.dma_start(out=wb2[wb_][:, k, 0:nf * 128], in_=wfi_d[k * 128:(k + 1) * 128, DFF + f0:DFF + f0 + nf * 128]),
                      wb2[wb_].t, True)
            for fi in range(nf):
                S.dma('pool', lambda e, fi=fi, f0=f0, wb_=wb_: e.dma_start(out=wo2[wb_][:, fi, :], in_=wfo_d[f0 + fi * 128:f0 + (fi + 1) * 128, :]), wo2[wb_].t, True)
            for (j0, ntl) in tchunks:
                t0 = TOK0[j0]
                w = ntl * 128
                ub = uT[nu % 2]
                nu += 1
                rtt = [h2nT.tts[j] for j in range(j0, j0 + ntl)]
                for fi in range(nf):
                    sb_ = sa[nsa % 2]
                    ba = 2 * (nsa % 2)
                    nsa += 1
                    for k in range(8):
                        S.op('pe', lambda e, k=k, fi=fi, t0=t0, w=w, ba=ba, wb_=wb_: e.matmul(pf(ba)[:, :w], wa[wb_][:, k, fi * 128:(fi + 1) * 128], h2nT[:, k, t0:t0 + w],
                                                                                         start=(k == 0), stop=(k == 7)), [wa[wb_].t] + rtt, [t_pb[ba]])
                    for k in range(8):
                        S.op('pe', lambda e, k=k, fi=fi, t0=t0, w=w, ba=ba, wb_=wb_: e.matmul(pf(ba + 1)[:, :w], wb2[wb_][:, k, fi * 128:(fi + 1) * 128], h2nT[:, k, t0:t0 + w],
                                                                                         start=(k == 0), stop=(k == 7)), [wb2[wb_].t] + rtt, [t_pb[ba + 1]])
                    S.op('act', lambda e, w=w, ba=ba, sb_=sb_: e.activation(out=sb_[:, :w], in_=pf(ba)[:, :w], func=AF.Silu), [t_pb[ba]], [sb_.t])
                    S.op('dve', lambda e, w=w, ba=ba, sb_=sb_, ub=ub, fi=fi: e.tensor_tensor(out=ub[:, fi, :w], in0=sb_[:, :w], in1=pf(ba + 1)[:, :w], op=ALU.mult),
                         [sb_.t, t_pb[ba + 1]], [ub.t])
                for jj in range(ntl):
                    j = j0 + jj
                    for half in range(2):
                        bank = 4 + (2 * j + half) % 4
                        for fi in range(nf):
                            S.op('pe', lambda e, fi=fi, jj=jj, half=half, bank=bank, ub=ub, wb_=wb_, nf=nf: e.matmul(pf(bank)[:, :], ub[:, fi, jj * 128:(jj + 1) * 128],
                                                                                                           wo2[wb_][:, fi, half * 512:(half + 1) * 512],
                                                                                                           start=(fi == 0), stop=(fi == nf - 1)),
                                 [ub.t, wo2[wb_].t], [t_pb[bank]])
                        S.op('dve', lambda e, j=j, half=half, bank=bank: e.tensor_tensor(out=yacc[:, j, half * 512:(half + 1) * 512], in0=pf(bank)[:, :],
                                                                                      in1=yacc[:, j, half * 512:(half + 1) * 512], op=ALU.add),
                             [t_pb[bank], yacc.tts[j]], [yacc.tts[j]])
                    if gi == len(groups) - 1:
                        dst = o_ys[:, :] if j == 17 else o_yp[128 * (j - 1):128 * j, :]
                        S.dma('sp', lambda e, j=j, dst=dst: e.dma_start(out=dst, in_=yacc[:, j, :]), yacc.tts[j], False)

        S.emit()
    return nc


_NC_CACHE = {}


def _consts():
    c = {}
    c["ident"] = np.eye(128, dtype=np.float32)
    half = 128
    inv = np.power(np.float32(10000.0), -np.arange(half, dtype=np.float32) / np.float32(half)).astype(np.float32)
    rot = np.zeros((NT, 128, 256), np.float32)
    for j in range(NT):
        if j == 0:
            pos = np.arange(16)
        elif j == 17:
            pos = 2048 + (np.arange(128) % 8)
        else:
            pos = 16 + 128 * (j - 1) + np.arange(128)
        ang = pos.astype(np.float32)[:, None] * inv[None, :]
        rot[j, :len(pos), 0:128] = np.cos(ang)
        rot[j, :len(pos), 128:256] = np.sin(ang)
    c["rot"] = rot
    lg = np.log1p(-np.exp2(-5.0 - np.arange(4, dtype=np.float64)))
    i = np.arange(128)
    decT = np.zeros((128, 8, 128), np.float32)
    for h in range(4):
        diff = i[None, :] - i[:, None]
        decT[:, h, :] = np.where(diff >= 0, np.exp(lg[h] * np.maximum(diff, 0)), 0.0)
        same = (i[None, :] // 8) == (i[:, None] // 8)
        d8 = (i[None, :] % 8) - (i[:, None] % 8)
        decT[:, 4 + h, :] = np.where(same & (d8 >= 0), np.exp(lg[h] * np.maximum(d8, 0)), 0.0)
    c["decT"] = decT
    qdec = np.zeros((128, 8), np.float32)
    kdec = np.zeros((128, 12), np.float32)
    for h in range(4):
        qdec[:, h] = np.exp(lg[h] * (i + 1.0))
        qdec[:, 4 + h] = np.exp(lg[h] * ((i % 8) + 1.0))
        kdec[:, h] = np.exp(lg[h] * (127.0 - i)) / 16.0
        kdec[:16, 4 + h] = np.exp(lg[h] * (15.0 - i[:16])) / 16.0
        kdec[:, 8 + h] = np.exp(lg[h] * (7.0 - (i % 8))) / 16.0
    c["qdec"] = qdec
    c["kdec"] = kdec
    bm = np.zeros((128, 16, 128), np.float32)
    rm = np.zeros((128, 16), np.float32)
    for b in range(16):
        bm[:, b, 8 * b:8 * b + 8] = 1.0
        rm[8 * b:8 * b + 8, b] = 1.0
    c["bm"] = bm
    c["rm"] = rm
    c["cbias"] = np.where(i[None, :] <= i[:, None], 0.0, NEG).astype(np.float32)
    c["pow2"] = np.broadcast_to((0.5 ** (np.arange(NBIS + 1) + 1.0))[None, :], (128, NBIS + 1)).astype(np.float32).copy()
    bm2 = np.zeros((128, 8, 128), np.float32)
    for p in range(8):
        bm2[0:64, p, 8 * p:8 * p + 8] = 1.0
        bm2[64:128, p, 8 * (p + 8):8 * (p + 8) + 8] = 1.0
    c["bm2"] = bm2
    c["cbs"] = np.where(np.arange(8)[None, :] <= (i % 8)[:, None], 0.0, NEG).astype(np.float32)
    c["pidx"] = (np.arange(128) % 32).astype(np.float32)[:, None].copy()
    c["thrcap"] = np.where(i <= 111, -1.0e29, 1.0e30).astype(np.float32)[:, None].copy()
    return c


def kernel(x_prompt, x_sample, cache_k, cache_v, cache_kidx, state_ret, page_table,
           meta_tokens, norm_mix_w, w_in, w_ret_proj, dsa_q_norm_w, dsa_k_norm_w,
           idx_k_norm_w, idx_k_norm_b, w_dsa_proj, w_out, norm_ffn_w, w_ffn_in, w_ffn_out):
    f = lambda a: np.ascontiguousarray(np.asarray(a))
    x_prompt, x_sample, state_ret = f(x_prompt), f(x_sample), f(state_ret)
    ck = f(cache_k)[0].reshape(NPOOL * 32, 4 * 256)
    cv = f(cache_v)[0].reshape(NPOOL * 32, 4 * 256)
    cki = f(cache_kidx)[0].reshape(NPOOL * 32, 4 * 64)
    ptab = f(page_table).astype(np.int32)
    if 'nc' not in _NC_CACHE:
        _NC_CACHE['nc'] = build_program()
    nc = _NC_CACHE['nc']
    ncore = 8
    common = {
        "meta": f(meta_tokens),
        "w_in": f(np.asarray(w_in)[0]),
        "nmw": f(np.asarray(norm_mix_w)[0].reshape(8, 128).T),
        "wq_bc": f(np.broadcast_to(np.asarray(dsa_q_norm_w)[0][None, :], (128, 128))),
        "wk_bc": f(np.broadcast_to(np.asarray(dsa_k_norm_w)[0][None, :], (128, 128))),
        "ikw_bc": f(np.broadcast_to(np.asarray(idx_k_norm_w)[0][None, :], (128, 64))),
        "ikb_bc": f(np.broadcast_to(np.asarray(idx_k_norm_b)[0][None, :], (128, 64))),
        "w_ret_proj": f(np.asarray(w_ret_proj)[0]),
        "w_dsa_proj": f(np.asarray(w_dsa_proj)[0]),
        "w_out": f(np.asarray(w_out)[0]),
        "w_ffn_in": f(np.asarray(w_ffn_in)[0]),
        "w_ffn_out": f(np.asarray(w_ffn_out)[0]),
        "nfw": f(np.asarray(norm_ffn_w)[0].reshape(8, 128).T),
    }
    common.update(_consts())
    in_maps = []
    for c in range(ncore):
        m = dict(common)
        m["xp"] = x_prompt[c]
        m["xs"] = f(x_sample[16 * c:16 * c + 16].reshape(128, D))
        m["state"] = state_ret[0, 16 * c:16 * c + 16]
        ptc = ptab[16 * c:16 * c + 16]
        m["pt2"] = f(np.repeat(ptc.reshape(16, 4, 4).transpose(2, 0, 1).reshape(4, 64), 32, axis=0))
        m["ck"], m["cv"], m["cki"] = ck, cv, cki
        in_maps.append(m)
    res = run_bass_kernel_spmd(nc, in_maps, core_ids=list(range(ncore)))
    R = res.results
    if DEBUG:
        _NC_CACHE['dbg'] = R[0]["o_dbg"]
    y_prompt = np.stack([R[c]["o_yp"] for c in range(ncore)]).reshape(8, 2048, D)
    y_sample = np.concatenate([R[c]["o_ys"].reshape(16, 8, D) for c in range(ncore)], 0)
    k_prompt = np.stack([R[c]["o_kp"].reshape(2064, 2, 128) for c in range(ncore)])[None]
    v_prompt = np.stack([R[c]["o_vp"].reshape(2064, 2, 128) for c in range(ncore)])[None]
    kidx_prompt = np.stack([R[c]["o_kip"] for c in range(ncore)])[None]
    ret_prompt = np.stack([R[c]["o_rp"] for c in range(ncore)])[None]
    k_sample = np.concatenate([R[c]["o_ks"].reshape(16, 8, 2, 128) for c in range(ncore)], 0)[None]
    v_sample = np.concatenate([R[c]["o_vs"].reshape(16, 8, 2, 128) for c in range(ncore)], 0)[None]
    kidx_sample = np.concatenate([R[c]["o_kis"].reshape(16, 8, 64) for c in range(ncore)], 0)[None]
    ret_sample = np.concatenate([R[c]["o_rs"] for c in range(ncore)], 0)[None]
    outs = (y_prompt, y_sample, k_prompt, v_prompt, kidx_prompt, ret_prompt, k_sample, v_sample, kidx_sample, ret_sample)
    return tuple(np.ascontiguousarray(o, dtype=np.float32) for o in outs)
```

```python
import contextlib
import numpy as np
import concourse.bass as bass
import concourse.mybir as mybir
from concourse.bass_utils import run_bass_kernel_spmd

F32 = mybir.dt.float32
BF16 = mybir.dt.bfloat16
I32 = mybir.dt.int32
AF = mybir.ActivationFunctionType
ALU = mybir.AluOpType
AX = mybir.AxisListType

DEBUG = False

D = 1024
NT = 18
ROWS = [16] + [128] * 17
TOK0 = [0] + [16 + 128 * i for i in range(17)]
TTOT = 2192
EPS = 1e-6
IN_COLS = 10312
C_RQ, C_RK, C_RV, C_RG = 0, 1024, 2048, 4096
C_DSA = 6144
N_DSA = 2120
C_GZ = 8264
DFF = 2816
IDX_SCALE = (8 ** -0.5) * (64 ** -0.5)
NBIS = 16
NEG = -1.0e30
GAM = [float(np.exp(np.log1p(-np.exp2(-5.0 - h)))) for h in range(4)]
ARENA_WORDS = 53200
NPOOL = 2560


class TT:
    def __init__(self, init=None):
        self.w = None
        self.r = list(init) if init else []
        self.dsem = None
        self.dcnt = 0


class Sched:
    ENG = ('pe', 'act', 'dve', 'pool', 'sp')

    def __init__(self, nc):
        self.nc = nc
        self.ops = {e: [] for e in self.ENG}
        self.cnt = {e: 0 for e in self.ENG}
        self.seen = {e: {} for e in self.ENG}
        self.dsems = []
        self.final = {}

    def _deps(self, eng, reads, writes):
        deps = {}

        def add(ev):
            if ev is None:
                return
            k, v = ev
            if k == eng and eng == 'pe':
                return
            if self.seen[eng].get(k, 0) >= v:
                return
            if deps.get(k, 0) < v:
                deps[k] = v
        for t in reads:
            add(t.w)
        for t in writes:
            add(t.w)
            for ev in t.r:
                add(ev)
        for k, v in deps.items():
            self.seen[eng][k] = v
        return list(deps.items())

    @staticmethod
    def _compact(evs):
        d = {}
        for k, v in evs:
            if d.get(k, 0) < v:
                d[k] = v
        return list(d.items())

    def op(self, eng, fn, reads=(), writes=()):
        waits = self._deps(eng, reads, writes)
        self.cnt[eng] += 1
        ev = (eng, self.cnt[eng])
        for t in reads:
            t.r.append(ev)
            if len(t.r) > 48:
                t.r = self._compact(t.r)
        for t in writes:
            t.w = ev
            t.r = []
        self.ops[eng].append((waits, fn, (eng, 1)))

    def dma(self, q, fn, tile, load, extra_reads=()):
        if load:
            waits = self._deps(q, extra_reads, (tile,))
        else:
            waits = self._deps(q, (tile,) + tuple(extra_reads), ())
        if tile.dsem is None:
            tile.dsem = 'd%d' % len(self.dsems)
            self.dsems.append(tile)
        tile.dcnt += 16
        ev = (tile.dsem, tile.dcnt)
        if load:
            tile.w = ev
            tile.r = []
            for t in extra_reads:
                t.r.append(ev)
        else:
            tile.r.append(ev)
            if len(tile.r) > 48:
                tile.r = self._compact(tile.r)
            self.final[tile.dsem] = tile.dcnt
        self.ops[q].append((waits, fn, (tile.dsem, 16)))

    def barrier(self):
        for e in self.ENG:
            waits = []
            for o in self.ENG:
                if o == e:
                    continue
                v = self.cnt[o]
                if v > self.seen[e].get(o, 0):
                    self.seen[e][o] = v
                    waits.append((o, v))
            for t in self.dsems:
                if t.dcnt > self.seen[e].get(t.dsem, 0):
                    self.seen[e][t.dsem] = t.dcnt
                    waits.append((t.dsem, t.dcnt))
            self.ops[e].append((waits, None, None))

    def emit(self):
        nc = self.nc
        with contextlib.ExitStack() as st:
            sems = {}
            for e in self.ENG:
                sems[e] = st.enter_context(nc.semaphore('s_' + e))
            for t in self.dsems:
                sems[t.dsem] = st.enter_context(nc.semaphore('s_' + t.dsem))
            block = st.enter_context(nc.Block())

            def run(engname, eng):
                for waits, fn, inc in self.ops[engname]:
                    for k, v in waits:
                        eng.wait_ge(sems[k], v)
                    if fn is None:
                        continue
                    ins = fn(eng)
                    ins.then_inc(sems[inc[0]], inc[1])
                if engname == 'sp':
                    for k, v in self.final.items():
                        eng.wait_ge(sems[k], v)

            @block.tensor
            def _(e):
                run('pe', e)

            @block.scalar
            def _(e):
                run('act', e)

            @block.vector
            def _(e):
                run('dve', e)

            @block.gpsimd
            def _(e):
                run('pool', e)

            @block.sync
            def _(e):
                run('sp', e)


class Buf:
    def __init__(self, ap, off, words, tts):
        self.ap = ap
        self.off = off
        self.words = words
        self.tts = tts
        self.t = tts[0]

    def __getitem__(self, key):
        return self.ap[key]


class Arena:
    def __init__(self, base, nwords):
        self.base = base
        self.free = [(0, nwords)]
        self.dead = {}

    def alloc(self, shape, dt, ntt=1):
        n = 1
        for s in shape[1:]:
            n *= s
        esz = 2 if dt == BF16 else 4
        words = (n * esz + 3) // 4
        words = (words + 15) // 16 * 16
        for i, (o, w) in enumerate(self.free):
            if w >= words:
                off = o
                if w == words:
                    self.free.pop(i)
                else:
                    self.free[i] = (o + words, w - words)
                break
        else:
            raise RuntimeError("arena out of SBUF: need %d words, free=%s" % (words, self.free))
        v = self.base[:, off:off + words]
        if dt != F32:
            v = v.bitcast(dt)
        v = v[:, 0:n]
        nd = len(shape) - 1
        if nd == 2:
            v = v.rearrange("p (a b) -> p a b", a=shape[1])
        elif nd == 3:
            v = v.rearrange("p (a b c) -> p a b c", a=shape[1], b=shape[2])
        v = v[:shape[0]]
        init = list(self.dead.items())
        return Buf(v, off, words, [TT(init) for _ in range(ntt)])

    def release(self, *bufs):
        for b in bufs:
            for t in b.tts:
                evs = list(t.r)
                if t.w is not None:
                    evs.append(t.w)
                for k, v in evs:
                    if self.dead.get(k, 0) < v:
                        self.dead[k] = v
            self.free.append((b.off, b.words))
        self.free.sort()
        merged = []
        for o, w in self.free:
            if merged and merged[-1][0] + merged[-1][1] == o:
                merged[-1] = (merged[-1][0], merged[-1][1] + w)
            else:
                merged.append((o, w))
        self.free = merged


def build_program():
    nc = bass.Bass("TRN2", target_bir_lowering=False)

    def din(name, shape, dt=F32):
        return nc.dram_tensor(name, list(shape), dt, kind="ExternalInput").ap()

    def dout(name, shape, dt=F32):
        return nc.dram_tensor(name, list(shape), dt, kind="ExternalOutput").ap()

    xp = din("xp", [2048, D])
    xs = din("xs", [128, D])
    meta = din("meta", [16, D])
    w_in = din("w_in", [D, IN_COLS])
    state = din("state", [16, 4, 256, 512])
    nmw = din("nmw", [128, 8])
    wq_bc = din("wq_bc", [128, 128])
    wk_bc = din("wk_bc", [128, 128])
    ikw_bc = din("ikw_bc", [128, 64])
    ikb_bc = din("ikb_bc", [128, 64])
    ident_d = din("ident", [128, 128])
    rot_d = din("rot", [NT, 128, 256])
    decT_d = din("decT", [128, 8, 128])
    qdec_d = din("qdec", [128, 8])
    kdec_d = din("kdec", [128, 12])
    bm_d = din("bm", [128, 16, 128])
    rm_d = din("rm", [128, 16])
    cbias_d = din("cbias", [128, 128])
    pow2_d = din("pow2", [128, NBIS + 1])
    thrcap_d = din("thrcap", [128, 1])
    pidx_d = din("pidx", [128, 1])
    bm2_d = din("bm2", [128, 8, 128])
    cbs_d = din("cbs", [128, 8])
    wrp_d = din("w_ret_proj", [2048, D])
    wdp_d = din("w_dsa_proj", [D, D])
    wo_d = din("w_out", [D, D])
    wfi_d = din("w_ffn_in", [D, 2 * DFF])
    wfo_d = din("w_ffn_out", [DFF, D])
    nfw_d = din("nfw", [128, 8])
    pt_d = din("pt2", [128, 64], I32)
    ck_d = din("ck", [NPOOL * 32, 4 * 256])
    cv_d = din("cv", [NPOOL * 32, 4 * 256])
    cki_d = din("cki", [NPOOL * 32, 4 * 64])

    o_yp = dout("o_yp", [2048, D])
    o_ys = dout("o_ys", [128, D])
    o_kp = dout("o_kp", [2064, 256])
    o_vp = dout("o_vp", [2064, 256])
    o_kip = dout("o_kip", [2064, 64])
    o_rp = dout("o_rp", [4, 256, 512])
    o_ks = dout("o_ks", [128, 256])
    o_vs = dout("o_vs", [128, 256])
    o_kis = dout("o_kis", [128, 64])
    o_rs = dout("o_rs", [16, 4, 256, 512])
    if DEBUG:
        o_dbg = dout("o_dbg", [128, 8, TTOT], BF16)

    S = Sched(nc)

    def pos0(j):
        return 0 if j == 0 else 16 + 128 * (j - 1)

    with contextlib.ExitStack() as top:
        arena_t = top.enter_context(nc.sbuf_tensor("arena", [128, ARENA_WORDS], F32))
        A = Arena(arena_t, ARENA_WORDS)
        pbank = [top.enter_context(nc.psum_tensor("pb%d" % i, [128, 512], F32)) for i in range(8)]
        t_pb = [TT() for _ in range(8)]

        def pf(i):
            return pbank[i]

        def pbf(i):
            return pbank[i][:, :].bitcast(BF16)

        ident = A.alloc([128, 128], BF16)
        nmw_sb = A.alloc([128, 8], F32)
        wq_sb = A.alloc([128, 128], F32)
        wk_sb = A.alloc([128, 128], F32)
        ikw_sb = A.alloc([128, 64], F32)
        ikb_sb = A.alloc([128, 64], F32)
        t_const = TT()
        S.dma('pool', lambda e: e.dma_start(out=ident[:, :], in_=ident_d[:, :]), ident.t, True)
        for dst, src in ((nmw_sb, nmw), (wq_sb, wq_bc), (wk_sb, wk_bc), (ikw_sb, ikw_bc), (ikb_sb, ikb_bc)):
            S.dma('sp', lambda e, dst=dst, src=src: e.dma_start(out=dst[:, :], in_=src[:, :]), t_const, True)

        xnT = A.alloc([128, 8, TTOT], BF16, ntt=NT)

        xt = [A.alloc([128, D], F32) for _ in range(2)]
        xsb = [A.alloc([128, D], BF16) for _ in range(2)]
        junk = A.alloc([128, D], F32)
        st1 = A.alloc([128, 2 * NT], F32, ntt=NT)
        for j in range(NT):
            r = ROWS[j]
            c0 = TOK0[j]
            bi = j % 2
            if j == 0:
                src = meta[:, :]
            elif j == 17:
                src = xs[:, :]
            else:
                src = xp[128 * (j - 1):128 * j, :]
            S.dma('sp', lambda e, bi=bi, r=r, src=src: e.dma_start(out=xt[bi][:r, :], in_=src), xt[bi].t, True)
            ss = st1[:r, 2 * j:2 * j + 1]
            rs = st1[:r, 2 * j + 1:2 * j + 2]
            tst = st1.tts[j]
            S.op('act', lambda e, bi=bi, r=r, ss=ss: e.activation(out=junk[:r, :], in_=xt[bi][:r, :], func=AF.Square, accum_out=ss),
                 [xt[bi].t], [junk.t, tst])
            S.op('act', lambda e, ss=ss, rs=rs: e.activation(out=rs, in_=ss, func=AF.Sqrt, scale=1.0 / D, bias=EPS), [tst], [tst])
            S.op('dve', lambda e, rs=rs: e.reciprocal(out=rs, in_=rs), [tst], [tst])
            S.op('dve', lambda e, bi=bi, r=r, rs=rs: e.tensor_scalar(out=xsb[bi][:r, :], in0=xt[bi][:r, :], scalar1=rs, scalar2=None, op0=ALU.mult),
                 [xt[bi].t, tst], [xsb[bi].t])
            pb = 6 + bi
            for k in range(8):
                S.op('pe', lambda e, bi=bi, r=r, k=k, pb=pb: e.transpose(pbf(pb)[:, k * 128:k * 128 + r], xsb[bi][:r, k * 128:(k + 1) * 128], ident[:r, :r]),
                     [xsb[bi].t, ident.t], [t_pb[pb]])
            for k in range(8):
                if k % 2 == 0:
                    S.op('act', lambda e, pb=pb, r=r, k=k, c0=c0: e.activation(out=xnT[:, k, c0:c0 + r], in_=pbf(pb)[:, k * 128:k * 128 + r],
                                                                             func=AF.Copy, scale=nmw_sb[:, k:k + 1]),
                         [t_pb[pb], t_const], [xnT.tts[j]])
                else:
                    S.op('dve', lambda e, pb=pb, r=r, k=k, c0=c0: e.tensor_scalar(out=xnT[:, k, c0:c0 + r], in0=pbf(pb)[:, k * 128:k * 128 + r],
                                                                                scalar1=nmw_sb[:, k:k + 1], scalar2=None, op0=ALU.mult),
                         [t_pb[pb], t_const], [xnT.tts[j]])
        A.release(xt[0], xt[1], xsb[0], xsb[1], junk, st1)

        o_dsaT = A.alloc([128, 8, TTOT], BF16, ntt=NT)
        wd = A.alloc([128, 8, N_DSA], BF16)
        for k in range(8):
            S.dma('pool', lambda e, k=k: e.dma_start(out=wd[:, k, :], in_=w_in[k * 128:(k + 1) * 128, C_DSA:C_DSA + N_DSA]), wd.t, True)
        st2 = A.alloc([128, 16], F32)
        junk2 = A.alloc([128, 128], F32)
        q_bf = A.alloc([128, 1024], BF16)
        k_f = A.alloc([128, 256], F32)
        k_bf = A.alloc([128, 256], BF16)
        v_f = A.alloc([128, 256], F32)
        iq_bf = A.alloc([128, 512], BF16)
        ki_f = A.alloc([128, 64], F32)
        ki_t = A.alloc([128, 64], F32)
        ik2_bf = A.alloc([128, 128], BF16)
        widx = A.alloc([128, NT, 8], F32, ntt=NT)
        V_bf = A.alloc([128, NT, 256], BF16, ntt=NT)
        KT = A.alloc([128, 2, 2064], BF16)
        ikT2 = A.alloc([128, 2064], BF16)
        QT = A.alloc([128, 8, 128], BF16)
        iqT = A.alloc([128, 4, 128], BF16)
        acc = A.alloc([128, 2064], F32)
        mask_bf = A.alloc([128, 2064], BF16)
        junk_bf = A.alloc([128, 2064], BF16)
        maskT = A.alloc([128, 17, 128], BF16)
        rl = [A.alloc([128, 2, 512], F32) for _ in range(2)]
        PT = [A.alloc([128, 4, 128], BF16) for _ in range(4)]
        rec = A.alloc([128, 512], F32)
        bs = A.alloc([128, 10 + NBIS], F32)
        tmpd = A.alloc([128, 128], F32)
        ones_bf = A.alloc([128, 128], BF16)
        cbias = A.alloc([128, 128], F32)
        pow2 = A.alloc([128, NBIS + 1], F32)
        thrcap = A.alloc([128, 1], F32)
        t_c2 = TT(list(A.dead.items()))
        for dst, src in ((cbias, cbias_d), (pow2, pow2_d), (thrcap, thrcap_d)):
            S.dma('sp', lambda e, dst=dst, src=src: e.dma_start(out=dst.ap, in_=src), t_c2, True)
        S.op('pool', lambda e: e.memset(ones_bf[:, :], 1.0), [], [ones_bf.t])
        att_cnt = [0, 0]
        QT_s = A.alloc([128, 8, 128], BF16)
        iqT8 = A.alloc([128, 8, 128], BF16)
        iq_dup = A.alloc([128, 8, 128], BF16)
        KT_s = A.alloc([128, 2, 128], BF16)
        ikT_s = A.alloc([128, 128], BF16)

        def idx_scores_prompt(r, L, iqT_ap, t_iqT, w_ap, t_w):
            for cc0 in range(0, L, 512):
                w = min(512, L - cc0)
                for hp in range(4):
                    rb = rl[att_cnt[0] % 2]
                    bo = 2 * (att_cnt[0] % 2)
                    att_cnt[0] += 1
                    for e_ in range(2):
                        S.op('pe', lambda e, e_=e_, hp=hp, w=w, cc0=cc0, bo=bo: e.matmul(pf(bo + e_)[:r, :w], iqT_ap[e_ * 64:(e_ + 1) * 64, hp, :], ikT2[e_ * 64:(e_ + 1) * 64, cc0:cc0 + w],
                                                                                        start=True, stop=True), [t_iqT, ikT2.t], [t_pb[bo + e_]])
                        S.op('act', lambda e, e_=e_, w=w, rb=rb, bo=bo: e.activation(out=rb[:r, e_, :w], in_=pf(bo + e_)[:r, :w], func=AF.Relu), [t_pb[bo + e_]], [rb.t])
                    for e_ in range(2):
                        h = 2 * hp + e_
                        if h == 0:
                            S.op('dve', lambda e, w=w, cc0=cc0, rb=rb: e.tensor_scalar(out=acc[:r, cc0:cc0 + w], in0=rb[:r, 0, :w], scalar1=w_ap(0), scalar2=None, op0=ALU.mult),
                                 [rb.t, t_w], [acc.t])
                        else:
                            S.op('dve', lambda e, w=w, cc0=cc0, rb=rb, e_=e_, h=h: e.scalar_tensor_tensor(out=acc[:r, cc0:cc0 + w], in0=rb[:r, e_, :w], scalar=w_ap(h),
                                                                                                        in1=acc[:r, cc0:cc0 + w], op0=ALU.mult, op1=ALU.add),
                                 [rb.t, t_w, acc.t], [acc.t])

        def thresh_mask(r, L, ktiles, diag0, dw, cb_ap, need_bis, cap, diag_min):
            S.op('dve', lambda e: e.tensor_tensor(out=acc[:r, diag0:diag0 + dw], in0=acc[:r, diag0:diag0 + dw], in1=cb_ap, op=ALU.add), [acc.t, t_c2], [acc.t])
            if not need_bis:
                S.op('dve', lambda e: e.memset(bs[:r, 6:7], -1.0e29), [], [bs.t])
            else:
                S.op('dve', lambda e: e.tensor_reduce(out=bs[:r, 0:1], in_=acc[:r, :L], axis=AX.X, op=ALU.max), [acc.t], [bs.t])
                S.op('dve', lambda e: e.tensor_reduce(out=bs[:r, 2:3], in_=acc[:r, :diag0], axis=AX.X, op=ALU.min), [acc.t], [bs.t])
                if diag_min:
                    S.op('dve', lambda e: e.scalar_tensor_tensor(out=tmpd[:r, :dw], in0=cb_ap, scalar=-2.0, in1=acc[:r, diag0:diag0 + dw], op0=ALU.mult, op1=ALU.add),
                         [acc.t, t_c2], [tmpd.t])
                    S.op('dve', lambda e: e.tensor_reduce(out=bs[:r, 1:2], in_=tmpd[:r, :dw], axis=AX.X, op=ALU.min), [tmpd.t], [bs.t])
                    S.op('dve', lambda e: e.tensor_tensor(out=bs[:r, 2:3], in0=bs[:r, 1:2], in1=bs[:r, 2:3], op=ALU.min), [bs.t], [bs.t])
                S.op('dve', lambda e: e.tensor_tensor(out=bs[:r, 7:8], in0=bs[:r, 0:1], in1=bs[:r, 2:3], op=ALU.subtract), [bs.t], [bs.t])
                S.op('dve', lambda e: e.tensor_scalar(out=bs[:r, 8:9 + NBIS], in0=pow2[:r, :], scalar1=bs[:r, 7:8], scalar2=None, op0=ALU.mult), [bs.t, t_c2], [bs.t])
                S.op('dve', lambda e: e.tensor_tensor(out=bs[:r, 3:4], in0=bs[:r, 2:3], in1=bs[:r, 8:9], op=ALU.add), [bs.t], [bs.t])
                for k in range(NBIS):
                    S.op('dve', lambda e: e.tensor_scalar(out=junk_bf[:r, :L], in0=acc[:r, :L], scalar1=bs[:r, 3:4], scalar2=None, op0=ALU.is_ge, op1=ALU.add,
                                                          accum_out=bs[:r, 4:5]), [acc.t, bs.t], [junk_bf.t, bs.t])
                    S.op('dve', lambda e, k=k: e.tensor_scalar(out=bs[:r, 5:6], in0=bs[:r, 4:5], scalar1=255.5, scalar2=bs[:r, 8 + k:9 + k], op0=ALU.is_ge, op1=ALU.mult),
                         [bs.t], [bs.t])
                    S.op('dve', lambda e, k=k: e.scalar_tensor_tensor(out=bs[:r, 3:4], in0=bs[:r, 3:4], scalar=bs[:r, 9 + k:10 + k], in1=bs[:r, 5:6], op0=ALU.subtract, op1=ALU.add),
                         [bs.t], [bs.t])
                if cap:
                    S.op('dve', lambda e: e.tensor_tensor(out=bs[:r, 2:3], in0=bs[:r, 3:4], in1=bs[:r, 8 + NBIS:9 + NBIS], op=ALU.subtract), [bs.t], [bs.t])
                    S.op('dve', lambda e: e.tensor_tensor(out=bs[:r, 6:7], in0=bs[:r, 2:3], in1=thrcap[:r, 0:1], op=ALU.min), [bs.t, t_c2], [bs.t])
                else:
                    S.op('dve', lambda e: e.tensor_tensor(out=bs[:r, 6:7], in0=bs[:r, 3:4], in1=bs[:r, 8 + NBIS:9 + NBIS], op=ALU.subtract), [bs.t], [bs.t])
            S.op('dve', lambda e: e.tensor_scalar(out=mask_bf[:r, :L], in0=acc[:r, :L], scalar1=bs[:r, 6:7], scalar2=None, op0=ALU.is_ge), [acc.t, bs.t], [mask_bf.t])
            for g0 in range(0, len(ktiles), 8):
                grp = ktiles[g0:g0 + 8]
                pb = 6 + (g0 // 8) % 2
                for s_, (kp0, kr, vi) in enumerate(grp):
                    S.op('pe', lambda e, s_=s_, kp0=kp0, kr=kr, pb=pb: e.transpose(pbf(pb)[:kr, s_ * 128:s_ * 128 + r], mask_bf[:r, kp0:kp0 + kr], ident[:r, :r]),
                         [mask_bf.t, ident.t], [t_pb[pb]])
                n_ = len(grp)
                S.op('act', lambda e, g0=g0, n_=n_, pb=pb: e.activation(out=maskT[:, g0:g0 + n_, :r], in_=pbf(pb)[:, 0:n_ * 128].rearrange("p (s q) -> p s q", s=n_)[:, :, :r],
                                                                       func=AF.Copy), [t_pb[pb]], [maskT.t])

        def attn_prompt(r, ktiles, qT_ap, t_qT, out_ap_fn, t_out):
            nk = len(ktiles)
            units = [(n, ii) + ktiles[ii] for n in range(2) for ii in range(nk)]
            LOOK = 3
            bufs = {}

            def emit_st(ui):
                n, ii, kp0, kr, vi = units[ui]
                sbk = att_cnt[1] % 4
                pt = PT[att_cnt[1] % 4]
                att_cnt[1] += 1
                bufs[ui] = pt
                S.op('pe', lambda e: e.matmul(pf(sbk)[:kr, 0:4 * r].rearrange("p (h q) -> p h q", h=4), KT[:, n, kp0:kp0 + kr],
                                              qT_ap[:, 4 * n:4 * n + 4, :], start=True, stop=True), [KT.t, t_qT], [t_pb[sbk]])
                S.op('act', lambda e: e.activation(out=pt[:kr, :, :r], in_=pf(sbk)[:kr, 0:4 * r].rearrange("p (h q) -> p h q", h=4),
                                                   func=AF.Exp, scale=128 ** -0.5), [t_pb[sbk]], [pt.t])
                meng = 'pool' if ui % 3 == 2 else 'dve'
                S.op(meng, lambda e: e.tensor_tensor(out=pt[:kr, :, :r], in0=pt[:kr, :, :r],
                                                     in1=maskT[:kr, ii:ii + 1, :r].to_broadcast([kr, 4, r]), op=ALU.mult), [pt.t, maskT.t], [pt.t])

            def emit_pv(ui):
                n, ii, kp0, kr, vi = units[ui]
                pt = bufs.pop(ui)
                S.op('pe', lambda e: e.matmul(pf(4)[:, 0:4 * r].rearrange("p (h q) -> p h q", h=4), V_bf[:kr, vi, n * 128:(n + 1) * 128],
                                              pt[:kr, :, :r], start=(ii == 0), stop=(ii == nk - 1)), [V_bf.tts[vi], pt.t], [t_pb[4]])
                S.op('pe', lambda e: e.matmul(pf(5)[:, 0:4 * r].rearrange("p (h q) -> p h q", h=4), ones_bf[:kr, :],
                                              pt[:kr, :, :r], start=(ii == 0), stop=(ii == nk - 1)), [ones_bf.t, pt.t], [t_pb[5]])
                if ii == nk - 1:
                    S.op('dve', lambda e: e.reciprocal(out=rec[:, 0:4 * r], in_=pf(5)[:, 0:4 * r]), [t_pb[5]], [rec.t])
                    S.op('dve', lambda e: e.tensor_tensor(out=out_ap_fn(n), in0=pf(4)[:, 0:4 * r].rearrange("p (h q) -> p h q", h=4),
                                                          in1=rec[:, 0:4 * r].rearrange("p (h q) -> p h q", h=4), op=ALU.mult), [t_pb[4], rec.t], [t_out])

            for i in range(len(units) + LOOK):
                if i < len(units):
                    emit_st(i)
                if i - LOOK >= 0:
                    emit_pv(i - LOOK)

        chunks = [(0, 512), (512, 512), (1024, 512), (1536, 512), (2048, 72)]
        for j in range(NT):
            r = ROWS[j]
            c0 = TOK0[j]
            for c, (cc0, cw) in enumerate(chunks):
                for k in range(8):
                    S.op('pe', lambda e, c=c, cc0=cc0, cw=cw, k=k, c0=c0, r=r: e.matmul(pf(c)[:r, :cw], xnT[:, k, c0:c0 + r], wd[:, k, cc0:cc0 + cw],
                                                                                     start=(k == 0), stop=(k == 7)),
                         [xnT.tts[j], wd.t], [t_pb[c]])
            for h in range(10):
                if h < 8:
                    src = pf(h // 4)[:r, (h % 4) * 128:(h % 4) * 128 + 128]
                    tp = t_pb[h // 4]
                else:
                    src = pf(2)[:r, (h - 8) * 128:(h - 8) * 128 + 128]
                    tp = t_pb[2]
                S.op('act', lambda e, src=src, r=r, h=h: e.activation(out=junk2[:r, :], in_=src, func=AF.Square, accum_out=st2[:r, h:h + 1]),
                     [tp], [junk2.t, st2.t])
            S.op('act', lambda e, r=r: e.activation(out=st2[:r, 0:10], in_=st2[:r, 0:10], func=AF.Sqrt, scale=1.0 / 128, bias=EPS), [st2.t], [st2.t])
            S.op('dve', lambda e, r=r: e.reciprocal(out=st2[:r, 0:10], in_=st2[:r, 0:10]), [st2.t], [st2.t])
            for h in range(8):
                src = pf(h // 4)[:r, (h % 4) * 128:(h % 4) * 128 + 128]
                S.op('dve', lambda e, src=src, r=r, h=h: e.scalar_tensor_tensor(out=q_bf[:r, h * 128:(h + 1) * 128], in0=src, scalar=st2[:r, h:h + 1],
                                                                              in1=wq_sb[:r, :], op0=ALU.mult, op1=ALU.mult),
                     [t_pb[h // 4], st2.t, t_const], [q_bf.t])
            for n in range(2):
                src = pf(2)[:r, n * 128:(n + 1) * 128]
                S.op('dve', lambda e, src=src, r=r, n=n: e.scalar_tensor_tensor(out=k_f[:r, n * 128:(n + 1) * 128], in0=src, scalar=st2[:r, 8 + n:9 + n],
                                                                              in1=wk_sb[:r, :], op0=ALU.mult, op1=ALU.mult),
                     [t_pb[2], st2.t, t_const], [k_f.t])
            S.op('act', lambda e, r=r: e.activation(out=k_bf[:r, :], in_=k_f[:r, :], func=AF.Copy), [k_f.t], [k_bf.t])
            S.op('act', lambda e, r=r: e.activation(out=v_f[:r, :], in_=pf(2)[:r, 256:512], func=AF.Copy), [t_pb[2]], [v_f.t])
            S.op('dve', lambda e, r=r, j=j: e.tensor_copy(out=V_bf[:r, j, :], in_=pf(2)[:r, 256:512]), [t_pb[2]], [V_bf.tts[j]])
            S.op('act', lambda e, r=r: e.activation(out=iq_bf[:r, :], in_=pf(3)[:r, :], func=AF.Copy), [t_pb[3]], [iq_bf.t])
            S.op('act', lambda e, r=r: e.activation(out=junk2[:r, 0:64], in_=pf(4)[:r, 0:64], func=AF.Copy, accum_out=st2[:r, 10:11]),
                 [t_pb[4]], [junk2.t, st2.t])
            S.op('act', lambda e, r=r: e.activation(out=junk2[:r, 0:64], in_=pf(4)[:r, 0:64], func=AF.Square, accum_out=st2[:r, 11:12]),
                 [t_pb[4]], [junk2.t, st2.t])
            S.op('dve', lambda e, r=r: e.tensor_scalar(out=st2[:r, 10:12], in0=st2[:r, 10:12], scalar1=1.0 / 64, scalar2=None, op0=ALU.mult), [st2.t], [st2.t])
            S.op('dve', lambda e, r=r: e.tensor_tensor(out=st2[:r, 12:13], in0=st2[:r, 10:11], in1=st2[:r, 10:11], op=ALU.mult), [st2.t], [st2.t])
            S.op('dve', lambda e, r=r: e.tensor_tensor(out=st2[:r, 12:13], in0=st2[:r, 11:12], in1=st2[:r, 12:13], op=ALU.subtract), [st2.t], [st2.t])
            S.op('act', lambda e, r=r: e.activation(out=st2[:r, 12:13], in_=st2[:r, 12:13], func=AF.Sqrt, scale=1.0, bias=EPS), [st2.t], [st2.t])
            S.op('dve', lambda e, r=r: e.reciprocal(out=st2[:r, 12:13], in_=st2[:r, 12:13]), [st2.t], [st2.t])
            S.op('dve', lambda e, r=r: e.tensor_scalar(out=ki_t[:r, :], in0=pf(4)[:r, 0:64], scalar1=st2[:r, 10:11], scalar2=st2[:r, 12:13],
                                                       op0=ALU.subtract, op1=ALU.mult), [t_pb[4], st2.t], [ki_t.t])
            S.op('dve', lambda e, r=r: e.tensor_tensor(out=ki_t[:r, :], in0=ki_t[:r, :], in1=ikw_sb[:r, :], op=ALU.mult), [ki_t.t, t_const], [ki_t.t])
            S.op('dve', lambda e, r=r: e.tensor_tensor(out=ki_f[:r, :], in0=ki_t[:r, :], in1=ikb_sb[:r, :], op=ALU.add), [ki_t.t, t_const], [ki_f.t])
            S.op('act', lambda e, r=r: e.activation(out=ik2_bf[:r, 0:64], in_=ki_f[:r, :], func=AF.Copy), [ki_f.t], [ik2_bf.t])
            S.op('act', lambda e, r=r: e.activation(out=ik2_bf[:r, 64:128], in_=ki_f[:r, :], func=AF.Copy), [ki_f.t], [ik2_bf.t])
            S.op('act', lambda e, r=r, j=j: e.activation(out=widx[:r, j, :], in_=pf(4)[:r, 64:72], func=AF.Copy, scale=IDX_SCALE), [t_pb[4]], [widx.tts[j]])
            if j == 17:
                dk, dv, dki = o_ks[:, :], o_vs[:, :], o_kis[:, :]
            else:
                p0 = pos0(j)
                dk, dv, dki = o_kp[p0:p0 + r, :], o_vp[p0:p0 + r, :], o_kip[p0:p0 + r, :]
            S.dma('sp', lambda e, r=r, dk=dk: e.dma_start(out=dk, in_=k_f[:r, :]), k_f.t, False)
            S.dma('sp', lambda e, r=r, dv=dv: e.dma_start(out=dv, in_=v_f[:r, :]), v_f.t, False)
            S.dma('sp', lambda e, r=r, dki=dki: e.dma_start(out=dki, in_=ki_f[:r, :]), ki_f.t, False)
            p0 = pos0(j) if j < 17 else 0
            for h in range(8):
                S.op('pe', lambda e, h=h, r=r: e.transpose(pbf(6)[:, h * 128:h * 128 + r], q_bf[:r, h * 128:(h + 1) * 128], ident[:r, :r]), [q_bf.t, ident.t], [t_pb[6]])
            QTd = QT if j < 17 else QT_s
            S.op('dve', lambda e, r=r, QTd=QTd: e.tensor_copy(out=QTd[:, :, :r], in_=pbf(6)[:, :].rearrange("p (s q) -> p s q", s=8)[:, :, :r]), [t_pb[6]], [QTd.t])
            for n in range(2):
                S.op('pe', lambda e, n=n, r=r: e.transpose(pbf(7)[:, n * 128:n * 128 + r], k_bf[:r, n * 128:(n + 1) * 128], ident[:r, :r]), [k_bf.t, ident.t], [t_pb[7]])
            for hp in range(4):
                S.op('pe', lambda e, hp=hp, r=r: e.transpose(pbf(7)[:, (2 + hp) * 128:(2 + hp) * 128 + r], iq_bf[:r, hp * 128:(hp + 1) * 128], ident[:r, :r]),
                     [iq_bf.t, ident.t], [t_pb[7]])
            S.op('pe', lambda e, r=r: e.transpose(pbf(7)[:, 6 * 128:6 * 128 + r], ik2_bf[:r, :], ident[:r, :r]), [ik2_bf.t, ident.t], [t_pb[7]])
            if j == 17:
                S.op('act', lambda e: e.activation(out=KT_s[:, :, :], in_=pbf(7)[:, 0:256].rearrange("p (s q) -> p s q", s=2), func=AF.Copy), [t_pb[7]], [KT_s.t])
                S.op('act', lambda e: e.activation(out=ikT_s[:, :], in_=pbf(7)[:, 768:896], func=AF.Copy), [t_pb[7]], [ikT_s.t])
                S.op('dve', lambda e: e.tensor_copy(out=iq_dup[:, :, 0:64], in_=iq_bf[:, :].rearrange("p (h d) -> p h d", h=8)), [iq_bf.t], [iq_dup.t])
                S.op('dve', lambda e: e.tensor_copy(out=iq_dup[:, :, 64:128], in_=iq_bf[:, :].rearrange("p (h d) -> p h d", h=8)), [iq_bf.t], [iq_dup.t])
                for h in range(8):
                    S.op('pe', lambda e, h=h: e.transpose(pbf(6)[:, h * 128:(h + 1) * 128], iq_dup[:, h, :], ident[:, :]), [iq_dup.t, ident.t], [t_pb[6]])
                S.op('act', lambda e: e.activation(out=iqT8[:, :, :], in_=pbf(6)[:, :].rearrange("p (s q) -> p s q", s=8), func=AF.Copy), [t_pb[6]], [iqT8.t])
                continue
            S.op('act', lambda e, r=r, p0=p0: e.activation(out=KT[:, :, p0:p0 + r], in_=pbf(7)[:, 0:256].rearrange("p (s q) -> p s q", s=2)[:, :, :r], func=AF.Copy),
                 [t_pb[7]], [KT.t])
            S.op('act', lambda e, r=r: e.activation(out=iqT[:, :, :r], in_=pbf(7)[:, 256:768].rearrange("p (s q) -> p s q", s=4)[:, :, :r], func=AF.Copy),
                 [t_pb[7]], [iqT.t])
            S.op('act', lambda e, r=r, p0=p0: e.activation(out=ikT2[:, p0:p0 + r], in_=pbf(7)[:, 768:768 + r], func=AF.Copy), [t_pb[7]], [ikT2.t])
            kts = [(pos0(i), ROWS[i], i) for i in range(j + 1)]
            idx_scores_prompt(r, p0 + r, iqT[:, :, :r], iqT.t, (lambda h, r=r, j=j: widx[:r, j, h:h + 1]), widx.tts[j])
            thresh_mask(r, p0 + r, kts, p0, r, cbias[:r, :r], (p0 + r - 1 >= 256), (j == 2), True)
            attn_prompt(r, kts, QT[:, :, :r], QT.t, (lambda n, r=r, c0=c0: o_dsaT[:, 4 * n:4 * n + 4, c0:c0 + r]), o_dsaT.tts[j])
        A.release(wd, st2, junk2, q_bf, k_f, k_bf, v_f, iq_bf, ki_f, ki_t, ik2_bf, QT, iqT, iq_dup, ikT2, tmpd)
        ptb = A.alloc([128, 64], I32)
        idx_i = A.alloc([128, 64], I32)
        pidx = A.alloc([128, 1], F32)
        S.dma('sp', lambda e: e.dma_start(out=ptb[:, :], in_=pt_d), ptb.t, True)
        S.dma('sp', lambda e: e.dma_start(out=pidx[:, :], in_=pidx_d), pidx.t, True)
        S.op('dve', lambda e: e.tensor_scalar(out=idx_i[:, :], in0=ptb[:, :], scalar1=32.0, scalar2=pidx[:, 0:1], op0=ALU.mult, op1=ALU.add),
             [ptb.t, pidx.t], [idx_i.t])
        ikT_all = A.alloc([128, 8, 2056], BF16)
        kipg = [A.alloc([128, 16, 64], F32) for _ in range(2)]
        ik2pg = A.alloc([128, 16, 128], BF16)
        iqm = A.alloc([128, 8, 8, 128], BF16)
        bm2 = A.alloc([128, 8, 128], BF16)
        cbs = A.alloc([128, 8], F32)
        S.dma('pool', lambda e: e.dma_start(out=bm2.ap, in_=bm2_d), bm2.t, True)
        S.dma('sp', lambda e: e.dma_start(out=cbs.ap, in_=cbs_d), cbs.t, True)
        S.op('dve', lambda e: e.tensor_tensor(out=iqm[:, :, :, :], in0=iqT8[:, :, :].unsqueeze(2).to_broadcast([128, 8, 8, 128]),
                                              in1=bm2[:, :, :].unsqueeze(1).to_broadcast([128, 8, 8, 128]), op=ALU.mult), [iqT8.t, bm2.t], [iqm.t])
        for p in range(8):
            for half in range(2):
                b = p + 8 * half
                for hh in range(4):
                    S.dma('pool', lambda e, b=b, half=half, hh=hh: e.indirect_dma_start(out=kipg[half][:, hh * 4:(hh + 1) * 4, :].rearrange("p a b -> p (a b)"), out_offset=None,
                                                                                    in_=cki_d[:, :], in_offset=bass.IndirectOffsetOnAxis(ap=idx_i[:, 4 * b + hh:4 * b + hh + 1], axis=0)),
                          kipg[half].t, True, extra_reads=[idx_i.t])
                S.op('act', lambda e, half=half: e.activation(out=ik2pg[:, :, half * 64:(half + 1) * 64], in_=kipg[half][:, :, :], func=AF.Copy), [kipg[half].t], [ik2pg.t])
            for g in range(2):
                for s_ in range(8):
                    S.op('pe', lambda e, g=g, s_=s_: e.transpose(pbf(6 + g)[:, s_ * 128:(s_ + 1) * 128], ik2pg[:, g * 8 + s_, :], ident[:, :]),
                         [ik2pg.t, ident.t], [t_pb[6 + g]])
                S.op('act', lambda e, g=g, p=p: e.activation(out=ikT_all[:, p, g * 1024:(g + 1) * 1024], in_=pbf(6 + g)[:, :], func=AF.Copy), [t_pb[6 + g]], [ikT_all.t])
            S.op('dve', lambda e, p=p: e.tensor_copy(out=ikT_all[0:64, p, 2048:2056], in_=ikT_s[0:64, p * 8:(p + 1) * 8]), [ikT_s.t], [ikT_all.t])
            S.op('dve', lambda e, p=p: e.tensor_copy(out=ikT_all[64:128, p, 2048:2056], in_=ikT_s[64:128, (p + 8) * 8:(p + 9) * 8]), [ikT_s.t], [ikT_all.t])
        for cc0 in range(0, 2056, 512):
            w = min(512, 2056 - cc0)
            for hp in range(4):
                rb = rl[att_cnt[0] % 2]
                att_cnt[0] += 1
                for e_ in range(2):
                    h = 2 * hp + e_
                    for p in range(8):
                        S.op('pe', lambda e, e_=e_, h=h, p=p, w=w, cc0=cc0: e.matmul(pf(e_)[:, :w], iqm[:, h, p, :], ikT_all[:, p, cc0:cc0 + w], start=(p == 0), stop=(p == 7)),
                             [iqm.t, ikT_all.t], [t_pb[e_]])
                    S.op('act', lambda e, e_=e_, w=w, rb=rb: e.activation(out=rb[:, e_, :w], in_=pf(e_)[:, :w], func=AF.Relu), [t_pb[e_]], [rb.t])
                for e_ in range(2):
                    h = 2 * hp + e_
                    if h == 0:
                        S.op('dve', lambda e, w=w, cc0=cc0, rb=rb: e.tensor_scalar(out=acc[:, cc0:cc0 + w], in0=rb[:, 0, :w], scalar1=widx[:, 17, 0:1], scalar2=None, op0=ALU.mult),
                             [rb.t, widx.tts[17]], [acc.t])
                    else:
                        S.op('dve', lambda e, w=w, cc0=cc0, rb=rb, e_=e_, h=h: e.scalar_tensor_tensor(out=acc[:, cc0:cc0 + w], in0=rb[:, e_, :w], scalar=widx[:, 17, h:h + 1],
                                                                                                    in1=acc[:, cc0:cc0 + w], op0=ALU.mult, op1=ALU.add),
                             [rb.t, widx.tts[17], acc.t], [acc.t])
        skt = [(pg * 128, 128, pg) for pg in range(16)] + [(2048, 8, 16)]
        thresh_mask(128, 2056, skt, 2048, 8, cbs[:, :], True, False, False)
        A.release(ikT_all, kipg[0], kipg[1], ik2pg, iqm, bm2, cbs, acc, junk_bf, rl[0], rl[1], mask_bf)
        Kg = [A.alloc([128, 16, 256], BF16) for _ in range(2)]
        Vg = [A.alloc([128, 16, 256], BF16) for _ in range(2)]
        Vn = [A.alloc([8, 256], BF16) for _ in range(2)]
        KTb = [KT, A.alloc([128, 2, 2064], BF16)]
        PTs = [A.alloc([128, 16, 4, 8], BF16) for _ in range(2)]
        PTn = [A.alloc([128, 4, 8], BF16) for _ in range(2)]
        pend = [None]
        sample_bufs = {}

        def emit_st_s(b, n, db, KTc):
            pts = PTs[att_cnt[1] % 2]
            ptn = PTn[att_cnt[1] % 2]
            sbk = 2 + att_cnt[1] % 2
            att_cnt[1] += 1
            sample_bufs[(b, n)] = (pts, ptn)
            qv = QT_s[:, 4 * n:4 * n + 4, b * 8:(b + 1) * 8]
            for pg in range(16):
                S.op('pe', lambda e, pg=pg: e.matmul(pf(sbk)[:, pg * 32:(pg + 1) * 32].rearrange("p (h q) -> p h q", h=4), KTc[:, n, pg * 128:(pg + 1) * 128],
                                                     qv, start=True, stop=True), [KTc.t, QT_s.t], [t_pb[sbk]])
            S.op('pe', lambda e: e.matmul(pf(0)[:8, 0:32].rearrange("p (h q) -> p h q", h=4), KTc[:, n, 2048:2056], qv, start=True, stop=True),
                 [KTc.t, QT_s.t], [t_pb[0]])
            S.op('act', lambda e: e.activation(out=pts[:, :, :, :], in_=pf(sbk)[:, :].rearrange("p (g h q) -> p g h q", g=16, h=4), func=AF.Exp, scale=128 ** -0.5),
                 [t_pb[sbk]], [pts.t])
            S.op('act', lambda e: e.activation(out=ptn[:8, :, :], in_=pf(0)[:8, 0:32].rearrange("p (h q) -> p h q", h=4), func=AF.Exp, scale=128 ** -0.5),
                 [t_pb[0]], [ptn.t])
            S.op('dve', lambda e: e.tensor_tensor(out=pts[:, :, :, :], in0=pts[:, :, :, :],
                                                  in1=maskT[:, 0:16, b * 8:(b + 1) * 8].unsqueeze(2).to_broadcast([128, 16, 4, 8]), op=ALU.mult), [pts.t, maskT.t], [pts.t])
            S.op('dve', lambda e: e.tensor_tensor(out=ptn[:8, :, :], in0=ptn[:8, :, :],
                                                  in1=maskT[:8, 16:17, b * 8:(b + 1) * 8].to_broadcast([8, 4, 8]), op=ALU.mult), [ptn.t, maskT.t], [ptn.t])

        def emit_pv_s(b, n, db, pb_):
            pts, ptn = pb_
            cs_ = TOK0[17] + b * 8
            for pg in range(17):
                if pg < 16:
                    lv = Vg[db][:, pg, n * 128:(n + 1) * 128]
                    lo_ = ones_bf[:, :]
                    rv = pts[:, pg, :, :]
                    rt = pts.t
                    vt = Vg[db].t
                else:
                    lv = Vn[db][:8, n * 128:(n + 1) * 128]
                    lo_ = ones_bf[:8, :]
                    rv = ptn[:8, :, :]
                    rt = ptn.t
                    vt = Vn[db].t
                S.op('pe', lambda e, lv=lv, rv=rv, pg=pg: e.matmul(pf(4)[:, 0:32].rearrange("p (h q) -> p h q", h=4), lv, rv, start=(pg == 0), stop=(pg == 16)),
                     [vt, rt], [t_pb[4]])
                S.op('pe', lambda e, lo_=lo_, rv=rv, pg=pg: e.matmul(pf(5)[:, 0:32].rearrange("p (h q) -> p h q", h=4), lo_, rv, start=(pg == 0), stop=(pg == 16)),
                     [ones_bf.t, rt], [t_pb[5]])
            S.op('dve', lambda e: e.reciprocal(out=rec[:, 0:32], in_=pf(5)[:, 0:32]), [t_pb[5]], [rec.t])
            S.op('dve', lambda e: e.tensor_tensor(out=o_dsaT[:, 4 * n:4 * n + 4, cs_:cs_ + 8], in0=pf(4)[:, 0:32].rearrange("p (h q) -> p h q", h=4),
                                                  in1=rec[:, 0:32].rearrange("p (h q) -> p h q", h=4), op=ALU.mult), [t_pb[4], rec.t], [o_dsaT.tts[17]])

        for b in range(16):
            db = b % 2
            KTc = KTb[db]
            for hh in range(4):
                S.dma('pool', lambda e, b=b, db=db, hh=hh: e.indirect_dma_start(out=Kg[db][:, hh * 4:(hh + 1) * 4, :].rearrange("p a b -> p (a b)"), out_offset=None, in_=ck_d[:, :],
                                                                            in_offset=bass.IndirectOffsetOnAxis(ap=idx_i[:, 4 * b + hh:4 * b + hh + 1], axis=0)),
                      Kg[db].t, True, extra_reads=[idx_i.t])
            for hh in range(4):
                S.dma('pool', lambda e, b=b, db=db, hh=hh: e.indirect_dma_start(out=Vg[db][:, hh * 4:(hh + 1) * 4, :].rearrange("p a b -> p (a b)"), out_offset=None, in_=cv_d[:, :],
                                                                            in_offset=bass.IndirectOffsetOnAxis(ap=idx_i[:, 4 * b + hh:4 * b + hh + 1], axis=0)),
                      Vg[db].t, True, extra_reads=[idx_i.t])
            for g in range(4):
                pb = 6 + g % 2
                for pl in range(4):
                    for n in range(2):
                        S.op('pe', lambda e, g=g, pl=pl, n=n, pb=pb, db=db: e.transpose(pbf(pb)[:, (pl * 2 + n) * 128:(pl * 2 + n + 1) * 128], Kg[db][:, g * 4 + pl, n * 128:(n + 1) * 128], ident[:, :]),
                             [Kg[db].t, ident.t], [t_pb[pb]])
                S.op('act', lambda e, g=g, pb=pb, KTc=KTc: e.activation(out=KTc[:, :, g * 512:(g + 1) * 512].rearrange("p n (g q) -> p n g q", g=4),
                                                                       in_=pbf(pb)[:, :].rearrange("p (g n q) -> p n g q", g=4, n=2), func=AF.Copy), [t_pb[pb]], [KTc.t])
            S.op('dve', lambda e, b=b, KTc=KTc: e.tensor_copy(out=KTc[:, :, 2048:2056], in_=KT_s[:, :, b * 8:(b + 1) * 8]), [KT_s.t], [KTc.t])
            S.dma('sp', lambda e, b=b, db=db: e.dma_start(out=Vn[db][0:8, :], in_=V_bf[b * 8:(b + 1) * 8, 17, :]), Vn[db].t, True, extra_reads=[V_bf.tts[17]])
            for n in range(2):
                emit_st_s(b, n, db, KTc)
                if pend[0] is not None:
                    emit_pv_s(*pend[0])
                pend[0] = (b, n, db, sample_bufs.pop((b, n)))
        emit_pv_s(*pend[0])
        A.release(widx, V_bf, KT, maskT, PT[0], PT[1], PT[2], PT[3], rec, bs, ones_bf, cbias, pow2, thrcap,
                  QT_s, iqT8, KT_s, ikT_s, ptb, idx_i, pidx, Kg[0], Kg[1], Vg[0], Vg[1], Vn[0], Vn[1], KTb[1], PTs[0], PTs[1], PTn[0], PTn[1])

        o_retT = A.alloc([128, 16, TTOT], BF16, ntt=NT)
        decT = A.alloc([128, 8, 128], F32)
        qdec = A.alloc([128, 8], F32)
        kdec = A.alloc([128, 12], F32)
        bm = A.alloc([128, 16, 128], BF16)
        rm = A.alloc([128, 16], F32)
        t_rc = TT(list(A.dead.items()))
        for dst, src in ((decT, decT_d), (qdec, qdec_d), (kdec, kdec_d), (rm, rm_d)):
            S.dma('sp', lambda e, dst=dst, src=src: e.dma_start(out=dst.ap, in_=src), t_rc, True)
        S.dma('pool', lambda e: e.dma_start(out=bm.ap, in_=bm_d), t_rc, True)
        wr = A.alloc([128, 8, 1536], BF16)
        rot = [A.alloc([128, 256], F32) for _ in range(2)]
        rA = [A.alloc([128, 512], F32) for _ in range(2)]
        rB = [A.alloc([128, 512], F32)] * 2
        qk6 = [A.alloc([128, 4, 256], BF16) for _ in range(2)]
        v_bf = [A.alloc([128, 512], BF16) for _ in range(2)]
        g_s = [A.alloc([128, 512], F32) for _ in range(2)]
        T6 = [A.alloc([128, 6, 128], BF16) for _ in range(2)]
        scT = A.alloc([128, 128], BF16)
        junk3 = A.alloc([128, 512], BF16)
        o_bf = A.alloc([128, 512], BF16)
        st3 = A.alloc([128, 4], F32)
        S_f = [A.alloc([128, 2, 512], F32, ntt=2) for _ in range(2)]
        S_bf = A.alloc([128, 2, 512], BF16, ntt=2)
        qsm = [A.alloc([128, 2, 128], BF16)] * 2
        kdm = [A.alloc([128, 256], BF16)] * 2

        def v4(ap):
            return ap.rearrange("p (a h d) -> p a h d", a=2, h=2)

        def ret_load_w(h):
            segs = [(C_RQ + h * 256, 256, 0), (C_RK + h * 256, 256, 256), (C_RV + h * 512, 512, 512), (C_RG + h * 512, 512, 1024)]
            for (s0, w, d0) in segs:
                for k in range(8):
                    S.dma('pool', lambda e, s0=s0, w=w, d0=d0, k=k: e.dma_start(out=wr[:, k, d0:d0 + w], in_=w_in[k * 128:(k + 1) * 128, s0:s0 + w]), wr.t, True)

        def ret_front(h, j, pp):
            r = ROWS[j]
            c0 = TOK0[j]
            var = 1 if j == 17 else 0
            kvar = 2 if j == 17 else (1 if j == 0 else 0)
            rA_, rB_, qk_, vb_, gs_, T6_, rot_ = rA[pp], rB[pp], qk6[pp], v_bf[pp], g_s[pp], T6[pp], rot[pp]
            S.dma('sp', lambda e: e.dma_start(out=rot_[:r, :], in_=rot_d[j, :r, :]), rot_.t, True)
            for (bank, cc0) in ((0, 0), (1, 512), (2, 1024)):
                for k in range(8):
                    S.op('pe', lambda e, bank=bank, cc0=cc0, k=k: e.matmul(pf(bank)[:r, :], xnT[:, k, c0:c0 + r], wr[:, k, cc0:cc0 + 512], start=(k == 0), stop=(k == 7)),
                         [xnT.tts[j], wr.t], [t_pb[bank]])
            S.op('dve', lambda e: e.tensor_tensor(out=rA_[:r, :].rearrange("p (b d) -> p b d", b=4), in0=pf(0)[:r, :].rearrange("p (b d) -> p b d", b=4),
                                                  in1=rot_[:r, 0:128].unsqueeze(1).to_broadcast([r, 4, 128]), op=ALU.mult), [t_pb[0], rot_.t], [rA_.t])
            S.op('dve', lambda e: e.tensor_tensor(out=rB_[:r, :].rearrange("p (b d) -> p b d", b=4), in0=pf(0)[:r, :].rearrange("p (b d) -> p b d", b=4),
                                                  in1=rot_[:r, 128:256].unsqueeze(1).to_broadcast([r, 4, 128]), op=ALU.mult), [t_pb[0], rot_.t], [rB_.t])
            S.op('pool', lambda e: e.tensor_tensor(out=v4(rA_[:r, :])[:, :, 0, :], in0=v4(rA_[:r, :])[:, :, 0, :], in1=v4(rB_[:r, :])[:, :, 1, :], op=ALU.subtract),
                 [rA_.t, rB_.t], [rA_.t])
            S.op('pool', lambda e: e.tensor_tensor(out=v4(rA_[:r, :])[:, :, 1, :], in0=v4(rB_[:r, :])[:, :, 0, :], in1=v4(rA_[:r, :])[:, :, 1, :], op=ALU.add),
                 [rA_.t, rB_.t], [rA_.t])
            S.op('act', lambda e: e.activation(out=qk_[:r, 0, :], in_=rA_[:r, 0:256], func=AF.Copy), [rA_.t], [qk_.t])
            S.op('act', lambda e: e.activation(out=qk_[:r, 1, :], in_=rA_[:r, 0:256], func=AF.Copy, scale=qdec[:r, var * 4 + h:var * 4 + h + 1]), [rA_.t, t_rc], [qk_.t])
            S.op('act', lambda e: e.activation(out=qk_[:r, 2, :], in_=rA_[:r, 256:512], func=AF.Copy, scale=1.0 / 16), [rA_.t], [qk_.t])
            S.op('dve', lambda e: e.tensor_scalar(out=qk_[:r, 3, :], in0=rA_[:r, 256:512], scalar1=kdec[:r, kvar * 4 + h:kvar * 4 + h + 1], scalar2=None, op0=ALU.mult),
                 [rA_.t, t_rc], [qk_.t])
            S.op('act', lambda e: e.activation(out=vb_[:r, :], in_=pf(1)[:r, :], func=AF.Copy), [t_pb[1]], [vb_.t])
            S.op('act', lambda e: e.activation(out=gs_[:r, :], in_=pf(2)[:r, :], func=AF.Silu), [t_pb[2]], [gs_.t])
            for s_ in range(3):
                for c in range(2):
                    S.op('pe', lambda e, s_=s_, c=c: e.transpose(pbf(6)[:, (2 * s_ + c) * 128:(2 * s_ + c) * 128 + r], qk_[:r, s_, c * 128:(c + 1) * 128], ident[:r, :r]),
                         [qk_.t, ident.t], [t_pb[6]])
            S.op('dve', lambda e: e.tensor_copy(out=T6_[:, :, :r], in_=pbf(6)[:, 0:768].rearrange("p (s q) -> p s q", s=6)[:, :, :r]), [t_pb[6]], [T6_.t])

        def ret_back(h, j, pp):
            r = ROWS[j]
            c0 = TOK0[j]
            var = 1 if j == 17 else 0
            Cj = 8 if j == 17 else r
            qk_, vb_, gs_, T6_ = qk6[pp], v_bf[pp], g_s[pp], T6[pp]
            if j == 0:
                S.op('dve', lambda e: e.memset(S_f[0][:, :, :], 0.0), [], S_f[0].tts)
                S.op('dve', lambda e: e.memset(S_bf[:, :, :], 0.0), [], S_bf.tts)
            for c in range(2):
                S.op('pe', lambda e, c=c: e.matmul(pf(3)[:r, :r], T6_[:, 4 + c, :r], T6_[:, c, :r], start=(c == 0), stop=(c == 1)), [T6_.t], [t_pb[3]])
            S.op('dve', lambda e: e.tensor_tensor(out=scT[:r, :r], in0=pf(3)[:r, :r], in1=decT[:r, var * 4 + h, :r], op=ALU.mult), [t_pb[3], t_rc], [scT.t])
            if j < 17:
                S.op('pe', lambda e: e.matmul(pf(4)[:r, :], scT[:r, :r], vb_[:r, :], start=True, stop=False), [scT.t, vb_.t], [t_pb[4]])
                for c in range(2):
                    S.op('pe', lambda e, c=c: e.matmul(pf(4)[:r, :], T6_[:, 2 + c, :r], S_bf[:, c, :], start=False, stop=(c == 1)), [T6_.t, S_bf.tts[c]], [t_pb[4]])
                for c in range(2):
                    bk = 5 if c == 0 else 3
                    S.op('pe', lambda e, c=c, bk=bk: e.matmul(pf(bk)[:, :], qk_[:r, 3, c * 128:(c + 1) * 128], vb_[:r, :], start=True, stop=True), [qk_.t, vb_.t], [t_pb[bk]])
                    S.op('dve', lambda e, c=c, bk=bk: e.scalar_tensor_tensor(out=S_f[0][:, c, :], in0=S_f[0][:, c, :], scalar=GAM[h] ** Cj, in1=pf(bk)[:, :],
                                                                           op0=ALU.mult, op1=ALU.add), [S_f[0].tts[c], t_pb[bk]], [S_f[0].tts[c]])
                    S.op('act', lambda e, c=c: e.activation(out=S_bf[:, c, :], in_=S_f[0][:, c, :], func=AF.Copy), [S_f[0].tts[c]], [S_bf.tts[c]])
                if j == 16:
                    for c in range(2):
                        S.dma('sp', lambda e, c=c: e.dma_start(out=o_rp[h, c * 128:(c + 1) * 128, :], in_=S_f[0][:, c, :]), S_f[0].tts[c], False)
            else:
                S.op('pe', lambda e: e.matmul(pf(4)[:, :], scT[:, :], vb_[:, :], start=True, stop=False), [scT.t, vb_.t], [t_pb[4]])
                for b in range(16):
                    S.op('dve', lambda e, b=b: e.tensor_tensor(out=qsm[0][:, :, :], in0=T6_[:, 2:4, :], in1=bm[:, b:b + 1, :].to_broadcast([128, 2, 128]), op=ALU.mult),
                         [T6_.t, t_rc], [qsm[0].t])
                    S.op('dve', lambda e, b=b: e.tensor_scalar(out=kdm[0][:, :], in0=qk_[:, 3, :], scalar1=rm[:, b:b + 1], scalar2=None, op0=ALU.mult),
                         [qk_.t, t_rc], [kdm[0].t])
                    for c in range(2):
                        u = 2 * b + c
                        su = u % 4
                        Sb, tS = S_f[su // 2], S_f[su // 2].tts[su % 2]
                        hs = su % 2
                        bk = 5 if u % 2 == 0 else 3
                        S.dma('pool', lambda e, b=b, c=c, Sb=Sb, hs=hs: e.dma_start(out=Sb[:, hs, :], in_=state[b, h, c * 128:(c + 1) * 128, :]), tS, True)
                        S.op('act', lambda e, Sb=Sb, hs=hs, u=u: e.activation(out=S_bf[:, u % 2, :], in_=Sb[:, hs, :], func=AF.Copy), [tS], [S_bf.tts[u % 2]])
                        S.op('pe', lambda e, c=c, b=b, u=u: e.matmul(pf(4)[:, :], qsm[0][:, c, :], S_bf[:, u % 2, :], start=False, stop=(b == 15 and c == 1)),
                             [qsm[0].t, S_bf.tts[u % 2]], [t_pb[4]])
                        S.op('pe', lambda e, c=c, bk=bk: e.matmul(pf(bk)[:, :], kdm[0][:, c * 128:(c + 1) * 128], vb_[:, :], start=True, stop=True),
                             [kdm[0].t, vb_.t], [t_pb[bk]])
                        S.op('dve', lambda e, Sb=Sb, hs=hs, bk=bk: e.scalar_tensor_tensor(out=Sb[:, hs, :], in0=Sb[:, hs, :], scalar=GAM[h] ** 8, in1=pf(bk)[:, :],
                                                                                        op0=ALU.mult, op1=ALU.add), [tS, t_pb[bk]], [tS])
                        S.dma('sp', lambda e, b=b, c=c, Sb=Sb, hs=hs: e.dma_start(out=o_rs[b, h, c * 128:(c + 1) * 128, :], in_=Sb[:, hs, :]), tS, False)
            S.op('act', lambda e: e.activation(out=junk3[:r, :], in_=pf(4)[:r, :], func=AF.Square, accum_out=st3[:r, 0:1]), [t_pb[4]], [junk3.t, st3.t])
            S.op('act', lambda e: e.activation(out=st3[:r, 0:1], in_=st3[:r, 0:1], func=AF.Sqrt, scale=1.0 / 512, bias=EPS), [st3.t], [st3.t])
            S.op('dve', lambda e: e.reciprocal(out=st3[:r, 0:1], in_=st3[:r, 0:1]), [st3.t], [st3.t])
            S.op('dve', lambda e: e.scalar_tensor_tensor(out=o_bf[:r, :], in0=pf(4)[:r, :], scalar=st3[:r, 0:1], in1=gs_[:r, :], op0=ALU.mult, op1=ALU.mult),
                 [t_pb[4], st3.t, gs_.t], [o_bf.t])
            for c in range(4):
                S.op('pe', lambda e, c=c: e.transpose(pbf(7)[:, c * 128:c * 128 + r], o_bf[:r, c * 128:(c + 1) * 128], ident[:r, :r]), [o_bf.t, ident.t], [t_pb[7]])
            S.op('act', lambda e: e.activation(out=o_retT[:, 4 * h:4 * h + 4, c0:c0 + r], in_=pbf(7)[:, 0:512].rearrange("p (s q) -> p s q", s=4)[:, :, :r], func=AF.Copy),
                 [t_pb[7]], [o_retT.tts[j]])

        steps = [(h, j) for h in range(4) for j in range(NT)]
        ret_load_w(0)
        ret_front(steps[0][0], steps[0][1], 0)
        for i in range(len(steps)):
            nxt = steps[i + 1] if i + 1 < len(steps) else None
            if nxt is not None and nxt[1] == 0:
                ret_load_w(nxt[0])
                ret_back(steps[i][0], steps[i][1], i % 2)
                ret_front(nxt[0], nxt[1], (i + 1) % 2)
            else:
                if nxt is not None:
                    ret_front(nxt[0], nxt[1], (i + 1) % 2)
                ret_back(steps[i][0], steps[i][1], i % 2)
        A.release(decT, qdec, kdec, bm, rm, wr, rot[0], rot[1], rA[0], rA[1], rB[0], qk6[0], qk6[1], v_bf[0], v_bf[1], g_s[0], g_s[1], T6[0], T6[1],
                  scT, junk3, o_bf, st3, S_f[0], S_f[1], S_bf, qsm[0], kdm[0])

        def tts_for(buf, t0, w):
            return [buf.tts[j] for j in range(NT) if TOK0[j] < t0 + w and TOK0[j] + ROWS[j] > t0]

        mT = A.alloc([128, 8, TTOT], BF16, ntt=NT)
        wrp_c = [A.alloc([128, 16, 128], BF16) for _ in range(2)]
        wdp_c = [A.alloc([128, 8, 128], BF16) for _ in range(2)]
        wg_c = [A.alloc([128, 8, 2, 128], BF16) for _ in range(2)]
        g1 = [A.alloc([128, 512], F32) for _ in range(2)]
        g2 = [A.alloc([128, 512], F32) for _ in range(2)]
        mchunks = [(t0, min(512, TTOT - t0)) for t0 in range(0, TTOT, 512)]
        nmc = 0
        ring = [A.alloc([128, 2, 128], F32) for _ in range(4)]
        nring = [0]

        def load_cast(dst, dst_tt, src):
            rb = ring[nring[0] % 4]
            nring[0] += 1
            S.dma('sp', lambda e: e.dma_start(out=rb[:, :, :], in_=src), rb.t, True)
            S.op('pool', lambda e: e.tensor_copy(out=dst, in_=rb[:, :, :]), [rb.t], [dst_tt])

        def merge_load(c):
            wb_ = c % 2
            cs = slice(c * 128, (c + 1) * 128)
            for k0 in range(0, 16, 2):
                load_cast(wrp_c[wb_][:, k0:k0 + 2, :], wrp_c[wb_].t, wrp_d[k0 * 128:(k0 + 2) * 128, cs].rearrange("(k p) c -> p k c", p=128))
            for k0 in range(0, 8, 2):
                load_cast(wdp_c[wb_][:, k0:k0 + 2, :], wdp_c[wb_].t, wdp_d[k0 * 128:(k0 + 2) * 128, cs].rearrange("(k p) c -> p k c", p=128))
                for gg in range(2):
                    gs = slice(C_GZ + gg * 1024 + c * 128, C_GZ + gg * 1024 + (c + 1) * 128)
                    load_cast(wg_c[wb_][:, k0:k0 + 2, gg, :], wg_c[wb_].t, w_in[k0 * 128:(k0 + 2) * 128, gs].rearrange("(k p) c -> p k c", p=128))

        merge_load(0)
        for c in range(8):
            wb_ = c % 2
            if c + 1 < 8:
                merge_load(c + 1)
            for (t0, w) in mchunks:
                bb = 4 * (nmc % 2)
                gb = nmc % 2
                nmc += 1
                for k in range(16):
                    S.op('pe', lambda e, k=k, t0=t0, w=w, bb=bb, wb_=wb_: e.matmul(pf(bb)[:, :w], wrp_c[wb_][:, k, :], o_retT[:, k, t0:t0 + w], start=(k == 0), stop=(k == 15)),
                         [wrp_c[wb_].t] + tts_for(o_retT, t0, w), [t_pb[bb]])
                for k in range(8):
                    S.op('pe', lambda e, k=k, t0=t0, w=w, bb=bb, wb_=wb_: e.matmul(pf(bb + 1)[:, :w], wdp_c[wb_][:, k, :], o_dsaT[:, k, t0:t0 + w], start=(k == 0), stop=(k == 7)),
                         [wdp_c[wb_].t] + tts_for(o_dsaT, t0, w), [t_pb[bb + 1]])
                for gg in range(2):
                    for k in range(8):
                        S.op('pe', lambda e, k=k, t0=t0, w=w, bb=bb, wb_=wb_, gg=gg: e.matmul(pf(bb + 2 + gg)[:, :w], wg_c[wb_][:, k, gg, :], xnT[:, k, t0:t0 + w],
                                                                                         start=(k == 0), stop=(k == 7)),
                             [wg_c[wb_].t] + tts_for(xnT, t0, w), [t_pb[bb + 2 + gg]])
                S.op('act', lambda e, w=w, bb=bb, gb=gb: e.activation(out=g1[gb][:, :w], in_=pf(bb + 2)[:, :w], func=AF.Sigmoid), [t_pb[bb + 2]], [g1[gb].t])
                S.op('act', lambda e, w=w, bb=bb, gb=gb: e.activation(out=g2[gb][:, :w], in_=pf(bb + 3)[:, :w], func=AF.Sigmoid), [t_pb[bb + 3]], [g2[gb].t])
                S.op('dve', lambda e, w=w, bb=bb, gb=gb: e.tensor_tensor(out=g1[gb][:, :w], in0=g1[gb][:, :w], in1=pf(bb)[:, :w], op=ALU.mult), [g1[gb].t, t_pb[bb]], [g1[gb].t])
                S.op('dve', lambda e, w=w, bb=bb, gb=gb: e.tensor_tensor(out=g2[gb][:, :w], in0=g2[gb][:, :w], in1=pf(bb + 1)[:, :w], op=ALU.mult), [g2[gb].t, t_pb[bb + 1]], [g2[gb].t])
                S.op('pool', lambda e, w=w, gb=gb, c=c, t0=t0: e.tensor_tensor(out=mT[:, c, t0:t0 + w], in0=g1[gb][:, :w], in1=g2[gb][:, :w], op=ALU.add),
                     [g1[gb].t, g2[gb].t], tts_for(mT, t0, w))
        if DEBUG:
            S.barrier()
            S.dma('sp', lambda e: e.dma_start(out=o_dbg, in_=mT.ap), mT.tts[0], False)
        A.release(xnT, o_dsaT, o_retT, wrp_c[0], wrp_c[1], wdp_c[0], wdp_c[1], wg_c[0], wg_c[1], g1[0], g1[1], g2[0], g2[1], ring[0], ring[1], ring[2], ring[3])

        wo = A.alloc([128, 8, D], BF16)
        for k in range(8):
            S.dma('pool', lambda e, k=k: e.dma_start(out=wo[:, k, :], in_=wo_d[k * 128:(k + 1) * 128, :]), wo.t, True)
        nfw = A.alloc([128, 8], F32)
        S.dma('sp', lambda e: e.dma_start(out=nfw[:, :], in_=nfw_d), nfw.t, True)
        yacc = A.alloc([128, NT, D], F32, ntt=NT)
        h2nT = A.alloc([128, 8, TTOT], BF16, ntt=NT)
        xt2 = [A.alloc([128, D], F32) for _ in range(2)]
        xsb2 = [A.alloc([128, D], BF16) for _ in range(2)]
        junk4 = A.alloc([128, D], F32)
        st4 = A.alloc([128, 2 * NT], F32, ntt=NT)
        for j in range(1, NT):
            c0 = TOK0[j]
            bi = j % 2
            src = xs[:, :] if j == 17 else xp[128 * (j - 1):128 * j, :]
            S.dma('sp', lambda e, bi=bi, src=src: e.dma_start(out=xt2[bi][:, :], in_=src), xt2[bi].t, True)
            for half in range(2):
                bank = (2 * j + half) % 4
                for c in range(8):
                    S.op('pe', lambda e, c=c, c0=c0, half=half, bank=bank: e.matmul(pf(bank)[:, :], mT[:, c, c0:c0 + 128], wo[:, c, half * 512:(half + 1) * 512],
                                                                                 start=(c == 0), stop=(c == 7)), [mT.tts[j], wo.t], [t_pb[bank]])
                S.op('dve', lambda e, j=j, half=half, bank=bank, bi=bi: e.tensor_tensor(out=yacc[:, j, half * 512:(half + 1) * 512], in0=pf(bank)[:, :],
                                                                                     in1=xt2[bi][:, half * 512:(half + 1) * 512], op=ALU.add),
                     [t_pb[bank], xt2[bi].t], [yacc.tts[j]])
            ss = st4[:, 2 * j:2 * j + 1]
            rs = st4[:, 2 * j + 1:2 * j + 2]
            tst = st4.tts[j]
            S.op('act', lambda e, j=j, ss=ss: e.activation(out=junk4[:, :], in_=yacc[:, j, :], func=AF.Square, accum_out=ss), [yacc.tts[j]], [junk4.t, tst])
            S.op('act', lambda e, ss=ss, rs=rs: e.activation(out=rs, in_=ss, func=AF.Sqrt, scale=1.0 / D, bias=EPS), [tst], [tst])
            S.op('dve', lambda e, rs=rs: e.reciprocal(out=rs, in_=rs), [tst], [tst])
            S.op('dve', lambda e, bi=bi, j=j, rs=rs: e.tensor_scalar(out=xsb2[bi][:, :], in0=yacc[:, j, :], scalar1=rs, scalar2=None, op0=ALU.mult),
                 [yacc.tts[j], tst], [xsb2[bi].t])
            pb = 6 + bi
            for k in range(8):
                S.op('pe', lambda e, bi=bi, k=k, pb=pb: e.transpose(pbf(pb)[:, k * 128:(k + 1) * 128], xsb2[bi][:, k * 128:(k + 1) * 128], ident[:, :]),
                     [xsb2[bi].t, ident.t], [t_pb[pb]])
            for k in range(8):
                if k % 2 == 0:
                    S.op('act', lambda e, pb=pb, k=k, c0=c0: e.activation(out=h2nT[:, k, c0:c0 + 128], in_=pbf(pb)[:, k * 128:(k + 1) * 128], func=AF.Copy, scale=nfw[:, k:k + 1]),
                         [t_pb[pb], nfw.t], [h2nT.tts[j]])
                else:
                    S.op('dve', lambda e, pb=pb, k=k, c0=c0: e.tensor_scalar(out=h2nT[:, k, c0:c0 + 128], in0=pbf(pb)[:, k * 128:(k + 1) * 128], scalar1=nfw[:, k:k + 1],
                                                                           scalar2=None, op0=ALU.mult), [t_pb[pb], nfw.t], [h2nT.tts[j]])
        A.release(mT, wo, nfw, xt2[0], xt2[1], xsb2[0], xsb2[1], junk4, st4)

        groups = [(0, 4), (4, 4), (8, 4), (12, 4), (16, 4), (20, 2)]
        wa = [A.alloc([128, 8, 512], BF16) for _ in range(2)]
        wb2 = [A.alloc([128, 8, 512], BF16) for _ in range(2)]
        wo2 = [A.alloc([128, 4, D], BF16) for _ in range(2)]
        uT = [A.alloc([128, 4, 512], BF16) for _ in range(2)]
        sa = [A.alloc([128, 512], F32) for _ in range(2)]
        tchunks = [(1, 4), (5, 4), (9, 4), (13, 4), (17, 1)]
        nu = 0
        nsa = 0
        for gi, (f0c, nf) in enumerate(groups):
            wb_ = gi % 2
            f0 = f0c * 128
            for k in range(8):
                S.dma('pool', lambda e, k=k, f0=f0, nf=nf, wb_=wb_: e.dma_start(out=wa[wb_][:, k, 0:nf * 128], in_=wfi_d[k * 128:(k + 1) * 128, f0:f0 + nf * 128]), wa[wb_].t, True)
                S.dma('pool', lambda e, k=k, f0=f0, nf=nf, wb_=wb_: e.dma_start(out=wb2[wb_][:, k, 0:nf * 128], in_=wfi_d[k * 128:(k + 1) * 128, DFF + f0:DFF + f0 + nf * 128]),
                      wb2[wb_].t, True)
            for fi in range(nf):
                S.dma('pool', lambda e, fi=fi, f0=f0, wb_=wb_: e.dma_start(out=wo2[wb_][:, fi, :], in_=wfo_d[f0 + fi * 128:f0 + (fi + 1) * 128, :]), wo2[wb_].t, True)
            for (j0, ntl) in tchunks:
                t0 = TOK0[j0]
                w = ntl * 128
                ub = uT[nu % 2]
                nu += 1
                rtt = [h2nT.tts[j] for j in range(j0, j0 + ntl)]
                for fi in range(nf):
                    sb_ = sa[nsa % 2]
                    ba = 2 * (nsa % 2)
                    nsa += 1
                    for k in range(8):
                        S.op('pe', lambda e, k=k, fi=fi, t0=t0, w=w, ba=ba, wb_=wb_: e.matmul(pf(ba)[:, :w], wa[wb_][:, k, fi * 128:(fi + 1) * 128], h2nT[:, k, t0:t0 + w],
                                                                                         start=(k == 0), stop=(k == 7)), [wa[wb_].t] + rtt, [t_pb[ba]])
                    for k in range(8):
                        S.op('pe', lambda e, k=k, fi=fi, t0=t0, w=w, ba=ba, wb_=wb_: e.matmul(pf(ba + 1)[:, :w], wb2[wb_][:, k, fi * 128:(fi + 1) * 128], h2nT[:, k, t0:t0 + w],
                                                                                         start=(k == 0), stop=(k == 7)), [wb2[wb_].t] + rtt, [t_pb[ba + 1]])
                    S.op('act', lambda e, w=w, ba=ba, sb_=sb_: e.activation(out=sb_[:, :w], in_=pf(ba)[:, :w], func=AF.Silu), [t_pb[ba]], [sb_.t])
                    S.op('dve', lambda e, w=w, ba=ba, sb_=sb_, ub=ub, fi=fi: e.tensor_tensor(out=ub[:, fi, :w], in0=sb_[:, :w], in1=pf(ba + 1)[:, :w], op=ALU.mult),
                         [sb_.t, t_pb[ba + 1]], [ub.t])
                for jj in range(ntl):
                    j = j0 + jj
                    for half in range(2):
                        bank = 4 + (2 * j + half) % 4
                        for fi in range(nf):
                            S.op('pe', lambda e, fi=fi, jj=jj, half=half, bank=bank, ub=ub, wb_=wb_, nf=nf: e.matmul(pf(bank)[:, :], ub[:, fi, jj * 128:(jj + 1) * 128],
                                                                                                           wo2[wb_][:, fi, half * 512:(half + 1) * 512],
                                                                                                           start=(fi == 0), stop=(fi == nf - 1)),
                                 [ub.t, wo2[wb_].t], [t_pb[bank]])
                        S.op('dve', lambda e, j=j, half=half, bank=bank: e.tensor_tensor(out=yacc[:, j, half * 512:(half + 1) * 512], in0=pf(bank)[:, :],
                                                                                      in1=yacc[:, j, half * 512:(half + 1) * 512], op=ALU.add),
                             [t_pb[bank], yacc.tts[j]], [yacc.tts[j]])
                    if gi == len(groups) - 1:
                        dst = o_ys[:, :] if j == 17 else o_yp[128 * (j - 1):128 * j, :]
                        S.dma('sp', lambda e, j=j, dst=dst: e.dma_start(out=dst, in_=yacc[:, j, :]), yacc.tts[j], False)

        S.emit()
    return nc


_NC_CACHE = {}


def _consts():
    c = {}
    c["ident"] = np.eye(128, dtype=np.float32)
    half = 128
    inv = np.power(np.float32(10000.0), -np.arange(half, dtype=np.float32) / np.float32(half)).astype(np.float32)
    rot = np.zeros((NT, 128, 256), np.float32)
    for j in range(NT):
        if j == 0:
            pos = np.arange(16)
        elif j == 17:
            pos = 2048 + (np.arange(128) % 8)
        else:
            pos = 16 + 128 * (j - 1) + np.arange(128)
        ang = pos.astype(np.float32)[:, None] * inv[None, :]
        rot[j, :len(pos), 0:128] = np.cos(ang)
        rot[j, :len(pos), 128:256] = np.sin(ang)
    c["rot"] = rot
    lg = np.log1p(-np.exp2(-5.0 - np.arange(4, dtype=np.float64)))
    i = np.arange(128)
    decT = np.zeros((128, 8, 128), np.float32)
    for h in range(4):
        diff = i[None, :] - i[:, None]
        decT[:, h, :] = np.where(diff >= 0, np.exp(lg[h] * np.maximum(diff, 0)), 0.0)
        same = (i[None, :] // 8) == (i[:, None] // 8)
        d8 = (i[None, :] % 8) - (i[:, None] % 8)
        decT[:, 4 + h, :] = np.where(same & (d8 >= 0), np.exp(lg[h] * np.maximum(d8, 0)), 0.0)
    c["decT"] = decT
    qdec = np.zeros((128, 8), np.float32)
    kdec = np.zeros((128, 12), np.float32)
    for h in range(4):
        qdec[:, h] = np.exp(lg[h] * (i + 1.0))
        qdec[:, 4 + h] = np.exp(lg[h] * ((i % 8) + 1.0))
        kdec[:, h] = np.exp(lg[h] * (127.0 - i)) / 16.0
        kdec[:16, 4 + h] = np.exp(lg[h] * (15.0 - i[:16])) / 16.0
        kdec[:, 8 + h] = np.exp(lg[h] * (7.0 - (i % 8))) / 16.0
    c["qdec"] = qdec
    c["kdec"] = kdec
    bm = np.zeros((128, 16, 128), np.float32)
    rm = np.zeros((128, 16), np.float32)
    for b in range(16):
        bm[:, b, 8 * b:8 * b + 8] = 1.0
        rm[8 * b:8 * b + 8, b] = 1.0
    c["bm"] = bm
    c["rm"] = rm
    c["cbias"] = np.where(i[None, :] <= i[:, None], 0.0, NEG).astype(np.float32)
    c["pow2"] = np.broadcast_to((0.5 ** (np.arange(NBIS + 1) + 1.0))[None, :], (128, NBIS + 1)).astype(np.float32).copy()
    bm2 = np.zeros((128, 8, 128), np.float32)
    for p in range(8):
        bm2[0:64, p, 8 * p:8 * p + 8] = 1.0
        bm2[64:128, p, 8 * (p + 8):8 * (p + 8) + 8] = 1.0
    c["bm2"] = bm2
    c["cbs"] = np.where(np.arange(8)[None, :] <= (i % 8)[:, None], 0.0, NEG).astype(np.float32)
    c["pidx"] = (np.arange(128) % 32).astype(np.float32)[:, None].copy()
    c["thrcap"] = np.where(i <= 111, -1.0e29, 1.0e30).astype(np.float32)[:, None].copy()
    return c


def kernel(x_prompt, x_sample, cache_k, cache_v, cache_kidx, state_ret, page_table,
           meta_tokens, norm_mix_w, w_in, w_ret_proj, dsa_q_norm_w, dsa_k_norm_w,
           idx_k_norm_w, idx_k_norm_b, w_dsa_proj, w_out, norm_ffn_w, w_ffn_in, w_ffn_out):
    f = lambda a: np.ascontiguousarray(np.asarray(a))
    x_prompt, x_sample, state_ret = f(x_prompt), f(x_sample), f(state_ret)
    ck = f(cache_k)[0].reshape(NPOOL * 32, 4 * 256)
    cv = f(cache_v)[0].reshape(NPOOL * 32, 4 * 256)
    cki = f(cache_kidx)[0].reshape(NPOOL * 32, 4 * 64)
    ptab = f(page_table).astype(np.int32)
    if 'nc' not in _NC_CACHE:
        _NC_CACHE['nc'] = build_program()
    nc = _NC_CACHE['nc']
    ncore = 8
    common = {
        "meta": f(meta_tokens),
        "w_in": f(np.asarray(w_in)[0]),
        "nmw": f(np.asarray(norm_mix_w)[0].reshape(8, 128).T),
        "wq_bc": f(np.broadcast_to(np.asarray(dsa_q_norm_w)[0][None, :], (128, 128))),
        "wk_bc": f(np.broadcast_to(np.asarray(dsa_k_norm_w)[0][None, :], (128, 128))),
        "ikw_bc": f(np.broadcast_to(np.asarray(idx_k_norm_w)[0][None, :], (128, 64))),
        "ikb_bc": f(np.broadcast_to(np.asarray(idx_k_norm_b)[0][None, :], (128, 64))),
        "w_ret_proj": f(np.asarray(w_ret_proj)[0]),
        "w_dsa_proj": f(np.asarray(w_dsa_proj)[0]),
        "w_out": f(np.asarray(w_out)[0]),
        "w_ffn_in": f(np.asarray(w_ffn_in)[0]),
        "w_ffn_out": f(np.asarray(w_ffn_out)[0]),
        "nfw": f(np.asarray(norm_ffn_w)[0].reshape(8, 128).T),
    }
    common.update(_consts())
    in_maps = []
    for c in range(ncore):
        m = dict(common)
        m["xp"] = x_prompt[c]
        m["xs"] = f(x_sample[16 * c:16 * c + 16].reshape(128, D))
        m["state"] = state_ret[0, 16 * c:16 * c + 16]
        ptc = ptab[16 * c:16 * c + 16]
        m["pt2"] = f(np.repeat(ptc.reshape(16, 4, 4).transpose(2, 0, 1).reshape(4, 64), 32, axis=0))
        m["ck"], m["cv"], m["cki"] = ck, cv, cki
        in_maps.append(m)
    res = run_bass_kernel_spmd(nc, in_maps, core_ids=list(range(ncore)))
    R = res.results
    if DEBUG:
        _NC_CACHE['dbg'] = R[0]["o_dbg"]
    y_prompt = np.stack([R[c]["o_yp"] for c in range(ncore)]).reshape(8, 2048, D)
    y_sample = np.concatenate([R[c]["o_ys"].reshape(16, 8, D) for c in range(ncore)], 0)
    k_prompt = np.stack([R[c]["o_kp"].reshape(2064, 2, 128) for c in range(ncore)])[None]
    v_prompt = np.stack([R[c]["o_vp"].reshape(2064, 2, 128) for c in range(ncore)])[None]
    kidx_prompt = np.stack([R[c]["o_kip"] for c in range(ncore)])[None]
    ret_prompt = np.stack([R[c]["o_rp"] for c in range(ncore)])[None]
    k_sample = np.concatenate([R[c]["o_ks"].reshape(16, 8, 2, 128) for c in range(ncore)], 0)[None]
    v_sample = np.concatenate([R[c]["o_vs"].reshape(16, 8, 2, 128) for c in range(ncore)], 0)[None]
    kidx_sample = np.concatenate([R[c]["o_kis"].reshape(16, 8, 64) for c in range(ncore)], 0)[None]
    ret_sample = np.concatenate([R[c]["o_rs"] for c in range(ncore)], 0)[None]
    outs = (y_prompt, y_sample, k_prompt, v_prompt, kidx_prompt, ret_prompt, k_sample, v_sample, kidx_sample, ret_sample)
    return tuple(np.ascontiguousarray(o, dtype=np.float32) for o in outs)
```
